# Optimizing a Trainium2 kernel written in Bass

```python
import math
import jax, jax.numpy as jnp
from jax import lax
import numpy as np

D_MODEL = 1024
BATCH = 2
SEQ = 8192
DEPTH = 1

EXPAND = 2
D_INNER = EXPAND * D_MODEL
D_CONV = D_INNER // 2
D_RET = D_INNER - D_CONV
RET_HEADS = 4
RET_QK_DIM = 256
RET_V_DIM = D_RET // RET_HEADS
CONV_WIDTH = 31
CHUNK = 128
ROPE_BASE = 10000.0
NORM_EPS = 1e-6
N_IN = 3 * D_CONV + 2 * RET_HEADS * RET_QK_DIM + 2 * D_RET

kernel_name = "hybrid_conformer_conv_retention_adaln"


def _rmsnorm(x, g):
    x32 = x.astype(jnp.float32)
    y = x32 * lax.rsqrt(jnp.mean(x32 * x32, axis=-1, keepdims=True) + NORM_EPS)
    return (y * g.astype(jnp.float32)).astype(x.dtype)


def _layernorm(x, g, b):
    x32 = x.astype(jnp.float32)
    mu = jnp.mean(x32, axis=-1, keepdims=True)
    var = jnp.mean(jnp.square(x32 - mu), axis=-1, keepdims=True)
    y = (x32 - mu) * lax.rsqrt(var + NORM_EPS)
    return (y * g.astype(jnp.float32) + b.astype(jnp.float32)).astype(x.dtype)


def _rotary(x):
    s, dh = x.shape[1], x.shape[-1]
    inv_freq = 1.0 / (ROPE_BASE ** jnp.linspace(0.0, 1.0, dh // 2, dtype=jnp.float32))
    pos = jnp.arange(s, dtype=jnp.float32)
    theta = pos[:, None] * inv_freq[None, :]
    cos = jnp.cos(theta)[None, :, None, :]
    sin = jnp.sin(theta)[None, :, None, :]
    x32 = x.astype(jnp.float32)
    x1, x2 = x32[..., : dh // 2], x32[..., dh // 2:]
    out = jnp.concatenate([x1 * cos - x2 * sin, x1 * sin + x2 * cos], axis=-1)
    return out.astype(x.dtype)


def _causal_depthwise_conv(u, w, b):
    c = u.shape[-1]
    kern = w.astype(u.dtype)[:, None, :]
    y = lax.conv_general_dilated(
        u, kern, window_strides=(1,), padding=[(CONV_WIDTH - 1, 0)],
        dimension_numbers=("NWC", "WIO", "NWC"), feature_group_count=c)
    return y + b.astype(u.dtype)


def _retention_chunkwise(q, k, v):
    bsz, s, h, dk = q.shape
    dv = v.shape[-1]
    n = s // CHUNK

    def to_chunks(t):
        d = t.shape[-1]
        return t.astype(jnp.float32).reshape(bsz, n, CHUNK, h, d).transpose(1, 0, 3, 2, 4)

    qc, kc, vc = to_chunks(q), to_chunks(k), to_chunks(v)

    log_g = jnp.log(1.0 - jnp.exp2(-5.0 - jnp.arange(h, dtype=jnp.float32)))
    idx = jnp.arange(CHUNK, dtype=jnp.float32)
    diff = idx[:, None] - idx[None, :]
    decay_mask = jnp.where(diff >= 0,
                           jnp.exp(log_g[:, None, None] * jnp.maximum(diff, 0.0)[None]),
                           0.0)
    query_decay = jnp.exp(log_g[:, None] * (idx + 1.0)[None, :])
    key_decay = jnp.exp(log_g[:, None] * (CHUNK - 1.0 - idx)[None, :])
    chunk_decay = jnp.exp(log_g * CHUNK)

    def step(state, inp):
        q_i, k_i, v_i = inp
        scores = jnp.einsum("bhid,bhjd->bhij", q_i, k_i) * decay_mask[None]
        inner = jnp.einsum("bhij,bhjv->bhiv", scores, v_i)
        cross = jnp.einsum("bhid,bhdv->bhiv", q_i, state) * query_decay[None, :, :, None]
        new_state = state * chunk_decay[None, :, None, None] + jnp.einsum(
            "bhjd,bhjv->bhdv", k_i * key_decay[None, :, :, None], v_i)
        return new_state, inner + cross

    state0 = jnp.zeros((bsz, h, dk, dv), jnp.float32)
    _, out = lax.scan(step, state0, (qc, kc, vc))
    return out.transpose(1, 0, 3, 2, 4).reshape(bsz, s, h, dv)


def _head_groupnorm(o, g, b):
    mu = jnp.mean(o, axis=-1, keepdims=True)
    var = jnp.mean(jnp.square(o - mu), axis=-1, keepdims=True)
    y = ((o - mu) * lax.rsqrt(var + NORM_EPS)).reshape(o.shape[0], o.shape[1], -1)
    return y * g.astype(jnp.float32) + b.astype(jnp.float32)


def setup_inputs(seed: int = 0) -> dict:
    key = jax.random.key(seed)
    ks = jax.random.split(key, 16)
    f32 = jnp.float32
    nrm = lambda k, shape, s: jax.random.normal(k, shape, f32) * s
    return {
        "x": nrm(ks[0], (BATCH, SEQ, D_MODEL), 1.0),
        "c": nrm(ks[1], (BATCH, D_MODEL), 1.0),
        "ada_w": nrm(ks[2], (DEPTH, D_MODEL, 3 * D_MODEL), 0.5 * D_MODEL ** -0.5),
        "ada_b": nrm(ks[3], (DEPTH, 3 * D_MODEL), 0.02),
        "norm_g": 1.0 + nrm(ks[4], (DEPTH, D_MODEL), 0.02),
        "w_in": nrm(ks[5], (DEPTH, D_MODEL, N_IN), D_MODEL ** -0.5),
        "conv_w": nrm(ks[6], (DEPTH, CONV_WIDTH, D_CONV), CONV_WIDTH ** -0.5),
        "conv_b": nrm(ks[7], (DEPTH, D_CONV), 0.02),
        "conv_ln_g": 1.0 + nrm(ks[8], (DEPTH, D_CONV), 0.02),
        "conv_ln_b": nrm(ks[9], (DEPTH, D_CONV), 0.02),
        "conv_pw": nrm(ks[10], (DEPTH, D_CONV, D_CONV), D_CONV ** -0.5),
        "ret_gn_g": 1.0 + nrm(ks[11], (DEPTH, D_RET), 0.02),
        "ret_gn_b": nrm(ks[12], (DEPTH, D_RET), 0.02),
        "w_out": nrm(ks[13], (DEPTH, D_INNER, D_MODEL), D_INNER ** -0.5),
        "final_g": 1.0 + nrm(ks[14], (D_MODEL,), 0.02),
    }


def reference(x, c, ada_w, ada_b, norm_g, w_in, conv_w, conv_b, conv_ln_g, conv_ln_b,
              conv_pw, ret_gn_g, ret_gn_b, w_out, final_g):
    bsz, s, _ = x.shape
    dt = x.dtype
    c_act = jax.nn.silu(c)
    h = x
    for l in range(DEPTH):
        mod = c_act @ ada_w[l] + ada_b[l]
        shift, scale, gate = jnp.split(mod[:, None, :], 3, axis=-1)
        u = _rmsnorm(h, norm_g[l]) * (1.0 + scale) + shift

        proj = u @ w_in[l]
        o1 = D_CONV
        o2 = o1 + D_CONV
        o3 = o2 + D_CONV
        o4 = o3 + RET_HEADS * RET_QK_DIM
        o5 = o4 + RET_HEADS * RET_QK_DIM
        o6 = o5 + D_RET
        conv_a, conv_bp, conv_gate = proj[..., :o1], proj[..., o1:o2], proj[..., o2:o3]
        q = proj[..., o3:o4].reshape(bsz, s, RET_HEADS, RET_QK_DIM)
        k = proj[..., o4:o5].reshape(bsz, s, RET_HEADS, RET_QK_DIM)
        v = proj[..., o5:o6].reshape(bsz, s, RET_HEADS, RET_V_DIM)
        ret_gate = proj[..., o6:]

        a = conv_a * jax.nn.sigmoid(conv_bp)
        a = _causal_depthwise_conv(a, conv_w[l], conv_b[l])
        a = jax.nn.silu(_layernorm(a, conv_ln_g[l], conv_ln_b[l]))
        a = a @ conv_pw[l]
        y_conv = a * jax.nn.silu(conv_gate)

        q = _rotary(q)
        k = _rotary(k) * (RET_QK_DIM ** -0.5)
        r = _retention_chunkwise(q, k, v)
        r = _head_groupnorm(r, ret_gn_g[l], ret_gn_b[l]).astype(dt)
        y_ret = r * jax.nn.silu(ret_gate)

        y = jnp.concatenate([y_conv, y_ret], axis=-1) @ w_out[l]
        h = h + gate * y
    return _rmsnorm(h, final_g)
```

```python
import math
from contextlib import ExitStack

import numpy as np
import concourse.bass as bass
import concourse.mybir as mybir
from concourse.bass_utils import run_bass_kernel_spmd

F32 = mybir.dt.float32
BF16 = mybir.dt.bfloat16
AF = mybir.ActivationFunctionType
ALU = mybir.AluOpType

D = 1024
T = 2048
HALO = 128
TU = T + HALO
NTG = 4
NCH = 16
N_IN = 7168
EPS = 1e-6
CW = 31

_o = 0
def _take(n):
    global _o
    r = _o
    _o += n
    return r
C_CCOL = _take(8)
C_ABSH = _take(8)
C_ABSC = _take(8)
C_NG = _take(8)
C_CB = _take(8)
C_LNG = _take(8)
C_LNB = _take(8)
C_GNG = _take(8)
C_GNB = _take(8)
C_CWT = _take(8 * CW)
C_VDEC = _take(4)
C_EPSI = _take(4)
C_G128 = _take(4)
C_COEF = _take(16)
C_HM = _take(1)
C_MASK = _take(256)
C_IDENT = _take(128)
NCF = _o

SB_GRAN = 32
SB_SPACE = 256 * 1024
PS_BASE = SB_SPACE
PS_SPACE = 16 * 1024
DR_BASE = PS_BASE + PS_SPACE
DR_SPACE = 64 * SB_GRAN


class View:
    __slots__ = ("ap", "runs")

    def __init__(self, ap, runs):
        self.ap = ap
        self.runs = runs


class Buf:
    def __init__(self, ap, addr, shape, esize):
        self.ap = ap
        self.addr = addr
        self.shape = tuple(shape)
        self.esize = esize

    def v(self, *idx):
        idx = list(idx) + [slice(None)] * (len(self.shape) - len(idx))
        ap = self.ap[(slice(None),) + tuple(idx)]
        strides = []
        s = self.esize
        for d in reversed(self.shape):
            strides.append(s)
            s *= d
        strides = strides[::-1]
        runs = [(self.addr, self.addr)]
        sel = []
        for d, i in zip(self.shape, idx):
            if isinstance(i, slice):
                a, b, st = i.indices(d)
                assert st == 1
                sel.append((a, b))
            else:
                sel.append((i, i + 1))
        nd = len(self.shape)
        t = nd
        while t > 0 and sel[t - 1] == (0, self.shape[t - 1]):
            t -= 1
        if t == 0:
            return View(ap, [(self.addr, self.addr + s)])
        inner = strides[t - 1]
        a, b = sel[t - 1]
        base_runs = [(a * inner, b * inner)]
        for dd in range(t - 2, -1, -1):
            a, b = sel[dd]
            new = []
            for i in range(a, b):
                for (x, y) in base_runs:
                    new.append((x + i * strides[dd], y + i * strides[dd]))
            base_runs = new
        return View(ap, [(self.addr + x, self.addr + y) for (x, y) in base_runs])


class Tracker:
    def __init__(self, nsem):
        n = (DR_BASE + DR_SPACE) // SB_GRAN
        self.lw_sem = np.full(n, -1, np.int32)
        self.lw_val = np.zeros(n, np.int64)
        self.rd = np.zeros((nsem, n), np.int64)

    @staticmethod
    def _g(runs):
        out = []
        for (a, b) in runs:
            if a >= PS_BASE and a < DR_BASE:
                a = PS_BASE + (a - PS_BASE) // 2048 * 2048
                b = PS_BASE + ((b - PS_BASE) + 2047) // 2048 * 2048
            out.append((a // SB_GRAN, (b + SB_GRAN - 1) // SB_GRAN))
        return out

    def deps(self, reads, writes):
        raw, waw, war = {}, {}, {}
        for tgt, lst in ((raw, reads), (waw, writes)):
            for (a, b) in self._g(lst):
                s = self.lw_sem[a:b]
                v = self.lw_val[a:b]
                for sem in np.unique(s):
                    if sem >= 0:
                        m = int(v[s == sem].max())
                        if m > tgt.get(int(sem), 0):
                            tgt[int(sem)] = m
        for (a, b) in self._g(writes):
            m = self.rd[:, a:b].max(axis=1)
            for sem in np.nonzero(m)[0]:
                if int(m[sem]) > war.get(int(sem), 0):
                    war[int(sem)] = int(m[sem])
        return raw, waw, war

    def commit(self, reads, writes, sem, val):
        for (a, b) in self._g(reads):
            self.rd[sem, a:b] = val
        for (a, b) in self._g(writes):
            self.lw_sem[a:b] = sem
            self.lw_val[a:b] = val
            self.rd[:, a:b] = 0


ENGS = ("pe", "act", "dve", "pool", "sp")
NDQ = 8


class KB:
    def __init__(self, nc, stack):
        self.nc = nc
        self.prog = {e: [] for e in ENGS}
        self.sem_handles = []
        self.sem_of = {}

        def newsem(name):
            h = stack.enter_context(nc.semaphore(name))
            self.sem_handles.append(h)
            return len(self.sem_handles) - 1

        for e in ("pe", "act", "dve", "pool"):
            self.sem_of[e] = newsem("s_" + e)
        self.dq = {"sp": [newsem("dq_sp%d" % i) for i in range(NDQ)],
                   "pool": [newsem("dq_pl%d" % i) for i in range(NDQ)]}
        self.cc_sem = newsem("s_cc")
        self.count = {i: 0 for i in range(len(self.sem_handles))}
        self.dq_next = {"sp": 0, "pool": 0}
        self.waited = {e: {} for e in ENGS}
        self.trk = Tracker(len(self.sem_handles))
        self.ninstr = 0
        self.trace = {e: [] for e in ENGS}

    def _wait(self, eng, sem, val):
        if self.waited[eng].get(sem, 0) >= val:
            return
        self.waited[eng][sem] = val
        h = self.sem_handles[sem]
        self.prog[eng].append(lambda e, h=h, val=val: e.wait_ge(h, val))
        self.trace[eng].append(("wait", sem, val))

    def _sync(self, eng, reads, writes):
        raw, waw, war = self.trk.deps(reads, writes)
        own = self.sem_of.get(eng, None)
        need = {}
        for dct, is_raw in ((raw, True), (waw, False), (war, False)):
            for sem, val in dct.items():
                if sem == own:
                    if eng == "pe":
                        continue
                if val > need.get(sem, 0):
                    need[sem] = val
        for sem, val in need.items():
            self._wait(eng, sem, val)

    def op(self, eng, fn, reads=(), writes=(), signal=True):
        r = [x for vw in reads for x in vw.runs]
        w = [x for vw in writes for x in vw.runs]
        self._sync(eng, r, w)
        sem = self.sem_of[eng]
        val = self.count[sem] + 1
        self.trk.commit(r, w, sem, val)
        self.ninstr += 1
        if signal:
            self.count[sem] = val
            h = self.sem_handles[sem]
            self.prog[eng].append(lambda e, fn=fn, h=h: fn(e).then_inc(h, 1))
            self.trace[eng].append(("inc", sem, 1))
        else:
            self.prog[eng].append(lambda e, fn=fn: fn(e))

    def dma(self, eng, out_ap, in_ap, reads=(), writes=()):
        r = [x for vw in reads for x in vw.runs]
        w = [x for vw in writes for x in vw.runs]
        self._sync(eng, r, w)
        qi = self.dq_next[eng]
        self.dq_next[eng] = (qi + 1) % NDQ
        sem = self.dq[eng][qi]
        self._wait(eng, sem, self.count[sem])
        val = self.count[sem] + 16
        self.count[sem] = val
        self.trk.commit(r, w, sem, val)
        h = self.sem_handles[sem]
        self.prog[eng].append(lambda e, o=out_ap, i=in_ap, h=h: e.dma_start(out=o, in_=i).then_inc(h, 16))
        self.trace[eng].append(("inc", sem, 16))
        return sem, val

    def wait_event(self, eng, ev):
        self._wait(eng, ev[0], ev[1])


def simulate_sync(kb):
    pc = {e: 0 for e in ENGS}
    sem = {}
    progress = True
    while progress:
        progress = False
        for e in ENGS:
            tr_ = kb.trace[e]
            while pc[e] < len(tr_):
                kind, s_, v = tr_[pc[e]]
                if kind == "wait":
                    if sem.get(s_, 0) >= v:
                        pc[e] += 1
                        progress = True
                    else:
                        break
                else:
                    sem[s_] = sem.get(s_, 0) + v
                    pc[e] += 1
                    progress = True
    stuck = {e: (pc[e], len(kb.trace[e]), kb.trace[e][pc[e]] if pc[e] < len(kb.trace[e]) else None) for e in ENGS}
    if all(pc[e] == len(kb.trace[e]) for e in ENGS):
        return None
    return stuck, sem


def dram_view(base_idx, ap):
    a = DR_BASE + base_idx * SB_GRAN
    return View(ap, [(a, a + SB_GRAN)])


def build_nc(dbg=(), upto=99):
    nc = bass.Bass("TRN2", target_bir_lowering=False)
    stack = ExitStack()
    kb = KB(nc, stack)

    x_own = nc.dram_tensor("x_own", [T, D], F32, kind="ExternalInput").ap()
    x_halo = nc.dram_tensor("x_halo", [HALO, D], F32, kind="ExternalInput").ap()
    cf32_d = nc.dram_tensor("cf32", [128, NCF], F32, kind="ExternalInput").ap()
    rot_d = nc.dram_tensor("rot", [128, 2, T], F32, kind="ExternalInput").ap()
    ada_w_d = nc.dram_tensor("ada_w", [D, 3 * D], F32, kind="ExternalInput").ap()
    ada_b_d = nc.dram_tensor("ada_b", [3 * D], F32, kind="ExternalInput").ap()
    w_in_d = nc.dram_tensor("w_in", [D, N_IN], F32, kind="ExternalInput").ap()
    conv_pw_d = nc.dram_tensor("conv_pw", [D, D], F32, kind="ExternalInput").ap()
    w_out_d = nc.dram_tensor("w_out", [2 * D, D], F32, kind="ExternalInput").ap()
    final_g_d = nc.dram_tensor("final_g", [D], F32, kind="ExternalInput").ap()
    y_d = nc.dram_tensor("y", [T, D], F32, kind="ExternalOutput").ap()
    st_in = [nc.dram_tensor("st_in%d" % i, [512, 256], F32) for i in range(2)]
    st_all = [nc.dram_tensor("st_all%d" % i, [4 * 512, 256], F32) for i in range(2)]
    dbg_out = {}

    ada_w_r = ada_w_d.rearrange("(k p) n -> p k n", p=128)
    w_in_r = w_in_d.rearrange("(k p) n -> p k n", p=128)
    conv_pw_r = conv_pw_d.rearrange("(k p) n -> p k n", p=128)
    w_out_r = w_out_d.rearrange("(k p) n -> p k n", p=128)

    ARENA = 212736
    arena = nc.alloc_sbuf_tensor("arena", [128, ARENA // 2], BF16)
    arena_addr = 0

    class Alloc:
        def __init__(self):
            self.top = 0

        def mark(self):
            return self.top

        def reset(self, m):
            self.top = m

        def at(self, off, shape, dt):
            save = self.top
            self.top = off
            b = self(shape, dt)
            self.top = save
            return b

        def __call__(self, shape, dt):
            es = 4 if dt == F32 else 2
            n = int(np.prod(shape))
            nb = (n * es + 31) // 32 * 32
            off = self.top
            self.top += nb
            if self.top > ARENA:
                raise AssertionError("SBUF arena overflow %d (%s %s)" % (self.top, shape, dt))
            ap = arena[:, off // 2: off // 2 + n * es // 2]
            if dt == F32:
                ap = ap.bitcast(F32)
            if len(shape) == 2:
                ap = ap.rearrange("p (a b) -> p a b", b=shape[1])
            elif len(shape) == 3:
                ap = ap.rearrange("p (a b c) -> p a b c", b=shape[1], c=shape[2])
            return Buf(ap, off, shape, es)

    al = Alloc()

    ps_all = nc.alloc_psum_tensor("ps_all", [128, 4096], F32)

    def psbank(b, dt=F32, shape=None):
        ap = ps_all[:, b * 512:(b + 1) * 512]
        if dt == BF16:
            ap = ap.bitcast(BF16)
            shape = shape or (1024,)
            es = 2
        else:
            shape = shape or (512,)
            es = 4
        if len(shape) == 2:
            ap = ap.rearrange("p (a b) -> p a b", b=shape[1])
        return Buf(ap, PS_BASE + b * 2048, shape, es)

    def pspair(b):
        return Buf(ps_all[:, b * 512:(b + 2) * 512], PS_BASE + b * 2048, (1024,), 4)

    def mm(out, lhsT, rhs, start, stop, signal=None):
        kb.op("pe", lambda e: e.matmul(out.ap, lhsT.ap, rhs.ap, start=start, stop=stop),
              reads=[lhsT, rhs], writes=[out], signal=(stop if signal is None else signal))

    def tr(out, in_, ident):
        kb.op("pe", lambda e: e.transpose(out.ap, in_.ap, ident.ap), reads=[in_, ident], writes=[out])

    def act(out, in_, func, bias=None, scale=None, accum=None, eng="act"):
        reads = [in_]
        kw = {}
        if bias is not None:
            if isinstance(bias, View):
                reads.append(bias)
                kw["bias"] = bias.ap
            else:
                kw["bias"] = float(bias)
        if scale is not None:
            if isinstance(scale, View):
                reads.append(scale)
                kw["scale"] = scale.ap
            else:
                kw["scale"] = float(scale)
        writes = [out]
        if accum is not None:
            writes.append(accum)
            kw["accum_out"] = accum.ap
        kb.op("act", lambda e: e.activation(out.ap, in_.ap, func, **kw), reads=reads, writes=writes)

    def tt(out, a, b, op, eng="dve"):
        kb.op(eng, lambda e: e.tensor_tensor(out.ap, a.ap, b.ap, op), reads=[a, b], writes=[out])

    def ts(out, a, s1, op0, s2=None, op1=None, eng="dve"):
        reads = [a]
        v1 = s1.ap if isinstance(s1, View) else float(s1)
        if isinstance(s1, View):
            reads.append(s1)
        v2 = None
        if s2 is not None:
            v2 = s2.ap if isinstance(s2, View) else float(s2)
            if isinstance(s2, View):
                reads.append(s2)
        if op1 is None:
            kb.op(eng, lambda e: e.tensor_scalar(out.ap, a.ap, v1, None, op0), reads=reads, writes=[out])
        else:
            kb.op(eng, lambda e: e.tensor_scalar(out.ap, a.ap, v1, v2, op0, op1), reads=reads, writes=[out])

    def stt(out, a, s, b, op0, op1):
        reads = [a, b]
        sv = s.ap if isinstance(s, View) else float(s)
        if isinstance(s, View):
            reads.append(s)
        kb.op("dve", lambda e: e.scalar_tensor_tensor(out.ap, a.ap, sv, b.ap, op0, op1), reads=reads, writes=[out])

    def cp(out, a, eng="dve"):
        kb.op(eng, lambda e: e.tensor_copy(out.ap, a.ap), reads=[a], writes=[out])

    def recip(out, a):
        kb.op("dve", lambda e: e.reciprocal(out.ap, a.ap), reads=[a], writes=[out])

    dram_ctr = [0]

    def dload(out, src_ap, eng="sp"):
        return kb.dma(eng, out.ap, src_ap, reads=[], writes=[out])

    def dstore(dst_ap, src, eng="sp", didx=None):
        w = [dram_view(didx, dst_ap)] if didx is not None else []
        return kb.dma(eng, dst_ap, src.ap, reads=[src], writes=w)

    def dump(name, buf_view, shape):
        if name in dbg:
            t = nc.dram_tensor("dbg_" + name, [128] + list(shape), buf_view.ap.dtype, kind="ExternalOutput").ap()
            dbg_out[name] = kb.dma("sp", t, buf_view.ap, reads=[buf_view], writes=[])

    cf = al((NCF,), F32)
    ident_bf = al((128,), BF16)
    ones_bf = al((128,), BF16)
    gate_row = al((D,), F32)
    fg_row = al((D,), F32)
    gs = al((8,), F32)
    shc = al((8,), F32)
    cwh = al((8, CW), F32)
    uT = al((8, TU), BF16)
    yret = al((8, T), BF16)
    m_persist = al.mark()

    def cfv(off, n=1):
        return cf.v(slice(off, off + n))

    dload(cf.v(), cf32_d)
    dload(fg_row.v(), final_g_d.partition_broadcast(128))
    adab_row = al((D,), F32)
    dload(adab_row.v(), ada_b_d[2 * D:3 * D].partition_broadcast(128))
    adaw = al((8, 3 * D), BF16)
    for j3 in range(2):
        dload(adaw.v(slice(None), slice(j3 * D, (j3 + 1) * D)), ada_w_r[:, :, j3 * D:(j3 + 1) * D], eng="pool")
    cp(ident_bf.v(), cfv(C_IDENT, 128))
    ts(ones_bf.v(), cfv(C_IDENT, 128), 0.0, ALU.mult, 1.0, ALU.add)
    ts(cwh.v(), Buf(cf.ap[:, C_CWT:C_CWT + 8 * CW].rearrange("p (a b) -> p a b", b=CW), cf.addr + 4 * C_CWT, (8, CW), 4).v(),
       0.5, ALU.mult)
    c_act = al((8,), F32)
    c_bf = al((8,), BF16)
    c_rep = al((8, 128), BF16)
    act(c_act.v(), cfv(C_CCOL, 8), AF.Silu)
    cp(c_bf.v(), c_act.v())
    for k in range(8):
        ts(c_rep.v(k), cfv(C_IDENT, 128), 0.0, ALU.mult, c_act.v(slice(k, k + 1)), ALU.add)
    def adaln_part():
        ps_mod = psbank(0)
        for jc in range(16):
            for k in range(8):
                mm(ps_mod.v(slice(jc, jc + 1)), adaw.v(k, slice(jc * 128, (jc + 1) * 128)), c_bf.v(slice(k, k + 1)),
                   start=(k == 0), stop=(k == 7))
        tt(shc.v(), ps_mod.v(slice(0, 8)), cfv(C_ABSH, 8), ALU.add)
        sc1 = al((8,), F32)
        tt(sc1.v(), ps_mod.v(slice(8, 16)), cfv(C_ABSC, 8), ALU.add)
        stt(gs.v(), sc1.v(), 1.0, cfv(C_NG, 8), ALU.add, ALU.mult)
        dump("gs", gs.v(), (8,))
        dump("shc", shc.v(), (8,))


    def gate_part():
        ps_g = pspair(2)
        for half in range(2):
            for k in range(8):
                mm(ps_g.v(slice(half * 512, (half + 1) * 512)), c_rep.v(k),
                   adaw.v(k, slice(2 * D + half * 512, 2 * D + (half + 1) * 512)), start=(k == 0), stop=(k == 7))
        tt(gate_row.v(), ps_g.v(), adab_row.v(), ALU.add)
        dump("gate_row", gate_row.v(), (D,))

    if upto >= 1:
        xs = al((20, D), BF16)
        junk = al((D,), BF16)
        ss = al((20,), F32)
        sd = al((20,), F32)
        rs = al((20,), F32)
        xb = [al((4, D), F32) for _ in range(2)]
        x_own_r = x_own.rearrange("(g j p) d -> g p j d", j=4, p=128)
        for g in range(5):
            nt = 1 if g == 0 else 4
            b = g % 2
            if g == 0:
                dload(xb[b].v(0), x_halo)
            else:
                dload(xb[b].v(), x_own_r[g - 1])
            for j in range(nt):
                act(junk.v(), xb[b].v(j), AF.Square, accum=ss.v(slice(g * 4 + j, g * 4 + j + 1)))
            act(sd.v(slice(g * 4, g * 4 + nt)), ss.v(slice(g * 4, g * 4 + nt)), AF.Sqrt, bias=EPS, scale=1.0 / D)
            recip(rs.v(slice(g * 4, g * 4 + nt)), sd.v(slice(g * 4, g * 4 + nt)))
            for j in range(nt):
                ts(xs.v(g * 4 + j), xb[b].v(j), rs.v(slice(g * 4 + j, g * 4 + j + 1)), ALU.mult)
        adaln_part()
        dload(adaw.v(slice(None), slice(2 * D, 3 * D)), ada_w_r[:, :, 2 * D:3 * D], eng="pool")
        for g in range(5):
            nt = 1 if g == 0 else 4
            ubase = 0 if g == 0 else HALO + (g - 1) * 512
            for k in range(8):
                pt = psbank(4 + (k % 4), BF16)
                for j in range(nt):
                    tr(pt.v(slice(j * 128, (j + 1) * 128)), xs.v(g * 4 + j, slice(k * 128, (k + 1) * 128)), ident_bf.v())
                if k % 2 == 0:
                    act(uT.v(k, slice(ubase, ubase + nt * 128)), pt.v(slice(0, nt * 128)), AF.Identity,
                        bias=shc.v(slice(k, k + 1)), scale=gs.v(slice(k, k + 1)))
                else:
                    ts(uT.v(k, slice(ubase, ubase + nt * 128)), pt.v(slice(0, nt * 128)),
                       gs.v(slice(k, k + 1)), ALU.mult, shc.v(slice(k, k + 1)), ALU.add)
        dump("uT", uT.v(), (8, TU))
        gate_part()
        al.reset(m_persist)

    m_ret = al.mark()
    wbuf_n = [0]

    def load_w(src_r, col0, ncols=512, nk=8, bufs=None):
        b = bufs[wbuf_n[0] % len(bufs)]
        wbuf_n[0] += 1
        dload(b.v(slice(0, nk), slice(0, ncols)), src_r[:, :, col0:col0 + ncols], eng="pool")
        return b

    def rotary_proj(wb, hp, dst, scale, rotb, tmps, psb):
        x1s, x2s, t1, t2, t3, t4 = tmps
        for hh in range(2):
            for tg in range(NTG):
                par = (hh * NTG + tg) % 2
                pa = psbank(psb[0] + 2 * par)
                pb = psbank(psb[1] + 2 * par)
                for e, pp in ((0, pa), (1, pb)):
                    for k in range(8):
                        mm(pp.v(), wb.v(k, slice(hh * 256 + e * 128, hh * 256 + (e + 1) * 128)),
                           uT.v(k, slice(HALO + tg * 512, HALO + (tg + 1) * 512)), start=(k == 0), stop=(k == 7))
                act(x1s.v(), pa.v(), AF.Copy, scale=scale)
                act(x2s.v(), pb.v(), AF.Copy, scale=scale)
                cs = rotb.v(0, slice(tg * 512, (tg + 1) * 512))
                sn = rotb.v(1, slice(tg * 512, (tg + 1) * 512))
                tt(t1.v(), x1s.v(), cs, ALU.mult)
                tt(t2.v(), x2s.v(), sn, ALU.mult)
                tt(dst.v(hh, 0, slice(tg * 512, (tg + 1) * 512)), t1.v(), t2.v(), ALU.subtract)
                tt(t3.v(), x1s.v(), sn, ALU.mult)
                tt(t4.v(), x2s.v(), cs, ALU.mult)
                tt(dst.v(hh, 1, slice(tg * 512, (tg + 1) * 512)), t3.v(), t4.v(), ALU.add)

    if upto >= 2:
        m_rot = al.mark()
        for hp in range(2 if upto >= 3 else 1):
            al.reset(m_rot)
            kT = al((2, 2, T), BF16)
            vp = al((NCH, 512), BF16)
            qT = al((2, 2, T), BF16)
            Z = al((2, 512), F32)
            Sbb = [al((2, 512), BF16) for _ in range(2)]
            kTM = [al((512,), BF16) for _ in range(3)]
            PTb = [al((2, 128), BF16) for _ in range(2)]
            gsum = al((2, NCH), F32)
            gsq = al((2, NCH), F32)
            gmean = al((2, NCH), F32)
            gmsq = al((2, NCH), F32)
            ve = al((2, NCH), F32)
            ve2 = al((2, NCH), F32)
            sdv = al((2, NCH), F32)
            rstd = al((2, NCH), F32)
            nmr = al((2, NCH), F32)
            m_hp2 = al.mark()
            rotb = al((2, T), F32)
            dload(rotb.v(), rot_d)
            tmps = [al((512,), F32) for _ in range(6)]
            slot = [al.at(tmps[0].addr, (2, 512), F32), al.at(tmps[2].addr, (2, 512), F32)]
            Lb = slot[1]
            pad_ = al((1024,), F32)
            wbufs = [al((8, 512), BF16) for _ in range(3)]
            m_p5 = al.mark()
            if hp == 0:
                wnext = [load_w(w_in_r, 4096 + hp * 512, bufs=wbufs), load_w(w_in_r, 5120 + hp * 512, bufs=wbufs),
                         load_w(w_in_r, 3072 + hp * 512, bufs=wbufs)]
            wk, wv, wq = wnext

            rotary_proj(wk, hp, kT, 1.0 / 16.0, rotb, tmps, (0, 1))
            wg = load_w(w_in_r, 6144 + hp * 512, bufs=wbufs)

            def k_tm(n):
                pkt = psbank(4 if n % 2 == 0 else 7, BF16)
                for hh in range(2):
                    for e in range(2):
                        tr(pkt.v(slice(hh * 256 + e * 128, hh * 256 + (e + 1) * 128)),
                           kT.v(hh, e, slice(n * 128, (n + 1) * 128)), ident_bf.v())
                act(kTM[n % 3].v(), pkt.v(slice(0, 512)), AF.Copy)

            def d_s(n):
                pds = [psbank(5), psbank(6)]
                km = kTM[n % 3]
                for hh in range(2):
                    for e in range(2):
                        mm(pds[hh].v(slice(e * 256, (e + 1) * 256)),
                           km.v(slice(hh * 256 + e * 128, hh * 256 + (e + 1) * 128)),
                           vp.v(n, slice(hh * 256, (hh + 1) * 256)), start=True, stop=True)
                return pds

            def v_proj(n):
                pv = psbank(2 + (n % 2))
                for k in range(8):
                    mm(pv.v(), uT.v(k, slice(HALO + n * 128, HALO + (n + 1) * 128)), wv.v(k), start=(k == 0), stop=(k == 7))
                for hh in range(2):
                    act(vp.v(n, slice(hh * 256, (hh + 1) * 256)), pv.v(slice(hh * 256, (hh + 1) * 256)), AF.Copy,
                        scale=cfv(C_VDEC + hp * 2 + hh))

            v_proj(0)
            v_proj(1)
            k_tm(0)
            for n in range(NCH):
                if n + 1 < NCH:
                    k_tm(n + 1)
                if n + 2 < NCH:
                    v_proj(n + 2)
                pds = d_s(n)
                for hh in range(2):
                    if n == 0:
                        cp(Z.v(hh), pds[hh].v())
                    else:
                        stt(Z.v(hh), Z.v(hh), cfv(C_G128 + hp * 2 + hh), pds[hh].v(), ALU.mult, ALU.add)
            dump("kT%d" % hp, kT.v(), (2, 2, T))
            dump("vp%d" % hp, vp.v(), (NCH, 512))
            for hh in range(2):
                ts(Lb.v(hh), Z.v(hh), cfv(C_G128 + hp * 2 + hh), ALU.mult)
            dump("Lb%d" % hp, Lb.v(), (2, 512))
            ev = dstore(st_in[hp].ap().rearrange("(h e p) d -> p h e d", h=2, e=2),
                        Buf(Lb.ap.rearrange("p h (e d) -> p h e d", e=2), Lb.addr, (2, 2, 256), 4).v(), didx=hp * 4)
            kb._sync("pool", [], [])
            kb.wait_event("pool", ev)
            cch = kb.sem_handles[kb.cc_sem]
            kb.count[kb.cc_sem] += 1
            ccv = kb.count[kb.cc_sem]
            si, so = st_in[hp], st_all[hp]
            kb.trace["pool"].append(("inc", kb.cc_sem, 1))
            kb.prog["pool"].append(lambda e, si=si, so=so, cch=cch: e.collective_compute(
                "AllGather", ALU.bypass, replica_groups=[[0, 1, 2, 3], [4, 5, 6, 7]],
                ins=[si.ap().opt()], outs=[so.ap().opt()]).then_inc(cch, 1))
            cc_event = (kb.cc_sem, ccv)

            rotary_proj(wq, hp, qT, 1.0, rotb, tmps, (0, 1))
            for fc in range(4):
                for tg in range(NTG):
                    pg = psbank(2 + (tg % 2))
                    for k in range(8):
                        mm(pg.v(), wg.v(k, slice(fc * 128, (fc + 1) * 128)),
                           uT.v(k, slice(HALO + tg * 512, HALO + (tg + 1) * 512)), start=(k == 0), stop=(k == 7))
                    act(yret.v(hp * 4 + fc, slice(tg * 512, (tg + 1) * 512)), pg.v(), AF.Silu)
            dump("qT%d" % hp, qT.v(), (2, 2, T))
            dump("sg%d" % hp, yret.v(slice(hp * 4, hp * 4 + 4)), (4, T))

            for r in range(4):
                sl = slot[r % 2]
                kb.wait_event("sp", cc_event)
                dload(Buf(sl.ap.rearrange("p h (e d) -> p h e d", e=2), sl.addr, (2, 2, 256), 4).v(),
                      st_all[hp].ap()[r * 512:(r + 1) * 512, :].rearrange("(h e p) d -> p h e d", h=2, e=2))
                for hh in range(2):
                    cfc = cfv(C_COEF + r * 4 + hp * 2 + hh)
                    if r == 0:
                        ts(Z.v(hh), sl.v(hh), cfc, ALU.mult)
                    else:
                        stt(Z.v(hh), sl.v(hh), cfc, Z.v(hh), ALU.mult, ALU.add)
            dump("Sinit%d" % hp, Z.v(), (2, 512))

            if upto >= 4:
                if hp == 0:
                    wnext = [load_w(w_in_r, 4096 + 512, bufs=wbufs), load_w(w_in_r, 5120 + 512, bufs=wbufs),
                             load_w(w_in_r, 3072 + 512, bufs=wbufs)]
                al.reset(m_hp2)
                obuf = al((NCH, 512), F32)
                assert al.top <= m_p5 - 3 * 8192
                al.reset(m_p5)
                rn4 = [al((4, 512), BF16) for _ in range(1)]
                rtt4 = al((4, 512), BF16)
                gjunk = al((256,), BF16)
                for hh in range(2):
                    cp(Sbb[0].v(hh), Z.v(hh))

                def scores(n):
                    psc = psbank(0 if n % 2 == 0 else 3)
                    for hh in range(2):
                        for e in range(2):
                            mm(psc.v(slice(hh * 128, (hh + 1) * 128)), kT.v(hh, e, slice(n * 128, (n + 1) * 128)),
                               qT.v(hh, e, slice(n * 128, (n + 1) * 128)), start=(e == 0), stop=(e == 1))
                    PT = PTb[n % 2]
                    tt(Buf(PT.ap.rearrange("p a b -> p (a b)"), PT.addr, (256,), 2).v(), psc.v(slice(0, 256)),
                       cfv(C_MASK, 256), ALU.mult)

                scores(0)
                k_tm(0)
                for n in range(NCH):
                    if n < NCH - 1:
                        pds = d_s(n)
                        Sbn = Sbb[(n + 1) % 2]
                        for hh in range(2):
                            g = cfv(C_G128 + hp * 2 + hh)
                            if n == 0:
                                tt(Z.v(hh), Z.v(hh), pds[hh].v(), ALU.add)
                            else:
                                stt(Z.v(hh), Z.v(hh), g, pds[hh].v(), ALU.mult, ALU.add)
                            ts(Sbn.v(hh), Z.v(hh), g, ALU.mult)
                    if n + 1 < NCH:
                        scores(n + 1)
                        if n + 1 < NCH - 1:
                            k_tm(n + 1)
                    PT = PTb[n % 2]
                    Sb = Sbb[n % 2]
                    po = psbank(1 + (n % 2))
                    for hh in range(2):
                        mm(po.v(slice(hh * 256, (hh + 1) * 256)), PT.v(hh), vp.v(n, slice(hh * 256, (hh + 1) * 256)),
                           start=True, stop=False)
                        for e in range(2):
                            mm(po.v(slice(hh * 256, (hh + 1) * 256)), qT.v(hh, e, slice(n * 128, (n + 1) * 128)),
                               Sb.v(hh, slice(e * 256, (e + 1) * 256)), start=False, stop=(e == 1))
                    for hh in range(2):
                        pv_ = po.v(slice(hh * 256, (hh + 1) * 256))
                        act(obuf.v(n, slice(hh * 256, (hh + 1) * 256)), pv_, AF.Identity, accum=gsum.v(hh, slice(n, n + 1)))
                        act(gjunk.v(), pv_, AF.Square, accum=gsq.v(hh, slice(n, n + 1)))
                ts(gmean.v(), gsum.v(), 1.0 / 256.0, ALU.mult)
                tt(gmsq.v(), gmean.v(), gmean.v(), ALU.mult)
                stt(ve.v(), gsq.v(), 1.0 / 256.0, gmsq.v(), ALU.mult, ALU.subtract)
                for hh in range(2):
                    ts(ve2.v(hh), ve.v(hh), cfv(C_EPSI + hp * 2 + hh), ALU.add)
                act(sdv.v(), ve2.v(), AF.Sqrt)
                recip(rstd.v(), sdv.v())
                stt(nmr.v(), gmean.v(), -1.0, rstd.v(), ALU.mult, ALU.mult)
                for g4 in range(4):
                    rb = rn4[0]
                    for c in range(4):
                        n = g4 * 4 + c
                        for hh in range(2):
                            ts(rb.v(c, slice(hh * 256, (hh + 1) * 256)), obuf.v(n, slice(hh * 256, (hh + 1) * 256)),
                               rstd.v(hh, slice(n, n + 1)), ALU.mult, nmr.v(hh, slice(n, n + 1)), ALU.add)
                    b0 = 2 * (g4 % 2)
                    prt = Buf(ps_all[:, b0 * 512:(b0 + 2) * 512].bitcast(BF16).rearrange("p (a b) -> p a b", b=512),
                              PS_BASE + b0 * 2048, (4, 512), 2)
                    for fb in range(4):
                        for c in range(4):
                            tr(prt.v(fb, slice(c * 128, (c + 1) * 128)), rb.v(c, slice(fb * 128, (fb + 1) * 128)), ident_bf.v())
                    for fb in range(4):
                        act(rtt4.v(fb), prt.v(fb), AF.Identity,
                            bias=cfv(C_GNB + hp * 4 + fb), scale=cfv(C_GNG + hp * 4 + fb))
                    yv = yret.v(slice(hp * 4, hp * 4 + 4), slice(g4 * 512, (g4 + 1) * 512))
                    tt(yv, rtt4.v(), yv, ALU.mult)
                dump("yret%d" % hp, yret.v(slice(hp * 4, hp * 4 + 4)), (4, T))
        al.reset(m_ret)

    if upto >= 5:
        yconv = al((8, T), BF16)
        m_conv = al.mark()
        cT = al((8, T), BF16)
        m_cv2 = al.mark()
        wbufs = [al((8, 512), BF16) for _ in range(3)]
        a0 = [al((TU,), BF16) for _ in range(2)]
        Dg = [al((CW, 128), BF16) for _ in range(2)]
        th = [al((512,), F32) for _ in range(3)]
        cacc = [al((512,), F32) for _ in range(2)]
        pasb = [al((512,), F32) for _ in range(3)]
        NPE = 21
        wsel = {}

        def conv_inproj(cc, tgis=range(5)):
            if cc == 0 and 0 in tgis:
                wsel[("a", 0)] = load_w(w_in_r, 0, bufs=wbufs)
                wsel[("b", 0)] = load_w(w_in_r, 1024, bufs=wbufs)
                wsel[("a", 1)] = load_w(w_in_r, 512, bufs=wbufs)
            if cc == 4 and 0 in tgis:
                wsel[("b", 1)] = load_w(w_in_r, 1024 + 512, bufs=wbufs)
            wa, wb_ = wsel[("a", cc // 4)], wsel[("b", cc // 4)]
            c4 = cc % 4
            a0c = a0[cc % 2]
            dg = Dg[cc % 2]
            if 0 in tgis:
                kb.op("pool", lambda e, o=dg.v(slice(0, NPE)), c=cwh.v(cc, slice(0, NPE)): e.tensor_tensor(
                    o.ap, ident_bf.v().ap.unsqueeze(1).broadcast_to([128, NPE, 128]),
                    c.ap.unsqueeze(2).broadcast_to([128, NPE, 128]), ALU.mult),
                    reads=[ident_bf.v(), cwh.v(cc)], writes=[dg.v(slice(0, NPE))])
            for tgi in tgis:
                if tgi == 0:
                    u0, n_ = 0, HALO
                else:
                    u0, n_ = HALO + (tgi - 1) * 512, 512
                pa = psbank(0 + 2 * (tgi % 2))
                pb = psbank(1 + 2 * (tgi % 2))
                for wsrc, pp in ((wa, pa), (wb_, pb)):
                    for k in range(8):
                        mm(pp.v(slice(0, n_)), wsrc.v(k, slice(c4 * 128, (c4 + 1) * 128)), uT.v(k, slice(u0, u0 + n_)),
                           start=(k == 0), stop=(k == 7))
                thb = th[tgi % 3]
                pab = pasb[tgi % 3]
                act(thb.v(slice(0, n_)), pb.v(slice(0, n_)), AF.Tanh, scale=0.5)
                act(pab.v(slice(0, n_)), pa.v(slice(0, n_)), AF.Copy)
                stt(a0c.v(slice(u0, u0 + n_)), thb.v(slice(0, n_)), 1.0, pab.v(slice(0, n_)), ALU.add, ALU.mult)
                if tgi == 0:
                    ts(a0c.v(slice(0, HALO)), a0c.v(slice(0, HALO)), cfv(C_HM), ALU.mult)

        def conv_taps(cc, pairs=range(2)):
            a0c = a0[cc % 2]
            dg = Dg[cc % 2]
            for tp_ in pairs:
                tgs = (2 * tp_, 2 * tp_ + 1)
                pcs = {}
                for tg in tgs:
                    pc = psbank(4 + tg)
                    pcs[tg] = pc
                    for j in range(NPE):
                        mm(pc.v(), dg.v(j), a0c.v(slice(98 + tg * 512 + j, 98 + tg * 512 + j + 512)),
                           start=(j == 0), stop=(j == NPE - 1))
                for j in range(NPE, CW):
                    for tg in tgs:
                        acc = cacc[tg % 2]
                        src = pcs[tg].v() if j == NPE else acc.v()
                        stt(acc.v(), a0c.v(slice(98 + tg * 512 + j, 98 + tg * 512 + j + 512)), cwh.v(cc, slice(j, j + 1)),
                            src, ALU.mult, ALU.add)
                for tg in tgs:
                    ts(cT.v(cc, slice(tg * 512, (tg + 1) * 512)), cacc[tg % 2].v(), cfv(C_CB + cc), ALU.add)

        conv_inproj(0)
        for cc in range(8):
            if cc + 1 < 8:
                conv_inproj(cc + 1, [0, 1])
            conv_taps(cc, [0])
            if cc + 1 < 8:
                conv_inproj(cc + 1, [2, 3])
            conv_taps(cc, [1])
            if cc + 1 < 8:
                conv_inproj(cc + 1, [4])
        dump("cT", cT.v(), (8, T))
        al.reset(m_cv2)
        wbufs = [al((8, 512), BF16) for _ in range(4)]
        wg = [load_w(w_in_r, 2048 + i * 512, bufs=wbufs) for i in range(2)]
        wp = [load_w(conv_pw_r, i * 512, bufs=wbufs) for i in range(2)]
        sq8 = al((8, 512), BF16)
        mean = al((T,), F32)
        rsl = al((T,), F32)
        nml = mean
        msq = al((512,), F32)
        n1 = [al((512,), F32) for _ in range(1)] * 2
        n2 = [al((512,), F32) for _ in range(2)]
        assert al.top - sq8.addr == 32768
        wout = al.at(sq8.addr, (16, D), BF16)

        def ln_squares(tg):
            tsl = slice(tg * 512, (tg + 1) * 512)
            for cc in range(8):
                sqb = sq8.v(cc)
                if cc % 3 == 2:
                    tt(sqb, cT.v(cc, tsl), cT.v(cc, tsl), ALU.mult, eng="pool")
                else:
                    act(sqb, cT.v(cc, tsl), AF.Square)

        def ln_stats_mm(tg):
            tsl = slice(tg * 512, (tg + 1) * 512)
            p1 = psbank(4)
            p2 = psbank(5)
            for cc in range(8):
                mm(p1.v(), ones_bf.v(), cT.v(cc, tsl), start=(cc == 0), stop=(cc == 7))
                mm(p2.v(), ones_bf.v(), sq8.v(cc), start=(cc == 0), stop=(cc == 7), signal=True)

        def ln_rs(tg):
            tsl = slice(tg * 512, (tg + 1) * 512)
            p1 = psbank(4)
            p2 = psbank(5)
            ts(mean.v(tsl), p1.v(), 1.0 / D, ALU.mult)
            tt(msq.v(), mean.v(tsl), mean.v(tsl), ALU.mult)
            stt(rsl.v(tsl), p2.v(), 1.0 / D, msq.v(), ALU.mult, ALU.subtract)
            act(rsl.v(tsl), rsl.v(tsl), AF.Sqrt, bias=EPS)
            recip(rsl.v(tsl), rsl.v(tsl))
            stt(nml.v(tsl), mean.v(tsl), -1.0, rsl.v(tsl), ALU.mult, ALU.mult)

        def ln_norm(tg, ccs=range(8)):
            tsl = slice(tg * 512, (tg + 1) * 512)
            for cc in ccs:
                tt(n1[cc % 2].v(), cT.v(cc, tsl), rsl.v(tsl), ALU.mult)
                tt(n2[cc % 2].v(), n1[cc % 2].v(), nml.v(tsl), ALU.add)
                act(cT.v(cc, tsl), n2[cc % 2].v(), AF.Silu, bias=cfv(C_LNB + cc), scale=cfv(C_LNG + cc))

        def pw_gate(tg, which, ocs=range(8)):
            tsl = slice(tg * 512, (tg + 1) * 512)
            for oc in ocs:
                o4 = oc % 4
                if which == "gate":
                    pg = psbank(0 + (oc % 2))
                    for k in range(8):
                        mm(pg.v(), wg[oc // 4].v(k, slice(o4 * 128, (o4 + 1) * 128)),
                           uT.v(k, slice(HALO + tg * 512, HALO + (tg + 1) * 512)), start=(k == 0), stop=(k == 7))
                    act(yconv.v(oc, tsl), pg.v(), AF.Silu)
                else:
                    py = psbank((2, 3, 6, 7)[oc % 4])
                    for k in range(8):
                        mm(py.v(), wp[oc // 4].v(k, slice(o4 * 128, (o4 + 1) * 128)), cT.v(k, tsl), start=(k == 0), stop=(k == 7))
                    tt(yconv.v(oc, tsl), py.v(), yconv.v(oc, tsl), ALU.mult)

        pw_gate(0, "gate")
        pw_gate(1, "gate")
        ln_squares(0)
        ln_stats_mm(0)
        ln_rs(0)
        ln_norm(0)
        ln_squares(1)
        for tg in range(NTG):
            if tg + 1 < NTG:
                ln_stats_mm(tg + 1)
                ln_rs(tg + 1)
            else:
                for half in range(2):
                    dload(wout.v(slice(None), slice(half * 512, (half + 1) * 512)),
                          w_out_r[:, :, half * 512:(half + 1) * 512], eng="pool")
            for i in range(8):
                if tg + 1 < NTG:
                    ln_norm(tg + 1, [i])
                pw_gate(tg, "pw", [i])
                if tg + 2 < NTG:
                    pw_gate(tg + 2, "gate", [i])
            if tg + 2 < NTG:
                ln_squares(tg + 2)
        dump("aT", cT.v(), (8, T))
        dump("yconv", yconv.v(), (8, T))
        al.reset(m_conv)

    if upto >= 6:
        xb = [al((4, D), F32) for _ in range(1)]
        hb = [al((4, D), F32) for _ in range(2)]
        junk = al((D,), BF16)
        ss = al((16,), F32)
        sd = al((16,), F32)
        rs = al((16,), F32)
        x_own_r = x_own.rearrange("(g j p) d -> g p j d", j=4, p=128)
        y_r = y_d.rearrange("(g j p) d -> g p j d", j=4, p=128)
        out_events = []
        for g in range(4):
            b = g % 2
            dload(xb[0].v(), x_own_r[g])
            for j in range(4):
                tt_ = g * 4 + j
                pout = pspair(2 * (tt_ % 4))
                for half in range(2):
                    for k in range(16):
                        src = yconv.v(k, slice(tt_ * 128, (tt_ + 1) * 128)) if k < 8 else \
                            yret.v(k - 8, slice(tt_ * 128, (tt_ + 1) * 128))
                        mm(pout.v(slice(half * 512, (half + 1) * 512)), src, wout.v(k, slice(half * 512, (half + 1) * 512)),
                           start=(k == 0), stop=(k == 15))
                tt(hb[b].v(j), pout.v(), gate_row.v(), ALU.mult)
                tt(hb[b].v(j), hb[b].v(j), xb[0].v(j), ALU.add)
                act(junk.v(), hb[b].v(j), AF.Square, accum=ss.v(slice(tt_, tt_ + 1)))
                if g == 3:
                    act(sd.v(slice(tt_, tt_ + 1)), ss.v(slice(tt_, tt_ + 1)), AF.Sqrt, bias=EPS, scale=1.0 / D)
                    recip(rs.v(slice(tt_, tt_ + 1)), sd.v(slice(tt_, tt_ + 1)))
                    stt(hb[b].v(j), hb[b].v(j), rs.v(slice(tt_, tt_ + 1)), fg_row.v(), ALU.mult, ALU.mult)
                    out_events.append(dstore(y_r[g][:, j, :], hb[b].v(j)))
            if g == 3:
                continue
            act(sd.v(slice(g * 4, g * 4 + 4)), ss.v(slice(g * 4, g * 4 + 4)), AF.Sqrt, bias=EPS, scale=1.0 / D)
            recip(rs.v(slice(g * 4, g * 4 + 4)), sd.v(slice(g * 4, g * 4 + 4)))
            for j in range(4):
                stt(hb[b].v(j), hb[b].v(j), rs.v(slice(g * 4 + j, g * 4 + j + 1)), fg_row.v(), ALU.mult, ALU.mult)
            out_events.append(dstore(y_r[g], hb[b].v()))
        for ev in out_events:
            kb.wait_event("sp", ev)

    for ev in dbg_out.values():
        kb.wait_event("sp", ev)
    for e in ("pe", "act", "dve", "pool"):
        s = kb.sem_of[e]
        if kb.count[s] > 0:
            kb._wait("sp", s, kb.count[s])
    for q in kb.dq["sp"] + kb.dq["pool"]:
        if kb.count[q] > 0:
            kb._wait("sp", q, kb.count[q])
    if kb.count[kb.cc_sem] > 0:
        kb._wait("sp", kb.cc_sem, kb.count[kb.cc_sem])

    with nc.Block() as block:
        @block.tensor
        def _(e):
            for f in kb.prog["pe"]:
                f(e)

        @block.scalar
        def _(e):
            for f in kb.prog["act"]:
                f(e)

        @block.vector
        def _(e):
            for f in kb.prog["dve"]:
                f(e)

        @block.gpsimd
        def _(e):
            for f in kb.prog["pool"]:
                f(e)

        @block.sync
        def _(e):
            for f in kb.prog["sp"]:
                f(e)
    stack.close()
    return nc, kb


def _col(v):
    return np.ascontiguousarray(np.asarray(v, np.float32).reshape(8, 128).T)


def _gammas():
    h = np.arange(4, dtype=np.float64)
    return np.log(1.0 - np.exp2(-5.0 - h))


def make_core_inputs(core, x, c, ada_w, ada_b, norm_g, w_in, conv_w, conv_b, conv_ln_g, conv_ln_b,
                     conv_pw, ret_gn_g, ret_gn_b, w_out, final_g):
    b, s = core // 4, core % 4
    x = np.asarray(x, np.float32)
    xo = np.ascontiguousarray(x[b, s * T:(s + 1) * T])
    if s == 0:
        xh = np.zeros((HALO, D), np.float32)
    else:
        xh = np.ascontiguousarray(x[b, s * T - HALO:s * T])
    cf = np.zeros((128, NCF), np.float32)
    cf[:, C_CCOL:C_CCOL + 8] = _col(np.asarray(c)[b])
    ab = np.asarray(ada_b, np.float32).reshape(-1)
    cf[:, C_ABSH:C_ABSH + 8] = _col(ab[0:D])
    cf[:, C_ABSC:C_ABSC + 8] = _col(ab[D:2 * D])
    cf[:, C_NG:C_NG + 8] = _col(np.asarray(norm_g).reshape(-1))
    cf[:, C_CB:C_CB + 8] = _col(np.asarray(conv_b).reshape(-1))
    cf[:, C_LNG:C_LNG + 8] = _col(np.asarray(conv_ln_g).reshape(-1))
    cf[:, C_LNB:C_LNB + 8] = _col(np.asarray(conv_ln_b).reshape(-1))
    cf[:, C_GNG:C_GNG + 8] = _col(np.asarray(ret_gn_g).reshape(-1))
    cf[:, C_GNB:C_GNB + 8] = _col(np.asarray(ret_gn_b).reshape(-1))
    cw = np.asarray(conv_w, np.float32).reshape(CW, 8, 128)
    cf[:, C_CWT:C_CWT + 8 * CW] = np.transpose(cw, (2, 1, 0)).reshape(128, 8 * CW)
    lg = _gammas()
    j = np.arange(128, dtype=np.float64)
    cf[:, C_VDEC:C_VDEC + 4] = np.exp(-lg[None, :] * (j[:, None] + 1.0))
    cf[:, C_EPSI:C_EPSI + 4] = EPS * np.exp(-2.0 * lg[None, :] * (j[:, None] + 1.0))
    cf[:, C_G128:C_G128 + 4] = np.exp(lg * 128.0)[None, :]
    coef = np.zeros((4, 4), np.float64)
    for r in range(4):
        if r < s:
            coef[r] = np.exp(lg * (float(T) * (s - 1 - r)))
    cf[:, C_COEF:C_COEF + 16] = coef.reshape(1, 16)
    cf[:, C_HM] = 0.0 if s == 0 else 1.0
    m = (j[:, None] <= j[None, :]).astype(np.float32)
    cf[:, C_MASK:C_MASK + 128] = m
    cf[:, C_MASK + 128:C_MASK + 256] = m
    cf[:, C_IDENT:C_IDENT + 128] = np.eye(128, dtype=np.float32)
    inv_freq = 1.0 / (10000.0 ** np.linspace(0.0, 1.0, 128, dtype=np.float64))
    pos = np.arange(s * T, (s + 1) * T, dtype=np.float64)
    theta = pos[None, :] * inv_freq[:, None]
    rot = np.stack([np.cos(theta), np.sin(theta)], axis=1).astype(np.float32)
    return {
        "x_own": xo, "x_halo": xh, "cf32": cf, "rot": np.ascontiguousarray(rot),
        "ada_w": np.ascontiguousarray(np.asarray(ada_w, np.float32).reshape(D, 3 * D)),
        "ada_b": np.ascontiguousarray(ab),
        "w_in": np.ascontiguousarray(np.asarray(w_in, np.float32).reshape(D, N_IN)),
        "conv_pw": np.ascontiguousarray(np.asarray(conv_pw, np.float32).reshape(D, D)),
        "w_out": np.ascontiguousarray(np.asarray(w_out, np.float32).reshape(2 * D, D)),
        "final_g": np.ascontiguousarray(np.asarray(final_g, np.float32).reshape(D)),
    }


_NC_CACHE = {}


def kernel(**inputs):
    if "nc" not in _NC_CACHE:
        _NC_CACHE["nc"] = build_nc()[0]
    nc = _NC_CACHE["nc"]
    in_maps = [make_core_inputs(core, **inputs) for core in range(8)]
    res = run_bass_kernel_spmd(nc, in_maps, core_ids=list(range(8)))
    out = np.zeros((2, 4 * T, D), np.float32)
    for core in range(8):
        b, s = core // 4, core % 4
        out[b, s * T:(s + 1) * T] = np.asarray(res.results[core]["y"], np.float32)
    return out
```

```python
import math
from contextlib import ExitStack

import numpy as np
import concourse.bass as bass
import concourse.mybir as mybir
from concourse.bass_utils import run_bass_kernel_spmd

F32 = mybir.dt.float32
BF16 = mybir.dt.bfloat16
AF = mybir.ActivationFunctionType
ALU = mybir.AluOpType

D = 1024
T = 2048
HALO = 128
TU = T + HALO
NTG = 4
NCH = 16
N_IN = 7168
EPS = 1e-6
CW = 31

_o = 0
def _take(n):
    global _o
    r = _o
    _o += n
    return r
C_CCOL = _take(8)
C_ABSH = _take(8)
C_ABSC = _take(8)
C_NG = _take(8)
C_CB = _take(8)
C_LNG = _take(8)
C_LNB = _take(8)
C_GNG = _take(8)
C_GNB = _take(8)
C_CWT = _take(8 * CW)
C_VDEC = _take(4)
C_EPSI = _take(4)
C_G128 = _take(4)
C_COEF = _take(16)
C_HM = _take(1)
C_MASK = _take(256)
C_IDENT = _take(128)
NCF = _o

SB_GRAN = 32
SB_SPACE = 256 * 1024
PS_BASE = SB_SPACE
PS_SPACE = 16 * 1024
DR_BASE = PS_BASE + PS_SPACE
DR_SPACE = 64 * SB_GRAN


class View:
    __slots__ = ("ap", "runs")

    def __init__(self, ap, runs):
        self.ap = ap
        self.runs = runs


class Buf:
    def __init__(self, ap, addr, shape, esize):
        self.ap = ap
        self.addr = addr
        self.shape = tuple(shape)
        self.esize = esize

    def v(self, *idx):
        idx = list(idx) + [slice(None)] * (len(self.shape) - len(idx))
        ap = self.ap[(slice(None),) + tuple(idx)]
        strides = []
        s = self.esize
        for d in reversed(self.shape):
            strides.append(s)
            s *= d
        strides = strides[::-1]
        runs = [(self.addr, self.addr)]
        sel = []
        for d, i in zip(self.shape, idx):
            if isinstance(i, slice):
                a, b, st = i.indices(d)
                assert st == 1
                sel.append((a, b))
            else:
                sel.append((i, i + 1))
        nd = len(self.shape)
        t = nd
        while t > 0 and sel[t - 1] == (0, self.shape[t - 1]):
            t -= 1
        if t == 0:
            return View(ap, [(self.addr, self.addr + s)])
        inner = strides[t - 1]
        a, b = sel[t - 1]
        base_runs = [(a * inner, b * inner)]
        for dd in range(t - 2, -1, -1):
            a, b = sel[dd]
            new = []
            for i in range(a, b):
                for (x, y) in base_runs:
                    new.append((x + i * strides[dd], y + i * strides[dd]))
            base_runs = new
        return View(ap, [(self.addr + x, self.addr + y) for (x, y) in base_runs])


class Tracker:
    def __init__(self, nsem):
        n = (DR_BASE + DR_SPACE) // SB_GRAN
        self.lw_sem = np.full(n, -1, np.int32)
        self.lw_val = np.zeros(n, np.int64)
        self.rd = np.zeros((nsem, n), np.int64)

    @staticmethod
    def _g(runs):
        out = []
        for (a, b) in runs:
            if a >= PS_BASE and a < DR_BASE:
                a = PS_BASE + (a - PS_BASE) // 2048 * 2048
                b = PS_BASE + ((b - PS_BASE) + 2047) // 2048 * 2048
            out.append((a // SB_GRAN, (b + SB_GRAN - 1) // SB_GRAN))
        return out

    def deps(self, reads, writes):
        raw, waw, war = {}, {}, {}
        for tgt, lst in ((raw, reads), (waw, writes)):
            for (a, b) in self._g(lst):
                s = self.lw_sem[a:b]
                v = self.lw_val[a:b]
                for sem in np.unique(s):
                    if sem >= 0:
                        m = int(v[s == sem].max())
                        if m > tgt.get(int(sem), 0):
                            tgt[int(sem)] = m
        for (a, b) in self._g(writes):
            m = self.rd[:, a:b].max(axis=1)
            for sem in np.nonzero(m)[0]:
                if int(m[sem]) > war.get(int(sem), 0):
                    war[int(sem)] = int(m[sem])
        return raw, waw, war

    def commit(self, reads, writes, sem, val):
        for (a, b) in self._g(reads):
            self.rd[sem, a:b] = val
        for (a, b) in self._g(writes):
            self.lw_sem[a:b] = sem
            self.lw_val[a:b] = val
            self.rd[:, a:b] = 0


ENGS = ("pe", "act", "dve", "pool", "sp")
NDQ = 8


class KB:
    def __init__(self, nc, stack):
        self.nc = nc
        self.prog = {e: [] for e in ENGS}
        self.sem_handles = []
        self.sem_of = {}

        def newsem(name):
            h = stack.enter_context(nc.semaphore(name))
            self.sem_handles.append(h)
            return len(self.sem_handles) - 1

        for e in ("pe", "act", "dve", "pool"):
            self.sem_of[e] = newsem("s_" + e)
        self.dq = {"sp": [newsem("dq_sp%d" % i) for i in range(NDQ)],
                   "pool": [newsem("dq_pl%d" % i) for i in range(NDQ)]}
        self.cc_sem = newsem("s_cc")
        self.count = {i: 0 for i in range(len(self.sem_handles))}
        self.dq_next = {"sp": 0, "pool": 0}
        self.waited = {e: {} for e in ENGS}
        self.trk = Tracker(len(self.sem_handles))
        self.ninstr = 0
        self.trace = {e: [] for e in ENGS}

    def _wait(self, eng, sem, val):
        if self.waited[eng].get(sem, 0) >= val:
            return
        self.waited[eng][sem] = val
        h = self.sem_handles[sem]
        self.prog[eng].append(lambda e, h=h, val=val: e.wait_ge(h, val))
        self.trace[eng].append(("wait", sem, val))

    def _sync(self, eng, reads, writes):
        raw, waw, war = self.trk.deps(reads, writes)
        own = self.sem_of.get(eng, None)
        need = {}
        for dct, is_raw in ((raw, True), (waw, False), (war, False)):
            for sem, val in dct.items():
                if sem == own:
                    if eng == "pe":
                        continue
                if val > need.get(sem, 0):
                    need[sem] = val
        for sem, val in need.items():
            self._wait(eng, sem, val)

    def op(self, eng, fn, reads=(), writes=(), signal=True):
        r = [x for vw in reads for x in vw.runs]
        w = [x for vw in writes for x in vw.runs]
        self._sync(eng, r, w)
        sem = self.sem_of[eng]
        val = self.count[sem] + 1
        self.trk.commit(r, w, sem, val)
        self.ninstr += 1
        if signal:
            self.count[sem] = val
            h = self.sem_handles[sem]
            self.prog[eng].append(lambda e, fn=fn, h=h: fn(e).then_inc(h, 1))
            self.trace[eng].append(("inc", sem, 1))
        else:
            self.prog[eng].append(lambda e, fn=fn: fn(e))

    def dma(self, eng, out_ap, in_ap, reads=(), writes=()):
        r = [x for vw in reads for x in vw.runs]
        w = [x for vw in writes for x in vw.runs]
        self._sync(eng, r, w)
        qi = self.dq_next[eng]
        self.dq_next[eng] = (qi + 1) % NDQ
        sem = self.dq[eng][qi]
        self._wait(eng, sem, self.count[sem])
        val = self.count[sem] + 16
        self.count[sem] = val
        self.trk.commit(r, w, sem, val)
        h = self.sem_handles[sem]
        self.prog[eng].append(lambda e, o=out_ap, i=in_ap, h=h: e.dma_start(out=o, in_=i).then_inc(h, 16))
        self.trace[eng].append(("inc", sem, 16))
        return sem, val

    def wait_event(self, eng, ev):
        self._wait(eng, ev[0], ev[1])


def simulate_sync(kb):
    pc = {e: 0 for e in ENGS}
    sem = {}
    progress = True
    while progress:
        progress = False
        for e in ENGS:
            tr_ = kb.trace[e]
            while pc[e] < len(tr_):
                kind, s_, v = tr_[pc[e]]
                if kind == "wait":
                    if sem.get(s_, 0) >= v:
                        pc[e] += 1
                        progress = True
                    else:
                        break
                else:
                    sem[s_] = sem.get(s_, 0) + v
                    pc[e] += 1
                    progress = True
    stuck = {e: (pc[e], len(kb.trace[e]), kb.trace[e][pc[e]] if pc[e] < len(kb.trace[e]) else None) for e in ENGS}
    if all(pc[e] == len(kb.trace[e]) for e in ENGS):
        return None
    return stuck, sem


def dram_view(base_idx, ap):
    a = DR_BASE + base_idx * SB_GRAN
    return View(ap, [(a, a + SB_GRAN)])


def build_nc(dbg=(), upto=99):
    nc = bass.Bass("TRN2", target_bir_lowering=False)
    stack = ExitStack()
    kb = KB(nc, stack)

    x_own = nc.dram_tensor("x_own", [T, D], F32, kind="ExternalInput").ap()
    x_halo = nc.dram_tensor("x_halo", [HALO, D], F32, kind="ExternalInput").ap()
    cf32_d = nc.dram_tensor("cf32", [128, NCF], F32, kind="ExternalInput").ap()
    rot_d = nc.dram_tensor("rot", [128, 2, T], F32, kind="ExternalInput").ap()
    ada_w_d = nc.dram_tensor("ada_w", [D, 3 * D], F32, kind="ExternalInput").ap()
    ada_b_d = nc.dram_tensor("ada_b", [3 * D], F32, kind="ExternalInput").ap()
    w_in_d = nc.dram_tensor("w_in", [D, N_IN], F32, kind="ExternalInput").ap()
    conv_pw_d = nc.dram_tensor("conv_pw", [D, D], F32, kind="ExternalInput").ap()
    w_out_d = nc.dram_tensor("w_out", [2 * D, D], F32, kind="ExternalInput").ap()
    final_g_d = nc.dram_tensor("final_g", [D], F32, kind="ExternalInput").ap()
    y_d = nc.dram_tensor("y", [T, D], F32, kind="ExternalOutput").ap()
    st_in = [nc.dram_tensor("st_in%d" % i, [512, 256], F32) for i in range(2)]
    st_all = [nc.dram_tensor("st_all%d" % i, [4 * 512, 256], F32) for i in range(2)]
    dbg_out = {}

    ada_w_r = ada_w_d.rearrange("(k p) n -> p k n", p=128)
    w_in_r = w_in_d.rearrange("(k p) n -> p k n", p=128)
    conv_pw_r = conv_pw_d.rearrange("(k p) n -> p k n", p=128)
    w_out_r = w_out_d.rearrange("(k p) n -> p k n", p=128)

    ARENA = 212736
    arena = nc.alloc_sbuf_tensor("arena", [128, ARENA // 2], BF16)
    arena_addr = 0

    class Alloc:
        def __init__(self):
            self.top = 0

        def mark(self):
            return self.top

        def reset(self, m):
            self.top = m

        def at(self, off, shape, dt):
            save = self.top
            self.top = off
            b = self(shape, dt)
            self.top = save
            return b

        def __call__(self, shape, dt):
            es = 4 if dt == F32 else 2
            n = int(np.prod(shape))
            nb = (n * es + 31) // 32 * 32
            off = self.top
            self.top += nb
            if self.top > ARENA:
                raise AssertionError("SBUF arena overflow %d (%s %s)" % (self.top, shape, dt))
            ap = arena[:, off // 2: off // 2 + n * es // 2]
            if dt == F32:
                ap = ap.bitcast(F32)
            if len(shape) == 2:
                ap = ap.rearrange("p (a b) -> p a b", b=shape[1])
            elif len(shape) == 3:
                ap = ap.rearrange("p (a b c) -> p a b c", b=shape[1], c=shape[2])
            return Buf(ap, off, shape, es)

    al = Alloc()

    ps_all = nc.alloc_psum_tensor("ps_all", [128, 4096], F32)

    def psbank(b, dt=F32, shape=None):
        ap = ps_all[:, b * 512:(b + 1) * 512]
        if dt == BF16:
            ap = ap.bitcast(BF16)
            shape = shape or (1024,)
            es = 2
        else:
            shape = shape or (512,)
            es = 4
        if len(shape) == 2:
            ap = ap.rearrange("p (a b) -> p a b", b=shape[1])
        return Buf(ap, PS_BASE + b * 2048, shape, es)

    def pspair(b):
        return Buf(ps_all[:, b * 512:(b + 2) * 512], PS_BASE + b * 2048, (1024,), 4)

    def mm(out, lhsT, rhs, start, stop, signal=None):
        kb.op("pe", lambda e: e.matmul(out.ap, lhsT.ap, rhs.ap, start=start, stop=stop),
              reads=[lhsT, rhs], writes=[out], signal=(stop if signal is None else signal))

    def tr(out, in_, ident):
        kb.op("pe", lambda e: e.transpose(out.ap, in_.ap, ident.ap), reads=[in_, ident], writes=[out])

    def act(out, in_, func, bias=None, scale=None, accum=None, eng="act"):
        reads = [in_]
        kw = {}
        if bias is not None:
            if isinstance(bias, View):
                reads.append(bias)
                kw["bias"] = bias.ap
            else:
                kw["bias"] = float(bias)
        if scale is not None:
            if isinstance(scale, View):
                reads.append(scale)
                kw["scale"] = scale.ap
            else:
                kw["scale"] = float(scale)
        writes = [out]
        if accum is not None:
            writes.append(accum)
            kw["accum_out"] = accum.ap
        kb.op("act", lambda e: e.activation(out.ap, in_.ap, func, **kw), reads=reads, writes=writes)

    def tt(out, a, b, op, eng="dve"):
        kb.op(eng, lambda e: e.tensor_tensor(out.ap, a.ap, b.ap, op), reads=[a, b], writes=[out])

    def ts(out, a, s1, op0, s2=None, op1=None, eng="dve"):
        reads = [a]
        v1 = s1.ap if isinstance(s1, View) else float(s1)
        if isinstance(s1, View):
            reads.append(s1)
        v2 = None
        if s2 is not None:
            v2 = s2.ap if isinstance(s2, View) else float(s2)
            if isinstance(s2, View):
                reads.append(s2)
        if op1 is None:
            kb.op(eng, lambda e: e.tensor_scalar(out.ap, a.ap, v1, None, op0), reads=reads, writes=[out])
        else:
            kb.op(eng, lambda e: e.tensor_scalar(out.ap, a.ap, v1, v2, op0, op1), reads=reads, writes=[out])

    def stt(out, a, s, b, op0, op1):
        reads = [a, b]
        sv = s.ap if isinstance(s, View) else float(s)
        if isinstance(s, View):
            reads.append(s)
        kb.op("dve", lambda e: e.scalar_tensor_tensor(out.ap, a.ap, sv, b.ap, op0, op1), reads=reads, writes=[out])

    def cp(out, a, eng="dve"):
        kb.op(eng, lambda e: e.tensor_copy(out.ap, a.ap), reads=[a], writes=[out])

    def recip(out, a):
        kb.op("dve", lambda e: e.reciprocal(out.ap, a.ap), reads=[a], writes=[out])

    dram_ctr = [0]

    def dload(out, src_ap, eng="sp"):
        return kb.dma(eng, out.ap, src_ap, reads=[], writes=[out])

    def dstore(dst_ap, src, eng="sp", didx=None):
        w = [dram_view(didx, dst_ap)] if didx is not None else []
        return kb.dma(eng, dst_ap, src.ap, reads=[src], writes=w)

    def dump(name, buf_view, shape):
        if name in dbg:
            t = nc.dram_tensor("dbg_" + name, [128] + list(shape), buf_view.ap.dtype, kind="ExternalOutput").ap()
            dbg_out[name] = kb.dma("sp", t, buf_view.ap, reads=[buf_view], writes=[])

    cf = al((NCF,), F32)
    ident_bf = al((128,), BF16)
    ones_bf = al((128,), BF16)
    gate_row = al((D,), F32)
    fg_row = al((D,), F32)
    gs = al((8,), F32)
    shc = al((8,), F32)
    cwh = al((8, CW), F32)
    uT = al((8, TU), BF16)
    yret = al((8, T), BF16)
    m_persist = al.mark()

    def cfv(off, n=1):
        return cf.v(slice(off, off + n))

    dload(cf.v(), cf32_d)
    dload(fg_row.v(), final_g_d.partition_broadcast(128))
    adab_row = al((D,), F32)
    dload(adab_row.v(), ada_b_d[2 * D:3 * D].partition_broadcast(128))
    adaw = al((8, 3 * D), BF16)
    for j3 in range(2):
        dload(adaw.v(slice(None), slice(j3 * D, (j3 + 1) * D)), ada_w_r[:, :, j3 * D:(j3 + 1) * D], eng="pool")
    cp(ident_bf.v(), cfv(C_IDENT, 128))
    ts(ones_bf.v(), cfv(C_IDENT, 128), 0.0, ALU.mult, 1.0, ALU.add)
    ts(cwh.v(), Buf(cf.ap[:, C_CWT:C_CWT + 8 * CW].rearrange("p (a b) -> p a b", b=CW), cf.addr + 4 * C_CWT, (8, CW), 4).v(),
       0.5, ALU.mult)
    c_act = al((8,), F32)
    c_bf = al((8,), BF16)
    c_rep = al((8, 128), BF16)
    act(c_act.v(), cfv(C_CCOL, 8), AF.Silu)
    cp(c_bf.v(), c_act.v())
    for k in range(8):
        ts(c_rep.v(k), cfv(C_IDENT, 128), 0.0, ALU.mult, c_act.v(slice(k, k + 1)), ALU.add)
    def adaln_part():
        ps_mod = psbank(0)
        for jc in range(16):
            for k in range(8):
                mm(ps_mod.v(slice(jc, jc + 1)), adaw.v(k, slice(jc * 128, (jc + 1) * 128)), c_bf.v(slice(k, k + 1)),
                   start=(k == 0), stop=(k == 7))
        tt(shc.v(), ps_mod.v(slice(0, 8)), cfv(C_ABSH, 8), ALU.add)
        sc1 = al((8,), F32)
        tt(sc1.v(), ps_mod.v(slice(8, 16)), cfv(C_ABSC, 8), ALU.add)
        stt(gs.v(), sc1.v(), 1.0, cfv(C_NG, 8), ALU.add, ALU.mult)
        dump("gs", gs.v(), (8,))
        dump("shc", shc.v(), (8,))


    def gate_part():
        ps_g = pspair(2)
        for half in range(2):
            for k in range(8):
                mm(ps_g.v(slice(half * 512, (half + 1) * 512)), c_rep.v(k),
                   adaw.v(k, slice(2 * D + half * 512, 2 * D + (half + 1) * 512)), start=(k == 0), stop=(k == 7))
        tt(gate_row.v(), ps_g.v(), adab_row.v(), ALU.add)
        dump("gate_row", gate_row.v(), (D,))

    if upto >= 1:
        xb = [al((4, D), F32) for _ in range(2)]
        xs = al((20, D), BF16)
        junk = al((D,), BF16)
        ss = al((20,), F32)
        sd = al((20,), F32)
        rs = al((20,), F32)
        x_own_r = x_own.rearrange("(g j p) d -> g p j d", j=4, p=128)
        for g in range(5):
            nt = 1 if g == 0 else 4
            b = g % 2
            if g == 0:
                dload(xb[b].v(0), x_halo)
            else:
                dload(xb[b].v(), x_own_r[g - 1])
            for j in range(nt):
                act(junk.v(), xb[b].v(j), AF.Square, accum=ss.v(slice(g * 4 + j, g * 4 + j + 1)))
            act(sd.v(slice(g * 4, g * 4 + nt)), ss.v(slice(g * 4, g * 4 + nt)), AF.Sqrt, bias=EPS, scale=1.0 / D)
            recip(rs.v(slice(g * 4, g * 4 + nt)), sd.v(slice(g * 4, g * 4 + nt)))
            for j in range(nt):
                ts(xs.v(g * 4 + j), xb[b].v(j), rs.v(slice(g * 4 + j, g * 4 + j + 1)), ALU.mult)
        adaln_part()
        dload(adaw.v(slice(None), slice(2 * D, 3 * D)), ada_w_r[:, :, 2 * D:3 * D], eng="pool")
        for g in range(5):
            nt = 1 if g == 0 else 4
            ubase = 0 if g == 0 else HALO + (g - 1) * 512
            for k in range(8):
                pt = psbank(4 + (k % 4), BF16)
                for j in range(nt):
                    tr(pt.v(slice(j * 128, (j + 1) * 128)), xs.v(g * 4 + j, slice(k * 128, (k + 1) * 128)), ident_bf.v())
                if k % 2 == 0:
                    act(uT.v(k, slice(ubase, ubase + nt * 128)), pt.v(slice(0, nt * 128)), AF.Identity,
                        bias=shc.v(slice(k, k + 1)), scale=gs.v(slice(k, k + 1)))
                else:
                    ts(uT.v(k, slice(ubase, ubase + nt * 128)), pt.v(slice(0, nt * 128)),
                       gs.v(slice(k, k + 1)), ALU.mult, shc.v(slice(k, k + 1)), ALU.add)
        dump("uT", uT.v(), (8, TU))
        gate_part()
        al.reset(m_persist)

    m_ret = al.mark()
    wbuf_n = [0]

    def load_w(src_r, col0, ncols=512, nk=8, bufs=None):
        b = bufs[wbuf_n[0] % len(bufs)]
        wbuf_n[0] += 1
        dload(b.v(slice(0, nk), slice(0, ncols)), src_r[:, :, col0:col0 + ncols], eng="pool")
        return b

    def rotary_proj(wb, hp, dst, scale, rotb, tmps, psb):
        x1s, x2s, t1, t2, t3, t4 = tmps
        for hh in range(2):
            for tg in range(NTG):
                par = (hh * NTG + tg) % 2
                pa = psbank(psb[0] + 2 * par)
                pb = psbank(psb[1] + 2 * par)
                for e, pp in ((0, pa), (1, pb)):
                    for k in range(8):
                        mm(pp.v(), wb.v(k, slice(hh * 256 + e * 128, hh * 256 + (e + 1) * 128)),
                           uT.v(k, slice(HALO + tg * 512, HALO + (tg + 1) * 512)), start=(k == 0), stop=(k == 7))
                act(x1s.v(), pa.v(), AF.Copy, scale=scale)
                act(x2s.v(), pb.v(), AF.Copy, scale=scale)
                cs = rotb.v(0, slice(tg * 512, (tg + 1) * 512))
                sn = rotb.v(1, slice(tg * 512, (tg + 1) * 512))
                tt(t1.v(), x1s.v(), cs, ALU.mult)
                tt(t2.v(), x2s.v(), sn, ALU.mult)
                tt(dst.v(hh, 0, slice(tg * 512, (tg + 1) * 512)), t1.v(), t2.v(), ALU.subtract)
                tt(t3.v(), x1s.v(), sn, ALU.mult)
                tt(t4.v(), x2s.v(), cs, ALU.mult)
                tt(dst.v(hh, 1, slice(tg * 512, (tg + 1) * 512)), t3.v(), t4.v(), ALU.add)

    if upto >= 2:
        m_rot = al.mark()
        for hp in range(2 if upto >= 3 else 1):
            al.reset(m_rot)
            kT = al((2, 2, T), BF16)
            vp = al((NCH, 512), BF16)
            qT = al((2, 2, T), BF16)
            Z = al((2, 512), F32)
            Sbb = [al((2, 512), BF16) for _ in range(2)]
            kTM = [al((512,), BF16) for _ in range(3)]
            PTb = [al((2, 128), BF16) for _ in range(2)]
            gsum = al((2, NCH), F32)
            gsq = al((2, NCH), F32)
            gmean = al((2, NCH), F32)
            gmsq = al((2, NCH), F32)
            ve = al((2, NCH), F32)
            ve2 = al((2, NCH), F32)
            sdv = al((2, NCH), F32)
            rstd = al((2, NCH), F32)
            nmr = al((2, NCH), F32)
            m_hp2 = al.mark()
            rotb = al((2, T), F32)
            dload(rotb.v(), rot_d)
            tmps = [al((512,), F32) for _ in range(6)]
            slot = [al.at(tmps[0].addr, (2, 512), F32), al.at(tmps[2].addr, (2, 512), F32)]
            Lb = slot[1]
            pad_ = al((1024,), F32)
            wbufs = [al((8, 512), BF16) for _ in range(3)]
            m_p5 = al.mark()
            if hp == 0:
                wnext = [load_w(w_in_r, 4096 + hp * 512, bufs=wbufs), load_w(w_in_r, 5120 + hp * 512, bufs=wbufs),
                         load_w(w_in_r, 3072 + hp * 512, bufs=wbufs)]
            wk, wv, wq = wnext

            rotary_proj(wk, hp, kT, 1.0 / 16.0, rotb, tmps, (0, 1))
            wg = load_w(w_in_r, 6144 + hp * 512, bufs=wbufs)

            def k_tm(n):
                pkt = psbank(4 if n % 2 == 0 else 7, BF16)
                for hh in range(2):
                    for e in range(2):
                        tr(pkt.v(slice(hh * 256 + e * 128, hh * 256 + (e + 1) * 128)),
                           kT.v(hh, e, slice(n * 128, (n + 1) * 128)), ident_bf.v())
                act(kTM[n % 3].v(), pkt.v(slice(0, 512)), AF.Copy)

            def d_s(n):
                pds = [psbank(5), psbank(6)]
                km = kTM[n % 3]
                for hh in range(2):
                    for e in range(2):
                        mm(pds[hh].v(slice(e * 256, (e + 1) * 256)),
                           km.v(slice(hh * 256 + e * 128, hh * 256 + (e + 1) * 128)),
                           vp.v(n, slice(hh * 256, (hh + 1) * 256)), start=True, stop=True)
                return pds

            def v_proj(n):
                pv = psbank(2 + (n % 2))
                for k in range(8):
                    mm(pv.v(), uT.v(k, slice(HALO + n * 128, HALO + (n + 1) * 128)), wv.v(k), start=(k == 0), stop=(k == 7))
                for hh in range(2):
                    act(vp.v(n, slice(hh * 256, (hh + 1) * 256)), pv.v(slice(hh * 256, (hh + 1) * 256)), AF.Copy,
                        scale=cfv(C_VDEC + hp * 2 + hh))

            v_proj(0)
            v_proj(1)
            k_tm(0)
            for n in range(NCH):
                if n + 1 < NCH:
                    k_tm(n + 1)
                if n + 2 < NCH:
                    v_proj(n + 2)
                pds = d_s(n)
                for hh in range(2):
                    if n == 0:
                        cp(Z.v(hh), pds[hh].v())
                    else:
                        stt(Z.v(hh), Z.v(hh), cfv(C_G128 + hp * 2 + hh), pds[hh].v(), ALU.mult, ALU.add)
            dump("kT%d" % hp, kT.v(), (2, 2, T))
            dump("vp%d" % hp, vp.v(), (NCH, 512))
            for hh in range(2):
                ts(Lb.v(hh), Z.v(hh), cfv(C_G128 + hp * 2 + hh), ALU.mult)
            dump("Lb%d" % hp, Lb.v(), (2, 512))
            ev = dstore(st_in[hp].ap().rearrange("(h e p) d -> p h e d", h=2, e=2),
                        Buf(Lb.ap.rearrange("p h (e d) -> p h e d", e=2), Lb.addr, (2, 2, 256), 4).v(), didx=hp * 4)
            kb._sync("pool", [], [])
            kb.wait_event("pool", ev)
            cch = kb.sem_handles[kb.cc_sem]
            kb.count[kb.cc_sem] += 1
            ccv = kb.count[kb.cc_sem]
            si, so = st_in[hp], st_all[hp]
            kb.trace["pool"].append(("inc", kb.cc_sem, 1))
            kb.prog["pool"].append(lambda e, si=si, so=so, cch=cch: e.collective_compute(
                "AllGather", ALU.bypass, replica_groups=[[0, 1, 2, 3], [4, 5, 6, 7]],
                ins=[si.ap().opt()], outs=[so.ap().opt()]).then_inc(cch, 1))
            cc_event = (kb.cc_sem, ccv)

            rotary_proj(wq, hp, qT, 1.0, rotb, tmps, (0, 1))
            for fc in range(4):
                for tg in range(NTG):
                    pg = psbank(2 + (tg % 2))
                    for k in range(8):
                        mm(pg.v(), wg.v(k, slice(fc * 128, (fc + 1) * 128)),
                           uT.v(k, slice(HALO + tg * 512, HALO + (tg + 1) * 512)), start=(k == 0), stop=(k == 7))
                    act(yret.v(hp * 4 + fc, slice(tg * 512, (tg + 1) * 512)), pg.v(), AF.Silu)
            dump("qT%d" % hp, qT.v(), (2, 2, T))
            dump("sg%d" % hp, yret.v(slice(hp * 4, hp * 4 + 4)), (4, T))

            for r in range(4):
                sl = slot[r % 2]
                kb.wait_event("sp", cc_event)
                dload(Buf(sl.ap.rearrange("p h (e d) -> p h e d", e=2), sl.addr, (2, 2, 256), 4).v(),
                      st_all[hp].ap()[r * 512:(r + 1) * 512, :].rearrange("(h e p) d -> p h e d", h=2, e=2))
                for hh in range(2):
                    cfc = cfv(C_COEF + r * 4 + hp * 2 + hh)
                    if r == 0:
                        ts(Z.v(hh), sl.v(hh), cfc, ALU.mult)
                    else:
                        stt(Z.v(hh), sl.v(hh), cfc, Z.v(hh), ALU.mult, ALU.add)
            dump("Sinit%d" % hp, Z.v(), (2, 512))

            if upto >= 4:
                if hp == 0:
                    wnext = [load_w(w_in_r, 4096 + 512, bufs=wbufs), load_w(w_in_r, 5120 + 512, bufs=wbufs),
                             load_w(w_in_r, 3072 + 512, bufs=wbufs)]
                al.reset(m_hp2)
                obuf = al((NCH, 512), F32)
                assert al.top <= m_p5 - 3 * 8192
                al.reset(m_p5)
                rn4 = [al((4, 512), BF16) for _ in range(1)]
                rtt4 = al((4, 512), BF16)
                gjunk = al((256,), BF16)
                for hh in range(2):
                    cp(Sbb[0].v(hh), Z.v(hh))

                def scores(n):
                    psc = psbank(0 if n % 2 == 0 else 3)
                    for hh in range(2):
                        for e in range(2):
                            mm(psc.v(slice(hh * 128, (hh + 1) * 128)), kT.v(hh, e, slice(n * 128, (n + 1) * 128)),
                               qT.v(hh, e, slice(n * 128, (n + 1) * 128)), start=(e == 0), stop=(e == 1))
                    PT = PTb[n % 2]
                    tt(Buf(PT.ap.rearrange("p a b -> p (a b)"), PT.addr, (256,), 2).v(), psc.v(slice(0, 256)),
                       cfv(C_MASK, 256), ALU.mult)

                scores(0)
                k_tm(0)
                for n in range(NCH):
                    if n < NCH - 1:
                        pds = d_s(n)
                        Sbn = Sbb[(n + 1) % 2]
                        for hh in range(2):
                            g = cfv(C_G128 + hp * 2 + hh)
                            if n == 0:
                                tt(Z.v(hh), Z.v(hh), pds[hh].v(), ALU.add)
                            else:
                                stt(Z.v(hh), Z.v(hh), g, pds[hh].v(), ALU.mult, ALU.add)
                            ts(Sbn.v(hh), Z.v(hh), g, ALU.mult)
                    if n + 1 < NCH:
                        scores(n + 1)
                        if n + 1 < NCH - 1:
                            k_tm(n + 1)
                    PT = PTb[n % 2]
                    Sb = Sbb[n % 2]
                    po = psbank(1 + (n % 2))
                    for hh in range(2):
                        mm(po.v(slice(hh * 256, (hh + 1) * 256)), PT.v(hh), vp.v(n, slice(hh * 256, (hh + 1) * 256)),
                           start=True, stop=False)
                        for e in range(2):
                            mm(po.v(slice(hh * 256, (hh + 1) * 256)), qT.v(hh, e, slice(n * 128, (n + 1) * 128)),
                               Sb.v(hh, slice(e * 256, (e + 1) * 256)), start=False, stop=(e == 1))
                    for hh in range(2):
                        pv_ = po.v(slice(hh * 256, (hh + 1) * 256))
                        act(obuf.v(n, slice(hh * 256, (hh + 1) * 256)), pv_, AF.Identity, accum=gsum.v(hh, slice(n, n + 1)))
                        act(gjunk.v(), pv_, AF.Square, accum=gsq.v(hh, slice(n, n + 1)))
                ts(gmean.v(), gsum.v(), 1.0 / 256.0, ALU.mult)
                tt(gmsq.v(), gmean.v(), gmean.v(), ALU.mult)
                stt(ve.v(), gsq.v(), 1.0 / 256.0, gmsq.v(), ALU.mult, ALU.subtract)
                for hh in range(2):
                    ts(ve2.v(hh), ve.v(hh), cfv(C_EPSI + hp * 2 + hh), ALU.add)
                act(sdv.v(), ve2.v(), AF.Sqrt)
                recip(rstd.v(), sdv.v())
                stt(nmr.v(), gmean.v(), -1.0, rstd.v(), ALU.mult, ALU.mult)
                for g4 in range(4):
                    rb = rn4[0]
                    for c in range(4):
                        n = g4 * 4 + c
                        for hh in range(2):
                            if hh == 1 and c >= 1:
                                act(rb.v(c, slice(hh * 256, (hh + 1) * 256)), obuf.v(n, slice(hh * 256, (hh + 1) * 256)),
                                    AF.Identity, bias=nmr.v(hh, slice(n, n + 1)), scale=rstd.v(hh, slice(n, n + 1)))
                            else:
                                ts(rb.v(c, slice(hh * 256, (hh + 1) * 256)), obuf.v(n, slice(hh * 256, (hh + 1) * 256)),
                                   rstd.v(hh, slice(n, n + 1)), ALU.mult, nmr.v(hh, slice(n, n + 1)), ALU.add)
                    b0 = 2 * (g4 % 2)
                    prt = Buf(ps_all[:, b0 * 512:(b0 + 2) * 512].bitcast(BF16).rearrange("p (a b) -> p a b", b=512),
                              PS_BASE + b0 * 2048, (4, 512), 2)
                    for fb in range(4):
                        for c in range(4):
                            tr(prt.v(fb, slice(c * 128, (c + 1) * 128)), rb.v(c, slice(fb * 128, (fb + 1) * 128)), ident_bf.v())
                    for fb in range(4):
                        act(rtt4.v(fb), prt.v(fb), AF.Identity,
                            bias=cfv(C_GNB + hp * 4 + fb), scale=cfv(C_GNG + hp * 4 + fb))
                    yv = yret.v(slice(hp * 4, hp * 4 + 4), slice(g4 * 512, (g4 + 1) * 512))
                    tt(yv, rtt4.v(), yv, ALU.mult)
                dump("yret%d" % hp, yret.v(slice(hp * 4, hp * 4 + 4)), (4, T))
        al.reset(m_ret)

    if upto >= 5:
        yconv = al((8, T), BF16)
        m_conv = al.mark()
        cT = al((8, T), BF16)
        m_cv2 = al.mark()
        wbufs = [al((8, 512), BF16) for _ in range(3)]
        a0 = [al((TU,), BF16) for _ in range(2)]
        Dg = [al((CW, 128), BF16) for _ in range(2)]
        th = [al((512,), F32) for _ in range(3)]
        cacc = [al((512,), F32) for _ in range(2)]
        pasb = [al((512,), F32) for _ in range(3)]
        NPE = 21
        wsel = {}

        def conv_inproj(cc, tgis=range(5)):
            if cc == 0 and 0 in tgis:
                wsel[("a", 0)] = load_w(w_in_r, 0, bufs=wbufs)
                wsel[("b", 0)] = load_w(w_in_r, 1024, bufs=wbufs)
                wsel[("a", 1)] = load_w(w_in_r, 512, bufs=wbufs)
            if cc == 4 and 0 in tgis:
                wsel[("b", 1)] = load_w(w_in_r, 1024 + 512, bufs=wbufs)
            wa, wb_ = wsel[("a", cc // 4)], wsel[("b", cc // 4)]
            c4 = cc % 4
            a0c = a0[cc % 2]
            dg = Dg[cc % 2]
            if 0 in tgis:
                kb.op("pool", lambda e, o=dg.v(slice(0, NPE)), c=cwh.v(cc, slice(0, NPE)): e.tensor_tensor(
                    o.ap, ident_bf.v().ap.unsqueeze(1).broadcast_to([128, NPE, 128]),
                    c.ap.unsqueeze(2).broadcast_to([128, NPE, 128]), ALU.mult),
                    reads=[ident_bf.v(), cwh.v(cc)], writes=[dg.v(slice(0, NPE))])
            for tgi in tgis:
                if tgi == 0:
                    u0, n_ = 0, HALO
                else:
                    u0, n_ = HALO + (tgi - 1) * 512, 512
                pa = psbank(0 + 2 * (tgi % 2))
                pb = psbank(1 + 2 * (tgi % 2))
                for wsrc, pp in ((wa, pa), (wb_, pb)):
                    for k in range(8):
                        mm(pp.v(slice(0, n_)), wsrc.v(k, slice(c4 * 128, (c4 + 1) * 128)), uT.v(k, slice(u0, u0 + n_)),
                           start=(k == 0), stop=(k == 7))
                thb = th[tgi % 3]
                pab = pasb[tgi % 3]
                act(thb.v(slice(0, n_)), pb.v(slice(0, n_)), AF.Tanh, scale=0.5)
                act(pab.v(slice(0, n_)), pa.v(slice(0, n_)), AF.Copy)
                stt(a0c.v(slice(u0, u0 + n_)), thb.v(slice(0, n_)), 1.0, pab.v(slice(0, n_)), ALU.add, ALU.mult)
                if tgi == 0:
                    ts(a0c.v(slice(0, HALO)), a0c.v(slice(0, HALO)), cfv(C_HM), ALU.mult)

        def conv_taps(cc, pairs=range(2)):
            a0c = a0[cc % 2]
            dg = Dg[cc % 2]
            for tp_ in pairs:
                tgs = (2 * tp_, 2 * tp_ + 1)
                pcs = {}
                for tg in tgs:
                    pc = psbank(4 + tg)
                    pcs[tg] = pc
                    for j in range(NPE):
                        mm(pc.v(), dg.v(j), a0c.v(slice(98 + tg * 512 + j, 98 + tg * 512 + j + 512)),
                           start=(j == 0), stop=(j == NPE - 1))
                for j in range(NPE, CW):
                    for tg in tgs:
                        acc = cacc[tg % 2]
                        src = pcs[tg].v() if j == NPE else acc.v()
                        stt(acc.v(), a0c.v(slice(98 + tg * 512 + j, 98 + tg * 512 + j + 512)), cwh.v(cc, slice(j, j + 1)),
                            src, ALU.mult, ALU.add)
                for tg in tgs:
                    ts(cT.v(cc, slice(tg * 512, (tg + 1) * 512)), cacc[tg % 2].v(), cfv(C_CB + cc), ALU.add)

        conv_inproj(0)
        for cc in range(8):
            if cc + 1 < 8:
                conv_inproj(cc + 1, [0, 1])
            conv_taps(cc, [0])
            if cc + 1 < 8:
                conv_inproj(cc + 1, [2, 3])
            conv_taps(cc, [1])
            if cc + 1 < 8:
                conv_inproj(cc + 1, [4])
        dump("cT", cT.v(), (8, T))
        al.reset(m_cv2)
        wbufs = [al((8, 512), BF16) for _ in range(4)]
        wg = [load_w(w_in_r, 2048 + i * 512, bufs=wbufs) for i in range(2)]
        wp = [load_w(conv_pw_r, i * 512, bufs=wbufs) for i in range(2)]
        sq8 = al((8, 512), BF16)
        mean = al((T,), F32)
        rsl = al((T,), F32)
        nml = mean
        msq = al((512,), F32)
        n1 = [al((512,), F32) for _ in range(1)] * 2
        n2 = [al((512,), F32) for _ in range(2)]
        assert al.top - sq8.addr == 32768
        wout = al.at(sq8.addr, (16, D), BF16)

        def ln_squares(tg):
            tsl = slice(tg * 512, (tg + 1) * 512)
            for cc in range(8):
                sqb = sq8.v(cc)
                if cc % 3 == 2:
                    tt(sqb, cT.v(cc, tsl), cT.v(cc, tsl), ALU.mult, eng="pool")
                else:
                    act(sqb, cT.v(cc, tsl), AF.Square)

        def ln_stats_mm(tg):
            tsl = slice(tg * 512, (tg + 1) * 512)
            p1 = psbank(4)
            p2 = psbank(5)
            for cc in range(8):
                mm(p1.v(), ones_bf.v(), cT.v(cc, tsl), start=(cc == 0), stop=(cc == 7))
                mm(p2.v(), ones_bf.v(), sq8.v(cc), start=(cc == 0), stop=(cc == 7), signal=True)

        def ln_rs(tg):
            tsl = slice(tg * 512, (tg + 1) * 512)
            p1 = psbank(4)
            p2 = psbank(5)
            ts(mean.v(tsl), p1.v(), 1.0 / D, ALU.mult)
            tt(msq.v(), mean.v(tsl), mean.v(tsl), ALU.mult)
            stt(rsl.v(tsl), p2.v(), 1.0 / D, msq.v(), ALU.mult, ALU.subtract)
            act(rsl.v(tsl), rsl.v(tsl), AF.Sqrt, bias=EPS)
            recip(rsl.v(tsl), rsl.v(tsl))
            stt(nml.v(tsl), mean.v(tsl), -1.0, rsl.v(tsl), ALU.mult, ALU.mult)

        def ln_norm(tg, ccs=range(8)):
            tsl = slice(tg * 512, (tg + 1) * 512)
            for cc in ccs:
                tt(n1[cc % 2].v(), cT.v(cc, tsl), rsl.v(tsl), ALU.mult)
                tt(n2[cc % 2].v(), n1[cc % 2].v(), nml.v(tsl), ALU.add)
                act(cT.v(cc, tsl), n2[cc % 2].v(), AF.Silu, bias=cfv(C_LNB + cc), scale=cfv(C_LNG + cc))

        def pw_gate(tg, which, ocs=range(8)):
            tsl = slice(tg * 512, (tg + 1) * 512)
            for oc in ocs:
                o4 = oc % 4
                if which == "gate":
                    pg = psbank(0 + (oc % 2))
                    for k in range(8):
                        mm(pg.v(), wg[oc // 4].v(k, slice(o4 * 128, (o4 + 1) * 128)),
                           uT.v(k, slice(HALO + tg * 512, HALO + (tg + 1) * 512)), start=(k == 0), stop=(k == 7))
                    act(yconv.v(oc, tsl), pg.v(), AF.Silu)
                else:
                    py = psbank((2, 3, 6, 7)[oc % 4])
                    for k in range(8):
                        mm(py.v(), wp[oc // 4].v(k, slice(o4 * 128, (o4 + 1) * 128)), cT.v(k, tsl), start=(k == 0), stop=(k == 7))
                    tt(yconv.v(oc, tsl), py.v(), yconv.v(oc, tsl), ALU.mult)

        pw_gate(0, "gate")
        pw_gate(1, "gate")
        ln_squares(0)
        ln_stats_mm(0)
        ln_rs(0)
        ln_norm(0)
        ln_squares(1)
        for tg in range(NTG):
            if tg + 1 < NTG:
                ln_stats_mm(tg + 1)
                ln_rs(tg + 1)
            else:
                for half in range(2):
                    dload(wout.v(slice(None), slice(half * 512, (half + 1) * 512)),
                          w_out_r[:, :, half * 512:(half + 1) * 512], eng="pool")
            for i in range(8):
                if tg + 1 < NTG:
                    ln_norm(tg + 1, [i])
                pw_gate(tg, "pw", [i])
                if tg + 2 < NTG:
                    pw_gate(tg + 2, "gate", [i])
            if tg + 2 < NTG:
                ln_squares(tg + 2)
        dump("aT", cT.v(), (8, T))
        dump("yconv", yconv.v(), (8, T))
        al.reset(m_conv)

    if upto >= 6:
        xb = [al((4, D), F32) for _ in range(1)]
        hb = [al((4, D), F32) for _ in range(2)]
        junk = al((D,), BF16)
        ss = al((16,), F32)
        sd = al((16,), F32)
        rs = al((16,), F32)
        x_own_r = x_own.rearrange("(g j p) d -> g p j d", j=4, p=128)
        y_r = y_d.rearrange("(g j p) d -> g p j d", j=4, p=128)
        out_events = []
        for g in range(4):
            b = g % 2
            dload(xb[0].v(), x_own_r[g])
            for j in range(4):
                tt_ = g * 4 + j
                pout = pspair(2 * (tt_ % 4))
                for half in range(2):
                    for k in range(16):
                        src = yconv.v(k, slice(tt_ * 128, (tt_ + 1) * 128)) if k < 8 else \
                            yret.v(k - 8, slice(tt_ * 128, (tt_ + 1) * 128))
                        mm(pout.v(slice(half * 512, (half + 1) * 512)), src, wout.v(k, slice(half * 512, (half + 1) * 512)),
                           start=(k == 0), stop=(k == 15))
                tt(hb[b].v(j), pout.v(), gate_row.v(), ALU.mult)
                tt(hb[b].v(j), hb[b].v(j), xb[0].v(j), ALU.add)
                act(junk.v(), hb[b].v(j), AF.Square, accum=ss.v(slice(tt_, tt_ + 1)))
                if g == 3:
                    act(sd.v(slice(tt_, tt_ + 1)), ss.v(slice(tt_, tt_ + 1)), AF.Sqrt, bias=EPS, scale=1.0 / D)
                    recip(rs.v(slice(tt_, tt_ + 1)), sd.v(slice(tt_, tt_ + 1)))
                    stt(hb[b].v(j), hb[b].v(j), rs.v(slice(tt_, tt_ + 1)), fg_row.v(), ALU.mult, ALU.mult)
                    out_events.append(dstore(y_r[g][:, j, :], hb[b].v(j)))
            if g == 3:
                continue
            act(sd.v(slice(g * 4, g * 4 + 4)), ss.v(slice(g * 4, g * 4 + 4)), AF.Sqrt, bias=EPS, scale=1.0 / D)
            recip(rs.v(slice(g * 4, g * 4 + 4)), sd.v(slice(g * 4, g * 4 + 4)))
            for j in range(4):
                stt(hb[b].v(j), hb[b].v(j), rs.v(slice(g * 4 + j, g * 4 + j + 1)), fg_row.v(), ALU.mult, ALU.mult)
            out_events.append(dstore(y_r[g], hb[b].v()))
        for ev in out_events:
            kb.wait_event("sp", ev)

    for ev in dbg_out.values():
        kb.wait_event("sp", ev)
    for e in ("pe", "act", "dve", "pool"):
        s = kb.sem_of[e]
        if kb.count[s] > 0:
            kb._wait("sp", s, kb.count[s])
    for q in kb.dq["sp"] + kb.dq["pool"]:
        if kb.count[q] > 0:
            kb._wait("sp", q, kb.count[q])
    if kb.count[kb.cc_sem] > 0:
        kb._wait("sp", kb.cc_sem, kb.count[kb.cc_sem])

    with nc.Block() as block:
        @block.tensor
        def _(e):
            for f in kb.prog["pe"]:
                f(e)

        @block.scalar
        def _(e):
            for f in kb.prog["act"]:
                f(e)

        @block.vector
        def _(e):
            for f in kb.prog["dve"]:
                f(e)

        @block.gpsimd
        def _(e):
            for f in kb.prog["pool"]:
                f(e)

        @block.sync
        def _(e):
            for f in kb.prog["sp"]:
                f(e)
    stack.close()
    return nc, kb


def _col(v):
    return np.ascontiguousarray(np.asarray(v, np.float32).reshape(8, 128).T)


def _gammas():
    h = np.arange(4, dtype=np.float64)
    return np.log(1.0 - np.exp2(-5.0 - h))


def make_core_inputs(core, x, c, ada_w, ada_b, norm_g, w_in, conv_w, conv_b, conv_ln_g, conv_ln_b,
                     conv_pw, ret_gn_g, ret_gn_b, w_out, final_g):
    b, s = core // 4, core % 4
    x = np.asarray(x, np.float32)
    xo = np.ascontiguousarray(x[b, s * T:(s + 1) * T])
    if s == 0:
        xh = np.zeros((HALO, D), np.float32)
    else:
        xh = np.ascontiguousarray(x[b, s * T - HALO:s * T])
    cf = np.zeros((128, NCF), np.float32)
    cf[:, C_CCOL:C_CCOL + 8] = _col(np.asarray(c)[b])
    ab = np.asarray(ada_b, np.float32).reshape(-1)
    cf[:, C_ABSH:C_ABSH + 8] = _col(ab[0:D])
    cf[:, C_ABSC:C_ABSC + 8] = _col(ab[D:2 * D])
    cf[:, C_NG:C_NG + 8] = _col(np.asarray(norm_g).reshape(-1))
    cf[:, C_CB:C_CB + 8] = _col(np.asarray(conv_b).reshape(-1))
    cf[:, C_LNG:C_LNG + 8] = _col(np.asarray(conv_ln_g).reshape(-1))
    cf[:, C_LNB:C_LNB + 8] = _col(np.asarray(conv_ln_b).reshape(-1))
    cf[:, C_GNG:C_GNG + 8] = _col(np.asarray(ret_gn_g).reshape(-1))
    cf[:, C_GNB:C_GNB + 8] = _col(np.asarray(ret_gn_b).reshape(-1))
    cw = np.asarray(conv_w, np.float32).reshape(CW, 8, 128)
    cf[:, C_CWT:C_CWT + 8 * CW] = np.transpose(cw, (2, 1, 0)).reshape(128, 8 * CW)
    lg = _gammas()
    j = np.arange(128, dtype=np.float64)
    cf[:, C_VDEC:C_VDEC + 4] = np.exp(-lg[None, :] * (j[:, None] + 1.0))
    cf[:, C_EPSI:C_EPSI + 4] = EPS * np.exp(-2.0 * lg[None, :] * (j[:, None] + 1.0))
    cf[:, C_G128:C_G128 + 4] = np.exp(lg * 128.0)[None, :]
    coef = np.zeros((4, 4), np.float64)
    for r in range(4):
        if r < s:
            coef[r] = np.exp(lg * (float(T) * (s - 1 - r)))
    cf[:, C_COEF:C_COEF + 16] = coef.reshape(1, 16)
    cf[:, C_HM] = 0.0 if s == 0 else 1.0
    m = (j[:, None] <= j[None, :]).astype(np.float32)
    cf[:, C_MASK:C_MASK + 128] = m
    cf[:, C_MASK + 128:C_MASK + 256] = m
    cf[:, C_IDENT:C_IDENT + 128] = np.eye(128, dtype=np.float32)
    inv_freq = 1.0 / (10000.0 ** np.linspace(0.0, 1.0, 128, dtype=np.float64))
    pos = np.arange(s * T, (s + 1) * T, dtype=np.float64)
    theta = pos[None, :] * inv_freq[:, None]
    rot = np.stack([np.cos(theta), np.sin(theta)], axis=1).astype(np.float32)
    return {
        "x_own": xo, "x_halo": xh, "cf32": cf, "rot": np.ascontiguousarray(rot),
        "ada_w": np.ascontiguousarray(np.asarray(ada_w, np.float32).reshape(D, 3 * D)),
        "ada_b": np.ascontiguousarray(ab),
        "w_in": np.ascontiguousarray(np.asarray(w_in, np.float32).reshape(D, N_IN)),
        "conv_pw": np.ascontiguousarray(np.asarray(conv_pw, np.float32).reshape(D, D)),
        "w_out": np.ascontiguousarray(np.asarray(w_out, np.float32).reshape(2 * D, D)),
        "final_g": np.ascontiguousarray(np.asarray(final_g, np.float32).reshape(D)),
    }


_NC_CACHE = {}


def kernel(**inputs):
    if "nc" not in _NC_CACHE:
        _NC_CACHE["nc"] = build_nc()[0]
    nc = _NC_CACHE["nc"]
    in_maps = [make_core_inputs(core, **inputs) for core in range(8)]
    res = run_bass_kernel_spmd(nc, in_maps, core_ids=list(range(8)))
    out = np.zeros((2, 4 * T, D), np.float32)
    for core in range(8):
        b, s = core // 4, core % 4
        out[b, s * T:(s + 1) * T] = np.asarray(res.results[core]["y"], np.float32)
    return out
```

```python
import math
from contextlib import ExitStack

import numpy as np
import concourse.bass as bass
import concourse.mybir as mybir
from concourse.bass_utils import run_bass_kernel_spmd

F32 = mybir.dt.float32
BF16 = mybir.dt.bfloat16
AF = mybir.ActivationFunctionType
ALU = mybir.AluOpType

D = 1024
T = 2048
HALO = 128
TU = T + HALO
NTG = 4
NCH = 16
N_IN = 7168
EPS = 1e-6
CW = 31

_o = 0
def _take(n):
    global _o
    r = _o
    _o += n
    return r
C_CCOL = _take(8)
C_ABSH = _take(8)
C_ABSC = _take(8)
C_NG = _take(8)
C_CB = _take(8)
C_LNG = _take(8)
C_LNB = _take(8)
C_GNG = _take(8)
C_GNB = _take(8)
C_CWT = _take(8 * CW)
C_VDEC = _take(4)
C_EPSI = _take(4)
C_G128 = _take(4)
C_COEF = _take(16)
C_HM = _take(1)
C_MASK = _take(256)
C_IDENT = _take(128)
NCF = _o

SB_GRAN = 32
SB_SPACE = 256 * 1024
PS_BASE = SB_SPACE
PS_SPACE = 16 * 1024
DR_BASE = PS_BASE + PS_SPACE
DR_SPACE = 64 * SB_GRAN


class View:
    __slots__ = ("ap", "runs")

    def __init__(self, ap, runs):
        self.ap = ap
        self.runs = runs


class Buf:
    def __init__(self, ap, addr, shape, esize):
        self.ap = ap
        self.addr = addr
        self.shape = tuple(shape)
        self.esize = esize

    def v(self, *idx):
        idx = list(idx) + [slice(None)] * (len(self.shape) - len(idx))
        ap = self.ap[(slice(None),) + tuple(idx)]
        strides = []
        s = self.esize
        for d in reversed(self.shape):
            strides.append(s)
            s *= d
        strides = strides[::-1]
        runs = [(self.addr, self.addr)]
        sel = []
        for d, i in zip(self.shape, idx):
            if isinstance(i, slice):
                a, b, st = i.indices(d)
                assert st == 1
                sel.append((a, b))
            else:
                sel.append((i, i + 1))
        nd = len(self.shape)
        t = nd
        while t > 0 and sel[t - 1] == (0, self.shape[t - 1]):
            t -= 1
        if t == 0:
            return View(ap, [(self.addr, self.addr + s)])
        inner = strides[t - 1]
        a, b = sel[t - 1]
        base_runs = [(a * inner, b * inner)]
        for dd in range(t - 2, -1, -1):
            a, b = sel[dd]
            new = []
            for i in range(a, b):
                for (x, y) in base_runs:
                    new.append((x + i * strides[dd], y + i * strides[dd]))
            base_runs = new
        return View(ap, [(self.addr + x, self.addr + y) for (x, y) in base_runs])


class Tracker:
    def __init__(self, nsem):
        n = (DR_BASE + DR_SPACE) // SB_GRAN
        self.lw_sem = np.full(n, -1, np.int32)
        self.lw_val = np.zeros(n, np.int64)
        self.rd = np.zeros((nsem, n), np.int64)

    @staticmethod
    def _g(runs):
        out = []
        for (a, b) in runs:
            if a >= PS_BASE and a < DR_BASE:
                a = PS_BASE + (a - PS_BASE) // 2048 * 2048
                b = PS_BASE + ((b - PS_BASE) + 2047) // 2048 * 2048
            out.append((a // SB_GRAN, (b + SB_GRAN - 1) // SB_GRAN))
        return out

    def deps(self, reads, writes):
        raw, waw, war = {}, {}, {}
        for tgt, lst in ((raw, reads), (waw, writes)):
            for (a, b) in self._g(lst):
                s = self.lw_sem[a:b]
                v = self.lw_val[a:b]
                for sem in np.unique(s):
                    if sem >= 0:
                        m = int(v[s == sem].max())
                        if m > tgt.get(int(sem), 0):
                            tgt[int(sem)] = m
        for (a, b) in self._g(writes):
            m = self.rd[:, a:b].max(axis=1)
            for sem in np.nonzero(m)[0]:
                if int(m[sem]) > war.get(int(sem), 0):
                    war[int(sem)] = int(m[sem])
        return raw, waw, war

    def commit(self, reads, writes, sem, val):
        for (a, b) in self._g(reads):
            self.rd[sem, a:b] = val
        for (a, b) in self._g(writes):
            self.lw_sem[a:b] = sem
            self.lw_val[a:b] = val
            self.rd[:, a:b] = 0


ENGS = ("pe", "act", "dve", "pool", "sp")
NDQ = 8


class KB:
    def __init__(self, nc, stack):
        self.nc = nc
        self.prog = {e: [] for e in ENGS}
        self.sem_handles = []
        self.sem_of = {}

        def newsem(name):
            h = stack.enter_context(nc.semaphore(name))
            self.sem_handles.append(h)
            return len(self.sem_handles) - 1

        for e in ("pe", "act", "dve", "pool"):
            self.sem_of[e] = newsem("s_" + e)
        self.dq = {"sp": [newsem("dq_sp%d" % i) for i in range(NDQ)],
                   "pool": [newsem("dq_pl%d" % i) for i in range(NDQ)]}
        self.cc_sem = newsem("s_cc")
        self.count = {i: 0 for i in range(len(self.sem_handles))}
        self.dq_next = {"sp": 0, "pool": 0}
        self.waited = {e: {} for e in ENGS}
        self.trk = Tracker(len(self.sem_handles))
        self.ninstr = 0
        self.trace = {e: [] for e in ENGS}

    def _wait(self, eng, sem, val):
        if self.waited[eng].get(sem, 0) >= val:
            return
        self.waited[eng][sem] = val
        h = self.sem_handles[sem]
        self.prog[eng].append(lambda e, h=h, val=val: e.wait_ge(h, val))
        self.trace[eng].append(("wait", sem, val))

    def _sync(self, eng, reads, writes):
        raw, waw, war = self.trk.deps(reads, writes)
        own = self.sem_of.get(eng, None)
        need = {}
        for dct, is_raw in ((raw, True), (waw, False), (war, False)):
            for sem, val in dct.items():
                if sem == own:
                    if eng == "pe":
                        continue
                if val > need.get(sem, 0):
                    need[sem] = val
        for sem, val in need.items():
            self._wait(eng, sem, val)

    def op(self, eng, fn, reads=(), writes=(), signal=True):
        r = [x for vw in reads for x in vw.runs]
        w = [x for vw in writes for x in vw.runs]
        self._sync(eng, r, w)
        sem = self.sem_of[eng]
        val = self.count[sem] + 1
        self.trk.commit(r, w, sem, val)
        self.ninstr += 1
        if signal:
            self.count[sem] = val
            h = self.sem_handles[sem]
            self.prog[eng].append(lambda e, fn=fn, h=h: fn(e).then_inc(h, 1))
            self.trace[eng].append(("inc", sem, 1))
        else:
            self.prog[eng].append(lambda e, fn=fn: fn(e))

    def dma(self, eng, out_ap, in_ap, reads=(), writes=()):
        r = [x for vw in reads for x in vw.runs]
        w = [x for vw in writes for x in vw.runs]
        self._sync(eng, r, w)
        qi = self.dq_next[eng]
        self.dq_next[eng] = (qi + 1) % NDQ
        sem = self.dq[eng][qi]
        self._wait(eng, sem, self.count[sem])
        val = self.count[sem] + 16
        self.count[sem] = val
        self.trk.commit(r, w, sem, val)
        h = self.sem_handles[sem]
        self.prog[eng].append(lambda e, o=out_ap, i=in_ap, h=h: e.dma_start(out=o, in_=i).then_inc(h, 16))
        self.trace[eng].append(("inc", sem, 16))
        return sem, val

    def wait_event(self, eng, ev):
        self._wait(eng, ev[0], ev[1])


def simulate_sync(kb):
    pc = {e: 0 for e in ENGS}
    sem = {}
    progress = True
    while progress:
        progress = False
        for e in ENGS:
            tr_ = kb.trace[e]
            while pc[e] < len(tr_):
                kind, s_, v = tr_[pc[e]]
                if kind == "wait":
                    if sem.get(s_, 0) >= v:
                        pc[e] += 1
                        progress = True
                    else:
                        break
                else:
                    sem[s_] = sem.get(s_, 0) + v
                    pc[e] += 1
                    progress = True
    stuck = {e: (pc[e], len(kb.trace[e]), kb.trace[e][pc[e]] if pc[e] < len(kb.trace[e]) else None) for e in ENGS}
    if all(pc[e] == len(kb.trace[e]) for e in ENGS):
        return None
    return stuck, sem


def dram_view(base_idx, ap):
    a = DR_BASE + base_idx * SB_GRAN
    return View(ap, [(a, a + SB_GRAN)])


def build_nc(dbg=(), upto=99):
    nc = bass.Bass("TRN2", target_bir_lowering=False)
    stack = ExitStack()
    kb = KB(nc, stack)

    x_own = nc.dram_tensor("x_own", [T, D], F32, kind="ExternalInput").ap()
    x_halo = nc.dram_tensor("x_halo", [HALO, D], F32, kind="ExternalInput").ap()
    cf32_d = nc.dram_tensor("cf32", [128, NCF], F32, kind="ExternalInput").ap()
    rot_d = nc.dram_tensor("rot", [128, 2, T], F32, kind="ExternalInput").ap()
    ada_w_d = nc.dram_tensor("ada_w", [D, 3 * D], F32, kind="ExternalInput").ap()
    ada_b_d = nc.dram_tensor("ada_b", [3 * D], F32, kind="ExternalInput").ap()
    w_in_d = nc.dram_tensor("w_in", [D, N_IN], F32, kind="ExternalInput").ap()
    conv_pw_d = nc.dram_tensor("conv_pw", [D, D], F32, kind="ExternalInput").ap()
    w_out_d = nc.dram_tensor("w_out", [2 * D, D], F32, kind="ExternalInput").ap()
    final_g_d = nc.dram_tensor("final_g", [D], F32, kind="ExternalInput").ap()
    y_d = nc.dram_tensor("y", [T, D], F32, kind="ExternalOutput").ap()
    st_in = [nc.dram_tensor("st_in%d" % i, [512, 256], F32) for i in range(2)]
    st_all = [nc.dram_tensor("st_all%d" % i, [4 * 512, 256], F32) for i in range(2)]
    dbg_out = {}

    ada_w_r = ada_w_d.rearrange("(k p) n -> p k n", p=128)
    w_in_r = w_in_d.rearrange("(k p) n -> p k n", p=128)
    conv_pw_r = conv_pw_d.rearrange("(k p) n -> p k n", p=128)
    w_out_r = w_out_d.rearrange("(k p) n -> p k n", p=128)

    ARENA = 212736
    arena = nc.alloc_sbuf_tensor("arena", [128, ARENA // 2], BF16)
    arena_addr = 0

    class Alloc:
        def __init__(self):
            self.top = 0

        def mark(self):
            return self.top

        def reset(self, m):
            self.top = m

        def at(self, off, shape, dt):
            save = self.top
            self.top = off
            b = self(shape, dt)
            self.top = save
            return b

        def __call__(self, shape, dt):
            es = 4 if dt == F32 else 2
            n = int(np.prod(shape))
            nb = (n * es + 31) // 32 * 32
            off = self.top
            self.top += nb
            if self.top > ARENA:
                raise AssertionError("SBUF arena overflow %d (%s %s)" % (self.top, shape, dt))
            ap = arena[:, off // 2: off // 2 + n * es // 2]
            if dt == F32:
                ap = ap.bitcast(F32)
            if len(shape) == 2:
                ap = ap.rearrange("p (a b) -> p a b", b=shape[1])
            elif len(shape) == 3:
                ap = ap.rearrange("p (a b c) -> p a b c", b=shape[1], c=shape[2])
            return Buf(ap, off, shape, es)

    al = Alloc()

    ps_all = nc.alloc_psum_tensor("ps_all", [128, 4096], F32)

    def psbank(b, dt=F32, shape=None):
        ap = ps_all[:, b * 512:(b + 1) * 512]
        if dt == BF16:
            ap = ap.bitcast(BF16)
            shape = shape or (1024,)
            es = 2
        else:
            shape = shape or (512,)
            es = 4
        if len(shape) == 2:
            ap = ap.rearrange("p (a b) -> p a b", b=shape[1])
        return Buf(ap, PS_BASE + b * 2048, shape, es)

    def pspair(b):
        return Buf(ps_all[:, b * 512:(b + 2) * 512], PS_BASE + b * 2048, (1024,), 4)

    def mm(out, lhsT, rhs, start, stop, signal=None):
        kb.op("pe", lambda e: e.matmul(out.ap, lhsT.ap, rhs.ap, start=start, stop=stop),
              reads=[lhsT, rhs], writes=[out], signal=(stop if signal is None else signal))

    def tr(out, in_, ident):
        kb.op("pe", lambda e: e.transpose(out.ap, in_.ap, ident.ap), reads=[in_, ident], writes=[out])

    def act(out, in_, func, bias=None, scale=None, accum=None, eng="act"):
        reads = [in_]
        kw = {}
        if bias is not None:
            if isinstance(bias, View):
                reads.append(bias)
                kw["bias"] = bias.ap
            else:
                kw["bias"] = float(bias)
        if scale is not None:
            if isinstance(scale, View):
                reads.append(scale)
                kw["scale"] = scale.ap
            else:
                kw["scale"] = float(scale)
        writes = [out]
        if accum is not None:
            writes.append(accum)
            kw["accum_out"] = accum.ap
        kb.op("act", lambda e: e.activation(out.ap, in_.ap, func, **kw), reads=reads, writes=writes)

    def tt(out, a, b, op, eng="dve"):
        kb.op(eng, lambda e: e.tensor_tensor(out.ap, a.ap, b.ap, op), reads=[a, b], writes=[out])

    def ts(out, a, s1, op0, s2=None, op1=None, eng="dve"):
        reads = [a]
        v1 = s1.ap if isinstance(s1, View) else float(s1)
        if isinstance(s1, View):
            reads.append(s1)
        v2 = None
        if s2 is not None:
            v2 = s2.ap if isinstance(s2, View) else float(s2)
            if isinstance(s2, View):
                reads.append(s2)
        if op1 is None:
            kb.op(eng, lambda e: e.tensor_scalar(out.ap, a.ap, v1, None, op0), reads=reads, writes=[out])
        else:
            kb.op(eng, lambda e: e.tensor_scalar(out.ap, a.ap, v1, v2, op0, op1), reads=reads, writes=[out])

    def stt(out, a, s, b, op0, op1):
        reads = [a, b]
        sv = s.ap if isinstance(s, View) else float(s)
        if isinstance(s, View):
            reads.append(s)
        kb.op("dve", lambda e: e.scalar_tensor_tensor(out.ap, a.ap, sv, b.ap, op0, op1), reads=reads, writes=[out])

    def cp(out, a, eng="dve"):
        kb.op(eng, lambda e: e.tensor_copy(out.ap, a.ap), reads=[a], writes=[out])

    def recip(out, a):
        kb.op("dve", lambda e: e.reciprocal(out.ap, a.ap), reads=[a], writes=[out])

    dram_ctr = [0]

    def dload(out, src_ap, eng="sp"):
        return kb.dma(eng, out.ap, src_ap, reads=[], writes=[out])

    def dstore(dst_ap, src, eng="sp", didx=None):
        w = [dram_view(didx, dst_ap)] if didx is not None else []
        return kb.dma(eng, dst_ap, src.ap, reads=[src], writes=w)

    def dump(name, buf_view, shape):
        if name in dbg:
            t = nc.dram_tensor("dbg_" + name, [128] + list(shape), buf_view.ap.dtype, kind="ExternalOutput").ap()
            dbg_out[name] = kb.dma("sp", t, buf_view.ap, reads=[buf_view], writes=[])

    cf = al((NCF,), F32)
    ident_bf = al((128,), BF16)
    ones_bf = al((128,), BF16)
    gate_row = al((D,), F32)
    fg_row = al((D,), F32)
    gs = al((8,), F32)
    shc = al((8,), F32)
    cwh = al((8, CW), F32)
    uT = al((8, TU), BF16)
    yret = al((8, T), BF16)
    m_persist = al.mark()

    def cfv(off, n=1):
        return cf.v(slice(off, off + n))

    dload(cf.v(), cf32_d)
    dload(fg_row.v(), final_g_d.partition_broadcast(128))
    adab_row = al((D,), F32)
    dload(adab_row.v(), ada_b_d[2 * D:3 * D].partition_broadcast(128))
    adaw = al((8, 3 * D), BF16)
    for j3 in range(2):
        dload(adaw.v(slice(None), slice(j3 * D, (j3 + 1) * D)), ada_w_r[:, :, j3 * D:(j3 + 1) * D], eng="pool")
    cp(ident_bf.v(), cfv(C_IDENT, 128))
    ts(ones_bf.v(), cfv(C_IDENT, 128), 0.0, ALU.mult, 1.0, ALU.add)
    ts(cwh.v(), Buf(cf.ap[:, C_CWT:C_CWT + 8 * CW].rearrange("p (a b) -> p a b", b=CW), cf.addr + 4 * C_CWT, (8, CW), 4).v(),
       0.5, ALU.mult)
    c_act = al((8,), F32)
    c_bf = al((8,), BF16)
    c_rep = al((8, 128), BF16)
    act(c_act.v(), cfv(C_CCOL, 8), AF.Silu)
    cp(c_bf.v(), c_act.v())
    for k in range(8):
        ts(c_rep.v(k), cfv(C_IDENT, 128), 0.0, ALU.mult, c_act.v(slice(k, k + 1)), ALU.add)
    def adaln_part():
        ps_mod = psbank(0)
        for jc in range(16):
            for k in range(8):
                mm(ps_mod.v(slice(jc, jc + 1)), adaw.v(k, slice(jc * 128, (jc + 1) * 128)), c_bf.v(slice(k, k + 1)),
                   start=(k == 0), stop=(k == 7))
        tt(shc.v(), ps_mod.v(slice(0, 8)), cfv(C_ABSH, 8), ALU.add)
        sc1 = al((8,), F32)
        tt(sc1.v(), ps_mod.v(slice(8, 16)), cfv(C_ABSC, 8), ALU.add)
        stt(gs.v(), sc1.v(), 1.0, cfv(C_NG, 8), ALU.add, ALU.mult)
        dump("gs", gs.v(), (8,))
        dump("shc", shc.v(), (8,))


    def gate_part():
        ps_g = pspair(2)
        for half in range(2):
            for k in range(8):
                mm(ps_g.v(slice(half * 512, (half + 1) * 512)), c_rep.v(k),
                   adaw.v(k, slice(2 * D + half * 512, 2 * D + (half + 1) * 512)), start=(k == 0), stop=(k == 7))
        tt(gate_row.v(), ps_g.v(), adab_row.v(), ALU.add)
        dump("gate_row", gate_row.v(), (D,))

    if upto >= 1:
        xb = [al((4, D), F32) for _ in range(2)]
        xs = al((20, D), BF16)
        junk = al((D,), BF16)
        ss = al((20,), F32)
        sd = al((20,), F32)
        rs = al((20,), F32)
        x_own_r = x_own.rearrange("(g j p) d -> g p j d", j=4, p=128)
        for g in range(5):
            nt = 1 if g == 0 else 4
            b = g % 2
            if g == 0:
                dload(xb[b].v(0), x_halo)
            else:
                dload(xb[b].v(), x_own_r[g - 1])
            for j in range(nt):
                act(junk.v(), xb[b].v(j), AF.Square, accum=ss.v(slice(g * 4 + j, g * 4 + j + 1)))
            act(sd.v(slice(g * 4, g * 4 + nt)), ss.v(slice(g * 4, g * 4 + nt)), AF.Sqrt, bias=EPS, scale=1.0 / D)
            recip(rs.v(slice(g * 4, g * 4 + nt)), sd.v(slice(g * 4, g * 4 + nt)))
            for j in range(nt):
                ts(xs.v(g * 4 + j), xb[b].v(j), rs.v(slice(g * 4 + j, g * 4 + j + 1)), ALU.mult)
        adaln_part()
        dload(adaw.v(slice(None), slice(2 * D, 3 * D)), ada_w_r[:, :, 2 * D:3 * D], eng="pool")
        for g in range(5):
            nt = 1 if g == 0 else 4
            ubase = 0 if g == 0 else HALO + (g - 1) * 512
            for k in range(8):
                pt = psbank(4 + (k % 4), BF16)
                for j in range(nt):
                    tr(pt.v(slice(j * 128, (j + 1) * 128)), xs.v(g * 4 + j, slice(k * 128, (k + 1) * 128)), ident_bf.v())
                if k % 2 == 0:
                    act(uT.v(k, slice(ubase, ubase + nt * 128)), pt.v(slice(0, nt * 128)), AF.Identity,
                        bias=shc.v(slice(k, k + 1)), scale=gs.v(slice(k, k + 1)))
                else:
                    ts(uT.v(k, slice(ubase, ubase + nt * 128)), pt.v(slice(0, nt * 128)),
                       gs.v(slice(k, k + 1)), ALU.mult, shc.v(slice(k, k + 1)), ALU.add)
        dump("uT", uT.v(), (8, TU))
        gate_part()
        al.reset(m_persist)

    m_ret = al.mark()
    wbuf_n = [0]

    def load_w(src_r, col0, ncols=512, nk=8, bufs=None):
        b = bufs[wbuf_n[0] % len(bufs)]
        wbuf_n[0] += 1
        dload(b.v(slice(0, nk), slice(0, ncols)), src_r[:, :, col0:col0 + ncols], eng="pool")
        return b

    def rotary_proj(wb, hp, dst, scale, rotb, tmps, psb):
        x1s, x2s, t1, t2, t3, t4 = tmps
        for hh in range(2):
            for tg in range(NTG):
                par = (hh * NTG + tg) % 2
                pa = psbank(psb[0] + 2 * par)
                pb = psbank(psb[1] + 2 * par)
                for e, pp in ((0, pa), (1, pb)):
                    for k in range(8):
                        mm(pp.v(), wb.v(k, slice(hh * 256 + e * 128, hh * 256 + (e + 1) * 128)),
                           uT.v(k, slice(HALO + tg * 512, HALO + (tg + 1) * 512)), start=(k == 0), stop=(k == 7))
                act(x1s.v(), pa.v(), AF.Copy, scale=scale)
                act(x2s.v(), pb.v(), AF.Copy, scale=scale)
                cs = rotb.v(0, slice(tg * 512, (tg + 1) * 512))
                sn = rotb.v(1, slice(tg * 512, (tg + 1) * 512))
                tt(t1.v(), x1s.v(), cs, ALU.mult)
                tt(t2.v(), x2s.v(), sn, ALU.mult)
                tt(dst.v(hh, 0, slice(tg * 512, (tg + 1) * 512)), t1.v(), t2.v(), ALU.subtract)
                tt(t3.v(), x1s.v(), sn, ALU.mult)
                tt(t4.v(), x2s.v(), cs, ALU.mult)
                tt(dst.v(hh, 1, slice(tg * 512, (tg + 1) * 512)), t3.v(), t4.v(), ALU.add)

    if upto >= 2:
        m_rot = al.mark()
        for hp in range(2 if upto >= 3 else 1):
            al.reset(m_rot)
            kT = al((2, 2, T), BF16)
            vp = al((NCH, 512), BF16)
            qT = al((2, 2, T), BF16)
            Z = al((2, 512), F32)
            Sbb = [al((2, 512), BF16) for _ in range(2)]
            kTM = [al((512,), BF16) for _ in range(3)]
            PTb = [al((2, 128), BF16) for _ in range(2)]
            gsum = al((2, NCH), F32)
            gsq = al((2, NCH), F32)
            gmean = al((2, NCH), F32)
            gmsq = al((2, NCH), F32)
            ve = al((2, NCH), F32)
            ve2 = al((2, NCH), F32)
            sdv = al((2, NCH), F32)
            rstd = al((2, NCH), F32)
            nmr = al((2, NCH), F32)
            m_hp2 = al.mark()
            rotb = al((2, T), F32)
            dload(rotb.v(), rot_d)
            tmps = [al((512,), F32) for _ in range(6)]
            slot = [al.at(tmps[0].addr, (2, 512), F32), al.at(tmps[2].addr, (2, 512), F32)]
            Lb = slot[1]
            pad_ = al((1024,), F32)
            wbufs = [al((8, 512), BF16) for _ in range(3)]
            m_p5 = al.mark()
            if hp == 0:
                wnext = [load_w(w_in_r, 4096 + hp * 512, bufs=wbufs), load_w(w_in_r, 5120 + hp * 512, bufs=wbufs),
                         load_w(w_in_r, 3072 + hp * 512, bufs=wbufs)]
            wk, wv, wq = wnext

            rotary_proj(wk, hp, kT, 1.0 / 16.0, rotb, tmps, (0, 1))
            wg = load_w(w_in_r, 6144 + hp * 512, bufs=wbufs)

            def k_tm(n):
                pkt = psbank(4 if n % 2 == 0 else 7, BF16)
                for hh in range(2):
                    for e in range(2):
                        tr(pkt.v(slice(hh * 256 + e * 128, hh * 256 + (e + 1) * 128)),
                           kT.v(hh, e, slice(n * 128, (n + 1) * 128)), ident_bf.v())
                act(kTM[n % 3].v(), pkt.v(slice(0, 512)), AF.Copy)

            def d_s(n):
                pds = [psbank(5), psbank(6)]
                km = kTM[n % 3]
                for hh in range(2):
                    for e in range(2):
                        mm(pds[hh].v(slice(e * 256, (e + 1) * 256)),
                           km.v(slice(hh * 256 + e * 128, hh * 256 + (e + 1) * 128)),
                           vp.v(n, slice(hh * 256, (hh + 1) * 256)), start=True, stop=True)
                return pds

            def v_proj(n):
                pv = psbank(2 + (n % 2))
                for k in range(8):
                    mm(pv.v(), uT.v(k, slice(HALO + n * 128, HALO + (n + 1) * 128)), wv.v(k), start=(k == 0), stop=(k == 7))
                for hh in range(2):
                    act(vp.v(n, slice(hh * 256, (hh + 1) * 256)), pv.v(slice(hh * 256, (hh + 1) * 256)), AF.Copy,
                        scale=cfv(C_VDEC + hp * 2 + hh))

            v_proj(0)
            v_proj(1)
            k_tm(0)
            for n in range(NCH):
                if n + 1 < NCH:
                    k_tm(n + 1)
                if n + 2 < NCH:
                    v_proj(n + 2)
                pds = d_s(n)
                for hh in range(2):
                    if n == 0:
                        cp(Z.v(hh), pds[hh].v())
                    else:
                        stt(Z.v(hh), Z.v(hh), cfv(C_G128 + hp * 2 + hh), pds[hh].v(), ALU.mult, ALU.add)
            dump("kT%d" % hp, kT.v(), (2, 2, T))
            dump("vp%d" % hp, vp.v(), (NCH, 512))
            for hh in range(2):
                ts(Lb.v(hh), Z.v(hh), cfv(C_G128 + hp * 2 + hh), ALU.mult)
            dump("Lb%d" % hp, Lb.v(), (2, 512))
            ev = dstore(st_in[hp].ap().rearrange("(h e p) d -> p h e d", h=2, e=2),
                        Buf(Lb.ap.rearrange("p h (e d) -> p h e d", e=2), Lb.addr, (2, 2, 256), 4).v(), didx=hp * 4)
            kb._sync("pool", [], [])
            kb.wait_event("pool", ev)
            cch = kb.sem_handles[kb.cc_sem]
            kb.count[kb.cc_sem] += 1
            ccv = kb.count[kb.cc_sem]
            si, so = st_in[hp], st_all[hp]
            kb.trace["pool"].append(("inc", kb.cc_sem, 1))
            kb.prog["pool"].append(lambda e, si=si, so=so, cch=cch: e.collective_compute(
                "AllGather", ALU.bypass, replica_groups=[[0, 1, 2, 3], [4, 5, 6, 7]],
                ins=[si.ap().opt()], outs=[so.ap().opt()]).then_inc(cch, 1))
            cc_event = (kb.cc_sem, ccv)

            rotary_proj(wq, hp, qT, 1.0, rotb, tmps, (0, 1))
            for fc in range(4):
                for tg in range(NTG):
                    pg = psbank(2 + (tg % 2))
                    for k in range(8):
                        mm(pg.v(), wg.v(k, slice(fc * 128, (fc + 1) * 128)),
                           uT.v(k, slice(HALO + tg * 512, HALO + (tg + 1) * 512)), start=(k == 0), stop=(k == 7))
                    act(yret.v(hp * 4 + fc, slice(tg * 512, (tg + 1) * 512)), pg.v(), AF.Silu)
            dump("qT%d" % hp, qT.v(), (2, 2, T))
            dump("sg%d" % hp, yret.v(slice(hp * 4, hp * 4 + 4)), (4, T))

            for r in range(4):
                sl = slot[r % 2]
                kb.wait_event("sp", cc_event)
                dload(Buf(sl.ap.rearrange("p h (e d) -> p h e d", e=2), sl.addr, (2, 2, 256), 4).v(),
                      st_all[hp].ap()[r * 512:(r + 1) * 512, :].rearrange("(h e p) d -> p h e d", h=2, e=2))
                for hh in range(2):
                    cfc = cfv(C_COEF + r * 4 + hp * 2 + hh)
                    if r == 0:
                        ts(Z.v(hh), sl.v(hh), cfc, ALU.mult)
                    else:
                        stt(Z.v(hh), sl.v(hh), cfc, Z.v(hh), ALU.mult, ALU.add)
            dump("Sinit%d" % hp, Z.v(), (2, 512))

            if upto >= 4:
                if hp == 0:
                    wnext = [load_w(w_in_r, 4096 + 512, bufs=wbufs), load_w(w_in_r, 5120 + 512, bufs=wbufs),
                             load_w(w_in_r, 3072 + 512, bufs=wbufs)]
                al.reset(m_hp2)
                obuf = al((NCH, 512), F32)
                assert al.top <= m_p5 - 3 * 8192
                al.reset(m_p5)
                rn4 = [al((4, 512), BF16) for _ in range(1)]
                rtt4 = al((4, 512), BF16)
                gjunk = al((256,), BF16)
                for hh in range(2):
                    cp(Sbb[0].v(hh), Z.v(hh))

                def scores(n):
                    psc = psbank(0 if n % 2 == 0 else 3)
                    for hh in range(2):
                        for e in range(2):
                            mm(psc.v(slice(hh * 128, (hh + 1) * 128)), kT.v(hh, e, slice(n * 128, (n + 1) * 128)),
                               qT.v(hh, e, slice(n * 128, (n + 1) * 128)), start=(e == 0), stop=(e == 1))
                    PT = PTb[n % 2]
                    tt(Buf(PT.ap.rearrange("p a b -> p (a b)"), PT.addr, (256,), 2).v(), psc.v(slice(0, 256)),
                       cfv(C_MASK, 256), ALU.mult)

                scores(0)
                k_tm(0)
                for n in range(NCH):
                    if n < NCH - 1:
                        pds = d_s(n)
                        Sbn = Sbb[(n + 1) % 2]
                        for hh in range(2):
                            g = cfv(C_G128 + hp * 2 + hh)
                            if n == 0:
                                tt(Z.v(hh), Z.v(hh), pds[hh].v(), ALU.add)
                            else:
                                stt(Z.v(hh), Z.v(hh), g, pds[hh].v(), ALU.mult, ALU.add)
                            ts(Sbn.v(hh), Z.v(hh), g, ALU.mult)
                    if n + 1 < NCH:
                        scores(n + 1)
                        if n + 1 < NCH - 1:
                            k_tm(n + 1)
                    PT = PTb[n % 2]
                    Sb = Sbb[n % 2]
                    po = psbank(1 + (n % 2))
                    for hh in range(2):
                        mm(po.v(slice(hh * 256, (hh + 1) * 256)), PT.v(hh), vp.v(n, slice(hh * 256, (hh + 1) * 256)),
                           start=True, stop=False)
                        for e in range(2):
                            mm(po.v(slice(hh * 256, (hh + 1) * 256)), qT.v(hh, e, slice(n * 128, (n + 1) * 128)),
                               Sb.v(hh, slice(e * 256, (e + 1) * 256)), start=False, stop=(e == 1))
                    for hh in range(2):
                        pv_ = po.v(slice(hh * 256, (hh + 1) * 256))
                        act(obuf.v(n, slice(hh * 256, (hh + 1) * 256)), pv_, AF.Identity, accum=gsum.v(hh, slice(n, n + 1)))
                        act(gjunk.v(), pv_, AF.Square, accum=gsq.v(hh, slice(n, n + 1)))
                ts(gmean.v(), gsum.v(), 1.0 / 256.0, ALU.mult)
                tt(gmsq.v(), gmean.v(), gmean.v(), ALU.mult)
                stt(ve.v(), gsq.v(), 1.0 / 256.0, gmsq.v(), ALU.mult, ALU.subtract)
                for hh in range(2):
                    ts(ve2.v(hh), ve.v(hh), cfv(C_EPSI + hp * 2 + hh), ALU.add)
                act(sdv.v(), ve2.v(), AF.Sqrt)
                recip(rstd.v(), sdv.v())
                stt(nmr.v(), gmean.v(), -1.0, rstd.v(), ALU.mult, ALU.mult)
                for g4 in range(4):
                    rb = rn4[0]
                    for c in range(4):
                        n = g4 * 4 + c
                        for hh in range(2):
                            ts(rb.v(c, slice(hh * 256, (hh + 1) * 256)), obuf.v(n, slice(hh * 256, (hh + 1) * 256)),
                               rstd.v(hh, slice(n, n + 1)), ALU.mult, nmr.v(hh, slice(n, n + 1)), ALU.add)
                    b0 = 2 * (g4 % 2)
                    prt = Buf(ps_all[:, b0 * 512:(b0 + 2) * 512].bitcast(BF16).rearrange("p (a b) -> p a b", b=512),
                              PS_BASE + b0 * 2048, (4, 512), 2)
                    for fb in range(4):
                        for c in range(4):
                            tr(prt.v(fb, slice(c * 128, (c + 1) * 128)), rb.v(c, slice(fb * 128, (fb + 1) * 128)), ident_bf.v())
                    for fb in range(4):
                        act(rtt4.v(fb), prt.v(fb), AF.Identity,
                            bias=cfv(C_GNB + hp * 4 + fb), scale=cfv(C_GNG + hp * 4 + fb))
                    yv = yret.v(slice(hp * 4, hp * 4 + 4), slice(g4 * 512, (g4 + 1) * 512))
                    tt(yv, rtt4.v(), yv, ALU.mult)
                dump("yret%d" % hp, yret.v(slice(hp * 4, hp * 4 + 4)), (4, T))
        al.reset(m_ret)

    if upto >= 5:
        yconv = al((8, T), BF16)
        m_conv = al.mark()
        cT = al((8, T), BF16)
        m_cv2 = al.mark()
        wbufs = [al((8, 512), BF16) for _ in range(3)]
        a0 = [al((TU,), BF16) for _ in range(2)]
        Dg = [al((CW, 128), BF16) for _ in range(2)]
        th = [al((512,), F32) for _ in range(3)]
        cacc = [al((512,), F32) for _ in range(2)]
        pasb = [al((512,), F32) for _ in range(3)]
        NPE = 20
        wsel = {}

        def conv_inproj(cc, tgis=range(5)):
            if cc == 0 and 0 in tgis:
                wsel[("a", 0)] = load_w(w_in_r, 0, bufs=wbufs)
                wsel[("b", 0)] = load_w(w_in_r, 1024, bufs=wbufs)
                wsel[("a", 1)] = load_w(w_in_r, 512, bufs=wbufs)
            if cc == 4 and 0 in tgis:
                wsel[("b", 1)] = load_w(w_in_r, 1024 + 512, bufs=wbufs)
            wa, wb_ = wsel[("a", cc // 4)], wsel[("b", cc // 4)]
            c4 = cc % 4
            a0c = a0[cc % 2]
            dg = Dg[cc % 2]
            if 0 in tgis:
                kb.op("pool", lambda e, o=dg.v(slice(0, NPE)), c=cwh.v(cc, slice(0, NPE)): e.tensor_tensor(
                    o.ap, ident_bf.v().ap.unsqueeze(1).broadcast_to([128, NPE, 128]),
                    c.ap.unsqueeze(2).broadcast_to([128, NPE, 128]), ALU.mult),
                    reads=[ident_bf.v(), cwh.v(cc)], writes=[dg.v(slice(0, NPE))])
            for tgi in tgis:
                if tgi == 0:
                    u0, n_ = 0, HALO
                else:
                    u0, n_ = HALO + (tgi - 1) * 512, 512
                pa = psbank(0 + 2 * (tgi % 2))
                pb = psbank(1 + 2 * (tgi % 2))
                for wsrc, pp in ((wa, pa), (wb_, pb)):
                    for k in range(8):
                        mm(pp.v(slice(0, n_)), wsrc.v(k, slice(c4 * 128, (c4 + 1) * 128)), uT.v(k, slice(u0, u0 + n_)),
                           start=(k == 0), stop=(k == 7))
                thb = th[tgi % 3]
                pab = pasb[tgi % 3]
                act(thb.v(slice(0, n_)), pb.v(slice(0, n_)), AF.Tanh, scale=0.5)
                act(pab.v(slice(0, n_)), pa.v(slice(0, n_)), AF.Copy)
                stt(a0c.v(slice(u0, u0 + n_)), thb.v(slice(0, n_)), 1.0, pab.v(slice(0, n_)), ALU.add, ALU.mult)
                if tgi == 0:
                    ts(a0c.v(slice(0, HALO)), a0c.v(slice(0, HALO)), cfv(C_HM), ALU.mult)

        def conv_taps(cc, pairs=range(2)):
            a0c = a0[cc % 2]
            dg = Dg[cc % 2]
            for tp_ in pairs:
                tgs = (2 * tp_, 2 * tp_ + 1)
                pcs = {}
                for tg in tgs:
                    pc = psbank(4 + tg)
                    pcs[tg] = pc
                    for j in range(NPE):
                        mm(pc.v(), dg.v(j), a0c.v(slice(98 + tg * 512 + j, 98 + tg * 512 + j + 512)),
                           start=(j == 0), stop=(j == NPE - 1))
                for j in range(NPE, CW):
                    for tg in tgs:
                        acc = cacc[tg % 2]
                        src = pcs[tg].v() if j == NPE else acc.v()
                        stt(acc.v(), a0c.v(slice(98 + tg * 512 + j, 98 + tg * 512 + j + 512)), cwh.v(cc, slice(j, j + 1)),
                            src, ALU.mult, ALU.add)
                for tg in tgs:
                    ts(cT.v(cc, slice(tg * 512, (tg + 1) * 512)), cacc[tg % 2].v(), cfv(C_CB + cc), ALU.add)

        conv_inproj(0)
        for cc in range(8):
            if cc + 1 < 8:
                conv_inproj(cc + 1, [0, 1])
            conv_taps(cc, [0])
            if cc + 1 < 8:
                conv_inproj(cc + 1, [2, 3])
            conv_taps(cc, [1])
            if cc + 1 < 8:
                conv_inproj(cc + 1, [4])
        dump("cT", cT.v(), (8, T))
        al.reset(m_cv2)
        wbufs = [al((8, 512), BF16) for _ in range(4)]
        wg = [load_w(w_in_r, 2048 + i * 512, bufs=wbufs) for i in range(2)]
        wp = [load_w(conv_pw_r, i * 512, bufs=wbufs) for i in range(2)]
        sq8 = al((8, 512), BF16)
        mean = al((T,), F32)
        rsl = al((T,), F32)
        nml = mean
        msq = al((512,), F32)
        n1 = [al((512,), F32) for _ in range(1)] * 2
        n2 = [al((512,), F32) for _ in range(2)]
        assert al.top - sq8.addr == 32768
        wout = al.at(sq8.addr, (16, D), BF16)

        def ln_squares(tg):
            tsl = slice(tg * 512, (tg + 1) * 512)
            for cc in range(8):
                sqb = sq8.v(cc)
                if cc % 3 == 2:
                    tt(sqb, cT.v(cc, tsl), cT.v(cc, tsl), ALU.mult, eng="pool")
                else:
                    act(sqb, cT.v(cc, tsl), AF.Square)

        def ln_stats_mm(tg):
            tsl = slice(tg * 512, (tg + 1) * 512)
            p1 = psbank(4)
            p2 = psbank(5)
            for cc in range(8):
                mm(p1.v(), ones_bf.v(), cT.v(cc, tsl), start=(cc == 0), stop=(cc == 7))
                mm(p2.v(), ones_bf.v(), sq8.v(cc), start=(cc == 0), stop=(cc == 7), signal=True)

        def ln_rs(tg):
            tsl = slice(tg * 512, (tg + 1) * 512)
            p1 = psbank(4)
            p2 = psbank(5)
            ts(mean.v(tsl), p1.v(), 1.0 / D, ALU.mult)
            tt(msq.v(), mean.v(tsl), mean.v(tsl), ALU.mult)
            stt(rsl.v(tsl), p2.v(), 1.0 / D, msq.v(), ALU.mult, ALU.subtract)
            act(rsl.v(tsl), rsl.v(tsl), AF.Sqrt, bias=EPS)
            recip(rsl.v(tsl), rsl.v(tsl))
            stt(nml.v(tsl), mean.v(tsl), -1.0, rsl.v(tsl), ALU.mult, ALU.mult)

        def ln_norm(tg, ccs=range(8)):
            tsl = slice(tg * 512, (tg + 1) * 512)
            for cc in ccs:
                tt(n1[cc % 2].v(), cT.v(cc, tsl), rsl.v(tsl), ALU.mult)
                tt(n2[cc % 2].v(), n1[cc % 2].v(), nml.v(tsl), ALU.add)
                act(cT.v(cc, tsl), n2[cc % 2].v(), AF.Silu, bias=cfv(C_LNB + cc), scale=cfv(C_LNG + cc))

        def pw_gate(tg, which, ocs=range(8)):
            tsl = slice(tg * 512, (tg + 1) * 512)
            for oc in ocs:
                o4 = oc % 4
                if which == "gate":
                    pg = psbank(0 + (oc % 2))
                    for k in range(8):
                        mm(pg.v(), wg[oc // 4].v(k, slice(o4 * 128, (o4 + 1) * 128)),
                           uT.v(k, slice(HALO + tg * 512, HALO + (tg + 1) * 512)), start=(k == 0), stop=(k == 7))
                    act(yconv.v(oc, tsl), pg.v(), AF.Silu)
                else:
                    py = psbank((2, 3, 6, 7)[oc % 4])
                    for k in range(8):
                        mm(py.v(), wp[oc // 4].v(k, slice(o4 * 128, (o4 + 1) * 128)), cT.v(k, tsl), start=(k == 0), stop=(k == 7))
                    tt(yconv.v(oc, tsl), py.v(), yconv.v(oc, tsl), ALU.mult)

        pw_gate(0, "gate")
        pw_gate(1, "gate")
        ln_squares(0)
        ln_stats_mm(0)
        ln_rs(0)
        ln_norm(0)
        ln_squares(1)
        for tg in range(NTG):
            if tg + 1 < NTG:
                ln_stats_mm(tg + 1)
                ln_rs(tg + 1)
            else:
                for half in range(2):
                    dload(wout.v(slice(None), slice(half * 512, (half + 1) * 512)),
                          w_out_r[:, :, half * 512:(half + 1) * 512], eng="pool")
            for i in range(8):
                if tg + 1 < NTG:
                    ln_norm(tg + 1, [i])
                pw_gate(tg, "pw", [i])
                if tg + 2 < NTG:
                    pw_gate(tg + 2, "gate", [i])
            if tg + 2 < NTG:
                ln_squares(tg + 2)
        dump("aT", cT.v(), (8, T))
        dump("yconv", yconv.v(), (8, T))
        al.reset(m_conv)

    if upto >= 6:
        xb = [al((4, D), F32) for _ in range(1)]
        hb = [al((4, D), F32) for _ in range(2)]
        junk = al((D,), BF16)
        ss = al((16,), F32)
        sd = al((16,), F32)
        rs = al((16,), F32)
        x_own_r = x_own.rearrange("(g j p) d -> g p j d", j=4, p=128)
        y_r = y_d.rearrange("(g j p) d -> g p j d", j=4, p=128)
        out_events = []
        for g in range(4):
            b = g % 2
            dload(xb[0].v(), x_own_r[g])
            for j in range(4):
                tt_ = g * 4 + j
                pout = pspair(2 * (tt_ % 4))
                for half in range(2):
                    for k in range(16):
                        src = yconv.v(k, slice(tt_ * 128, (tt_ + 1) * 128)) if k < 8 else \
                            yret.v(k - 8, slice(tt_ * 128, (tt_ + 1) * 128))
                        mm(pout.v(slice(half * 512, (half + 1) * 512)), src, wout.v(k, slice(half * 512, (half + 1) * 512)),
                           start=(k == 0), stop=(k == 15))
                tt(hb[b].v(j), pout.v(), gate_row.v(), ALU.mult)
                tt(hb[b].v(j), hb[b].v(j), xb[0].v(j), ALU.add)
                act(junk.v(), hb[b].v(j), AF.Square, accum=ss.v(slice(tt_, tt_ + 1)))
                if g == 3:
                    act(sd.v(slice(tt_, tt_ + 1)), ss.v(slice(tt_, tt_ + 1)), AF.Sqrt, bias=EPS, scale=1.0 / D)
                    recip(rs.v(slice(tt_, tt_ + 1)), sd.v(slice(tt_, tt_ + 1)))
                    stt(hb[b].v(j), hb[b].v(j), rs.v(slice(tt_, tt_ + 1)), fg_row.v(), ALU.mult, ALU.mult)
                    out_events.append(dstore(y_r[g][:, j, :], hb[b].v(j)))
            if g == 3:
                continue
            act(sd.v(slice(g * 4, g * 4 + 4)), ss.v(slice(g * 4, g * 4 + 4)), AF.Sqrt, bias=EPS, scale=1.0 / D)
            recip(rs.v(slice(g * 4, g * 4 + 4)), sd.v(slice(g * 4, g * 4 + 4)))
            for j in range(4):
                stt(hb[b].v(j), hb[b].v(j), rs.v(slice(g * 4 + j, g * 4 + j + 1)), fg_row.v(), ALU.mult, ALU.mult)
            out_events.append(dstore(y_r[g], hb[b].v()))
        for ev in out_events:
            kb.wait_event("sp", ev)

    for ev in dbg_out.values():
        kb.wait_event("sp", ev)
    for e in ("pe", "act", "dve", "pool"):
        s = kb.sem_of[e]
        if kb.count[s] > 0:
            kb._wait("sp", s, kb.count[s])
    for q in kb.dq["sp"] + kb.dq["pool"]:
        if kb.count[q] > 0:
            kb._wait("sp", q, kb.count[q])
    if kb.count[kb.cc_sem] > 0:
        kb._wait("sp", kb.cc_sem, kb.count[kb.cc_sem])

    with nc.Block() as block:
        @block.tensor
        def _(e):
            for f in kb.prog["pe"]:
                f(e)

        @block.scalar
        def _(e):
            for f in kb.prog["act"]:
                f(e)

        @block.vector
        def _(e):
            for f in kb.prog["dve"]:
                f(e)

        @block.gpsimd
        def _(e):
            for f in kb.prog["pool"]:
                f(e)

        @block.sync
        def _(e):
            for f in kb.prog["sp"]:
                f(e)
    stack.close()
    return nc, kb


def _col(v):
    return np.ascontiguousarray(np.asarray(v, np.float32).reshape(8, 128).T)


def _gammas():
    h = np.arange(4, dtype=np.float64)
    return np.log(1.0 - np.exp2(-5.0 - h))


def make_core_inputs(core, x, c, ada_w, ada_b, norm_g, w_in, conv_w, conv_b, conv_ln_g, conv_ln_b,
                     conv_pw, ret_gn_g, ret_gn_b, w_out, final_g):
    b, s = core // 4, core % 4
    x = np.asarray(x, np.float32)
    xo = np.ascontiguousarray(x[b, s * T:(s + 1) * T])
    if s == 0:
        xh = np.zeros((HALO, D), np.float32)
    else:
        xh = np.ascontiguousarray(x[b, s * T - HALO:s * T])
    cf = np.zeros((128, NCF), np.float32)
    cf[:, C_CCOL:C_CCOL + 8] = _col(np.asarray(c)[b])
    ab = np.asarray(ada_b, np.float32).reshape(-1)
    cf[:, C_ABSH:C_ABSH + 8] = _col(ab[0:D])
    cf[:, C_ABSC:C_ABSC + 8] = _col(ab[D:2 * D])
    cf[:, C_NG:C_NG + 8] = _col(np.asarray(norm_g).reshape(-1))
    cf[:, C_CB:C_CB + 8] = _col(np.asarray(conv_b).reshape(-1))
    cf[:, C_LNG:C_LNG + 8] = _col(np.asarray(conv_ln_g).reshape(-1))
    cf[:, C_LNB:C_LNB + 8] = _col(np.asarray(conv_ln_b).reshape(-1))
    cf[:, C_GNG:C_GNG + 8] = _col(np.asarray(ret_gn_g).reshape(-1))
    cf[:, C_GNB:C_GNB + 8] = _col(np.asarray(ret_gn_b).reshape(-1))
    cw = np.asarray(conv_w, np.float32).reshape(CW, 8, 128)
    cf[:, C_CWT:C_CWT + 8 * CW] = np.transpose(cw, (2, 1, 0)).reshape(128, 8 * CW)
    lg = _gammas()
    j = np.arange(128, dtype=np.float64)
    cf[:, C_VDEC:C_VDEC + 4] = np.exp(-lg[None, :] * (j[:, None] + 1.0))
    cf[:, C_EPSI:C_EPSI + 4] = EPS * np.exp(-2.0 * lg[None, :] * (j[:, None] + 1.0))
    cf[:, C_G128:C_G128 + 4] = np.exp(lg * 128.0)[None, :]
    coef = np.zeros((4, 4), np.float64)
    for r in range(4):
        if r < s:
            coef[r] = np.exp(lg * (float(T) * (s - 1 - r)))
    cf[:, C_COEF:C_COEF + 16] = coef.reshape(1, 16)
    cf[:, C_HM] = 0.0 if s == 0 else 1.0
    m = (j[:, None] <= j[None, :]).astype(np.float32)
    cf[:, C_MASK:C_MASK + 128] = m
    cf[:, C_MASK + 128:C_MASK + 256] = m
    cf[:, C_IDENT:C_IDENT + 128] = np.eye(128, dtype=np.float32)
    inv_freq = 1.0 / (10000.0 ** np.linspace(0.0, 1.0, 128, dtype=np.float64))
    pos = np.arange(s * T, (s + 1) * T, dtype=np.float64)
    theta = pos[None, :] * inv_freq[:, None]
    rot = np.stack([np.cos(theta), np.sin(theta)], axis=1).astype(np.float32)
    return {
        "x_own": xo, "x_halo": xh, "cf32": cf, "rot": np.ascontiguousarray(rot),
        "ada_w": np.ascontiguousarray(np.asarray(ada_w, np.float32).reshape(D, 3 * D)),
        "ada_b": np.ascontiguousarray(ab),
        "w_in": np.ascontiguousarray(np.asarray(w_in, np.float32).reshape(D, N_IN)),
        "conv_pw": np.ascontiguousarray(np.asarray(conv_pw, np.float32).reshape(D, D)),
        "w_out": np.ascontiguousarray(np.asarray(w_out, np.float32).reshape(2 * D, D)),
        "final_g": np.ascontiguousarray(np.asarray(final_g, np.float32).reshape(D)),
    }


_NC_CACHE = {}


def kernel(**inputs):
    if "nc" not in _NC_CACHE:
        _NC_CACHE["nc"] = build_nc()[0]
    nc = _NC_CACHE["nc"]
    in_maps = [make_core_inputs(core, **inputs) for core in range(8)]
    res = run_bass_kernel_spmd(nc, in_maps, core_ids=list(range(8)))
    out = np.zeros((2, 4 * T, D), np.float32)
    for core in range(8):
        b, s = core // 4, core % 4
        out[b, s * T:(s + 1) * T] = np.asarray(res.results[core]["y"], np.float32)
    return out
```

```python
import math
from contextlib import ExitStack

import numpy as np
import concourse.bass as bass
import concourse.mybir as mybir
from concourse.bass_utils import run_bass_kernel_spmd

F32 = mybir.dt.float32
BF16 = mybir.dt.bfloat16
AF = mybir.ActivationFunctionType
ALU = mybir.AluOpType

D = 1024
T = 2048
HALO = 128
TU = T + HALO
NTG = 4
NCH = 16
N_IN = 7168
EPS = 1e-6
CW = 31

_o = 0
def _take(n):
    global _o
    r = _o
    _o += n
    return r
C_CCOL = _take(8)
C_ABSH = _take(8)
C_ABSC = _take(8)
C_NG = _take(8)
C_CB = _take(8)
C_LNG = _take(8)
C_LNB = _take(8)
C_GNG = _take(8)
C_GNB = _take(8)
C_CWT = _take(8 * CW)
C_VDEC = _take(4)
C_EPSI = _take(4)
C_G128 = _take(4)
C_COEF = _take(16)
C_HM = _take(1)
C_MASK = _take(256)
C_IDENT = _take(128)
NCF = _o

SB_GRAN = 32
SB_SPACE = 256 * 1024
PS_BASE = SB_SPACE
PS_SPACE = 16 * 1024
DR_BASE = PS_BASE + PS_SPACE
DR_SPACE = 64 * SB_GRAN


class View:
    __slots__ = ("ap", "runs")

    def __init__(self, ap, runs):
        self.ap = ap
        self.runs = runs


class Buf:
    def __init__(self, ap, addr, shape, esize):
        self.ap = ap
        self.addr = addr
        self.shape = tuple(shape)
        self.esize = esize

    def v(self, *idx):
        idx = list(idx) + [slice(None)] * (len(self.shape) - len(idx))
        ap = self.ap[(slice(None),) + tuple(idx)]
        strides = []
        s = self.esize
        for d in reversed(self.shape):
            strides.append(s)
            s *= d
        strides = strides[::-1]
        runs = [(self.addr, self.addr)]
        sel = []
        for d, i in zip(self.shape, idx):
            if isinstance(i, slice):
                a, b, st = i.indices(d)
                assert st == 1
                sel.append((a, b))
            else:
                sel.append((i, i + 1))
        nd = len(self.shape)
        t = nd
        while t > 0 and sel[t - 1] == (0, self.shape[t - 1]):
            t -= 1
        if t == 0:
            return View(ap, [(self.addr, self.addr + s)])
        inner = strides[t - 1]
        a, b = sel[t - 1]
        base_runs = [(a * inner, b * inner)]
        for dd in range(t - 2, -1, -1):
            a, b = sel[dd]
            new = []
            for i in range(a, b):
                for (x, y) in base_runs:
                    new.append((x + i * strides[dd], y + i * strides[dd]))
            base_runs = new
        return View(ap, [(self.addr + x, self.addr + y) for (x, y) in base_runs])


class Tracker:
    def __init__(self, nsem):
        n = (DR_BASE + DR_SPACE) // SB_GRAN
        self.lw_sem = np.full(n, -1, np.int32)
        self.lw_val = np.zeros(n, np.int64)
        self.rd = np.zeros((nsem, n), np.int64)

    @staticmethod
    def _g(runs):
        out = []
        for (a, b) in runs:
            if a >= PS_BASE and a < DR_BASE:
                a = PS_BASE + (a - PS_BASE) // 2048 * 2048
                b = PS_BASE + ((b - PS_BASE) + 2047) // 2048 * 2048
            out.append((a // SB_GRAN, (b + SB_GRAN - 1) // SB_GRAN))
        return out

    def deps(self, reads, writes):
        raw, waw, war = {}, {}, {}
        for tgt, lst in ((raw, reads), (waw, writes)):
            for (a, b) in self._g(lst):
                s = self.lw_sem[a:b]
                v = self.lw_val[a:b]
                for sem in np.unique(s):
                    if sem >= 0:
                        m = int(v[s == sem].max())
                        if m > tgt.get(int(sem), 0):
                            tgt[int(sem)] = m
        for (a, b) in self._g(writes):
            m = self.rd[:, a:b].max(axis=1)
            for sem in np.nonzero(m)[0]:
                if int(m[sem]) > war.get(int(sem), 0):
                    war[int(sem)] = int(m[sem])
        return raw, waw, war

    def commit(self, reads, writes, sem, val):
        for (a, b) in self._g(reads):
            self.rd[sem, a:b] = val
        for (a, b) in self._g(writes):
            self.lw_sem[a:b] = sem
            self.lw_val[a:b] = val
            self.rd[:, a:b] = 0


ENGS = ("pe", "act", "dve", "pool", "sp")
NDQ = 8


class KB:
    def __init__(self, nc, stack):
        self.nc = nc
        self.prog = {e: [] for e in ENGS}
        self.sem_handles = []
        self.sem_of = {}

        def newsem(name):
            h = stack.enter_context(nc.semaphore(name))
            self.sem_handles.append(h)
            return len(self.sem_handles) - 1

        for e in ("pe", "act", "dve", "pool"):
            self.sem_of[e] = newsem("s_" + e)
        self.dq = {"sp": [newsem("dq_sp%d" % i) for i in range(NDQ)],
                   "pool": [newsem("dq_pl%d" % i) for i in range(NDQ)]}
        self.cc_sem = newsem("s_cc")
        self.count = {i: 0 for i in range(len(self.sem_handles))}
        self.dq_next = {"sp": 0, "pool": 0}
        self.waited = {e: {} for e in ENGS}
        self.trk = Tracker(len(self.sem_handles))
        self.ninstr = 0
        self.trace = {e: [] for e in ENGS}

    def _wait(self, eng, sem, val):
        if self.waited[eng].get(sem, 0) >= val:
            return
        self.waited[eng][sem] = val
        h = self.sem_handles[sem]
        self.prog[eng].append(lambda e, h=h, val=val: e.wait_ge(h, val))
        self.trace[eng].append(("wait", sem, val))

    def _sync(self, eng, reads, writes):
        raw, waw, war = self.trk.deps(reads, writes)
        own = self.sem_of.get(eng, None)
        need = {}
        for dct, is_raw in ((raw, True), (waw, False), (war, False)):
            for sem, val in dct.items():
                if sem == own:
                    if eng == "pe":
                        continue
                if val > need.get(sem, 0):
                    need[sem] = val
        for sem, val in need.items():
            self._wait(eng, sem, val)

    def op(self, eng, fn, reads=(), writes=(), signal=True):
        r = [x for vw in reads for x in vw.runs]
        w = [x for vw in writes for x in vw.runs]
        self._sync(eng, r, w)
        sem = self.sem_of[eng]
        val = self.count[sem] + 1
        self.trk.commit(r, w, sem, val)
        self.ninstr += 1
        if signal:
            self.count[sem] = val
            h = self.sem_handles[sem]
            self.prog[eng].append(lambda e, fn=fn, h=h: fn(e).then_inc(h, 1))
            self.trace[eng].append(("inc", sem, 1))
        else:
            self.prog[eng].append(lambda e, fn=fn: fn(e))

    def dma(self, eng, out_ap, in_ap, reads=(), writes=()):
        r = [x for vw in reads for x in vw.runs]
        w = [x for vw in writes for x in vw.runs]
        self._sync(eng, r, w)
        qi = self.dq_next[eng]
        self.dq_next[eng] = (qi + 1) % NDQ
        sem = self.dq[eng][qi]
        self._wait(eng, sem, self.count[sem])
        val = self.count[sem] + 16
        self.count[sem] = val
        self.trk.commit(r, w, sem, val)
        h = self.sem_handles[sem]
        self.prog[eng].append(lambda e, o=out_ap, i=in_ap, h=h: e.dma_start(out=o, in_=i).then_inc(h, 16))
        self.trace[eng].append(("inc", sem, 16))
        return sem, val

    def wait_event(self, eng, ev):
        self._wait(eng, ev[0], ev[1])


def simulate_sync(kb):
    pc = {e: 0 for e in ENGS}
    sem = {}
    progress = True
    while progress:
        progress = False
        for e in ENGS:
            tr_ = kb.trace[e]
            while pc[e] < len(tr_):
                kind, s_, v = tr_[pc[e]]
                if kind == "wait":
                    if sem.get(s_, 0) >= v:
                        pc[e] += 1
                        progress = True
                    else:
                        break
                else:
                    sem[s_] = sem.get(s_, 0) + v
                    pc[e] += 1
                    progress = True
    stuck = {e: (pc[e], len(kb.trace[e]), kb.trace[e][pc[e]] if pc[e] < len(kb.trace[e]) else None) for e in ENGS}
    if all(pc[e] == len(kb.trace[e]) for e in ENGS):
        return None
    return stuck, sem


def dram_view(base_idx, ap):
    a = DR_BASE + base_idx * SB_GRAN
    return View(ap, [(a, a + SB_GRAN)])


def build_nc(dbg=(), upto=99):
    nc = bass.Bass("TRN2", target_bir_lowering=False)
    stack = ExitStack()
    kb = KB(nc, stack)

    x_own = nc.dram_tensor("x_own", [T, D], F32, kind="ExternalInput").ap()
    x_halo = nc.dram_tensor("x_halo", [HALO, D], F32, kind="ExternalInput").ap()
    cf32_d = nc.dram_tensor("cf32", [128, NCF], F32, kind="ExternalInput").ap()
    rot_d = nc.dram_tensor("rot", [128, 2, T], F32, kind="ExternalInput").ap()
    ada_w_d = nc.dram_tensor("ada_w", [D, 3 * D], F32, kind="ExternalInput").ap()
    ada_b_d = nc.dram_tensor("ada_b", [3 * D], F32, kind="ExternalInput").ap()
    w_in_d = nc.dram_tensor("w_in", [D, N_IN], F32, kind="ExternalInput").ap()
    conv_pw_d = nc.dram_tensor("conv_pw", [D, D], F32, kind="ExternalInput").ap()
    w_out_d = nc.dram_tensor("w_out", [2 * D, D], F32, kind="ExternalInput").ap()
    final_g_d = nc.dram_tensor("final_g", [D], F32, kind="ExternalInput").ap()
    y_d = nc.dram_tensor("y", [T, D], F32, kind="ExternalOutput").ap()
    st_in = [nc.dram_tensor("st_in%d" % i, [512, 256], F32) for i in range(2)]
    st_all = [nc.dram_tensor("st_all%d" % i, [4 * 512, 256], F32) for i in range(2)]
    dbg_out = {}

    ada_w_r = ada_w_d.rearrange("(k p) n -> p k n", p=128)
    w_in_r = w_in_d.rearrange("(k p) n -> p k n", p=128)
    conv_pw_r = conv_pw_d.rearrange("(k p) n -> p k n", p=128)
    w_out_r = w_out_d.rearrange("(k p) n -> p k n", p=128)

    ARENA = 212736
    arena = nc.alloc_sbuf_tensor("arena", [128, ARENA // 2], BF16)
    arena_addr = 0

    class Alloc:
        def __init__(self):
            self.top = 0

        def mark(self):
            return self.top

        def reset(self, m):
            self.top = m

        def at(self, off, shape, dt):
            save = self.top
            self.top = off
            b = self(shape, dt)
            self.top = save
            return b

        def __call__(self, shape, dt):
            es = 4 if dt == F32 else 2
            n = int(np.prod(shape))
            nb = (n * es + 31) // 32 * 32
            off = self.top
            self.top += nb
            if self.top > ARENA:
                raise AssertionError("SBUF arena overflow %d (%s %s)" % (self.top, shape, dt))
            ap = arena[:, off // 2: off // 2 + n * es // 2]
            if dt == F32:
                ap = ap.bitcast(F32)
            if len(shape) == 2:
                ap = ap.rearrange("p (a b) -> p a b", b=shape[1])
            elif len(shape) == 3:
                ap = ap.rearrange("p (a b c) -> p a b c", b=shape[1], c=shape[2])
            return Buf(ap, off, shape, es)

    al = Alloc()

    ps_all = nc.alloc_psum_tensor("ps_all", [128, 4096], F32)

    def psbank(b, dt=F32, shape=None):
        ap = ps_all[:, b * 512:(b + 1) * 512]
        if dt == BF16:
            ap = ap.bitcast(BF16)
            shape = shape or (1024,)
            es = 2
        else:
            shape = shape or (512,)
            es = 4
        if len(shape) == 2:
            ap = ap.rearrange("p (a b) -> p a b", b=shape[1])
        return Buf(ap, PS_BASE + b * 2048, shape, es)

    def pspair(b):
        return Buf(ps_all[:, b * 512:(b + 2) * 512], PS_BASE + b * 2048, (1024,), 4)

    def mm(out, lhsT, rhs, start, stop, signal=None):
        kb.op("pe", lambda e: e.matmul(out.ap, lhsT.ap, rhs.ap, start=start, stop=stop),
              reads=[lhsT, rhs], writes=[out], signal=(stop if signal is None else signal))

    def tr(out, in_, ident):
        kb.op("pe", lambda e: e.transpose(out.ap, in_.ap, ident.ap), reads=[in_, ident], writes=[out])

    def act(out, in_, func, bias=None, scale=None, accum=None, eng="act"):
        reads = [in_]
        kw = {}
        if bias is not None:
            if isinstance(bias, View):
                reads.append(bias)
                kw["bias"] = bias.ap
            else:
                kw["bias"] = float(bias)
        if scale is not None:
            if isinstance(scale, View):
                reads.append(scale)
                kw["scale"] = scale.ap
            else:
                kw["scale"] = float(scale)
        writes = [out]
        if accum is not None:
            writes.append(accum)
            kw["accum_out"] = accum.ap
        kb.op("act", lambda e: e.activation(out.ap, in_.ap, func, **kw), reads=reads, writes=writes)

    def tt(out, a, b, op, eng="dve"):
        kb.op(eng, lambda e: e.tensor_tensor(out.ap, a.ap, b.ap, op), reads=[a, b], writes=[out])

    def ts(out, a, s1, op0, s2=None, op1=None, eng="dve"):
        reads = [a]
        v1 = s1.ap if isinstance(s1, View) else float(s1)
        if isinstance(s1, View):
            reads.append(s1)
        v2 = None
        if s2 is not None:
            v2 = s2.ap if isinstance(s2, View) else float(s2)
            if isinstance(s2, View):
                reads.append(s2)
        if op1 is None:
            kb.op(eng, lambda e: e.tensor_scalar(out.ap, a.ap, v1, None, op0), reads=reads, writes=[out])
        else:
            kb.op(eng, lambda e: e.tensor_scalar(out.ap, a.ap, v1, v2, op0, op1), reads=reads, writes=[out])

    def stt(out, a, s, b, op0, op1):
        reads = [a, b]
        sv = s.ap if isinstance(s, View) else float(s)
        if isinstance(s, View):
            reads.append(s)
        kb.op("dve", lambda e: e.scalar_tensor_tensor(out.ap, a.ap, sv, b.ap, op0, op1), reads=reads, writes=[out])

    def cp(out, a, eng="dve"):
        kb.op(eng, lambda e: e.tensor_copy(out.ap, a.ap), reads=[a], writes=[out])

    def recip(out, a):
        kb.op("dve", lambda e: e.reciprocal(out.ap, a.ap), reads=[a], writes=[out])

    dram_ctr = [0]

    def dload(out, src_ap, eng="sp"):
        return kb.dma(eng, out.ap, src_ap, reads=[], writes=[out])

    def dstore(dst_ap, src, eng="sp", didx=None):
        w = [dram_view(didx, dst_ap)] if didx is not None else []
        return kb.dma(eng, dst_ap, src.ap, reads=[src], writes=w)

    def dump(name, buf_view, shape):
        if name in dbg:
            t = nc.dram_tensor("dbg_" + name, [128] + list(shape), buf_view.ap.dtype, kind="ExternalOutput").ap()
            dbg_out[name] = kb.dma("sp", t, buf_view.ap, reads=[buf_view], writes=[])

    cf = al((NCF,), F32)
    ident_bf = al((128,), BF16)
    ones_bf = al((128,), BF16)
    gate_row = al((D,), F32)
    fg_row = al((D,), F32)
    gs = al((8,), F32)
    shc = al((8,), F32)
    cwh = al((8, CW), F32)
    uT = al((8, TU), BF16)
    yret = al((8, T), BF16)
    m_persist = al.mark()

    def cfv(off, n=1):
        return cf.v(slice(off, off + n))

    dload(cf.v(), cf32_d)
    dload(fg_row.v(), final_g_d.partition_broadcast(128))
    adab_row = al((D,), F32)
    dload(adab_row.v(), ada_b_d[2 * D:3 * D].partition_broadcast(128))
    adaw = al((8, 3 * D), BF16)
    for j3 in range(2):
        dload(adaw.v(slice(None), slice(j3 * D, (j3 + 1) * D)), ada_w_r[:, :, j3 * D:(j3 + 1) * D], eng="pool")
    cp(ident_bf.v(), cfv(C_IDENT, 128))
    ts(ones_bf.v(), cfv(C_IDENT, 128), 0.0, ALU.mult, 1.0, ALU.add)
    ts(cwh.v(), Buf(cf.ap[:, C_CWT:C_CWT + 8 * CW].rearrange("p (a b) -> p a b", b=CW), cf.addr + 4 * C_CWT, (8, CW), 4).v(),
       0.5, ALU.mult)
    c_act = al((8,), F32)
    c_bf = al((8,), BF16)
    c_rep = al((8, 128), BF16)
    act(c_act.v(), cfv(C_CCOL, 8), AF.Silu)
    cp(c_bf.v(), c_act.v())
    for k in range(8):
        ts(c_rep.v(k), cfv(C_IDENT, 128), 0.0, ALU.mult, c_act.v(slice(k, k + 1)), ALU.add)
    def adaln_part():
        ps_mod = psbank(0)
        for jc in range(16):
            for k in range(8):
                mm(ps_mod.v(slice(jc, jc + 1)), adaw.v(k, slice(jc * 128, (jc + 1) * 128)), c_bf.v(slice(k, k + 1)),
                   start=(k == 0), stop=(k == 7))
        tt(shc.v(), ps_mod.v(slice(0, 8)), cfv(C_ABSH, 8), ALU.add)
        sc1 = al((8,), F32)
        tt(sc1.v(), ps_mod.v(slice(8, 16)), cfv(C_ABSC, 8), ALU.add)
        stt(gs.v(), sc1.v(), 1.0, cfv(C_NG, 8), ALU.add, ALU.mult)
        dump("gs", gs.v(), (8,))
        dump("shc", shc.v(), (8,))


    def gate_part():
        ps_g = pspair(2)
        for half in range(2):
            for k in range(8):
                mm(ps_g.v(slice(half * 512, (half + 1) * 512)), c_rep.v(k),
                   adaw.v(k, slice(2 * D + half * 512, 2 * D + (half + 1) * 512)), start=(k == 0), stop=(k == 7))
        tt(gate_row.v(), ps_g.v(), adab_row.v(), ALU.add)
        dump("gate_row", gate_row.v(), (D,))

    if upto >= 1:
        xb = [al((4, D), F32) for _ in range(2)]
        xs = al((20, D), BF16)
        junk = al((D,), BF16)
        ss = al((20,), F32)
        sd = al((20,), F32)
        rs = al((20,), F32)
        x_own_r = x_own.rearrange("(g j p) d -> g p j d", j=4, p=128)
        for g in range(5):
            nt = 1 if g == 0 else 4
            b = g % 2
            if g == 0:
                dload(xb[b].v(0), x_halo)
            else:
                dload(xb[b].v(), x_own_r[g - 1])
            for j in range(nt):
                act(junk.v(), xb[b].v(j), AF.Square, accum=ss.v(slice(g * 4 + j, g * 4 + j + 1)))
            act(sd.v(slice(g * 4, g * 4 + nt)), ss.v(slice(g * 4, g * 4 + nt)), AF.Sqrt, bias=EPS, scale=1.0 / D)
            recip(rs.v(slice(g * 4, g * 4 + nt)), sd.v(slice(g * 4, g * 4 + nt)))
            for j in range(nt):
                ts(xs.v(g * 4 + j), xb[b].v(j), rs.v(slice(g * 4 + j, g * 4 + j + 1)), ALU.mult)
        adaln_part()
        dload(adaw.v(slice(None), slice(2 * D, 3 * D)), ada_w_r[:, :, 2 * D:3 * D], eng="pool")
        for g in range(5):
            nt = 1 if g == 0 else 4
            ubase = 0 if g == 0 else HALO + (g - 1) * 512
            for k in range(8):
                pt = psbank(4 + (k % 4), BF16)
                for j in range(nt):
                    tr(pt.v(slice(j * 128, (j + 1) * 128)), xs.v(g * 4 + j, slice(k * 128, (k + 1) * 128)), ident_bf.v())
                if k % 2 == 0:
                    act(uT.v(k, slice(ubase, ubase + nt * 128)), pt.v(slice(0, nt * 128)), AF.Identity,
                        bias=shc.v(slice(k, k + 1)), scale=gs.v(slice(k, k + 1)))
                else:
                    ts(uT.v(k, slice(ubase, ubase + nt * 128)), pt.v(slice(0, nt * 128)),
                       gs.v(slice(k, k + 1)), ALU.mult, shc.v(slice(k, k + 1)), ALU.add)
        dump("uT", uT.v(), (8, TU))
        gate_part()
        al.reset(m_persist)

    m_ret = al.mark()
    wbuf_n = [0]

    def load_w(src_r, col0, ncols=512, nk=8, bufs=None):
        b = bufs[wbuf_n[0] % len(bufs)]
        wbuf_n[0] += 1
        dload(b.v(slice(0, nk), slice(0, ncols)), src_r[:, :, col0:col0 + ncols], eng="pool")
        return b

    def rotary_proj(wb, hp, dst, scale, rotb, tmps, psb):
        x1s, x2s, t1, t2, t3, t4 = tmps
        for hh in range(2):
            for tg in range(NTG):
                par = (hh * NTG + tg) % 2
                pa = psbank(psb[0] + 2 * par)
                pb = psbank(psb[1] + 2 * par)
                for e, pp in ((0, pa), (1, pb)):
                    for k in range(8):
                        mm(pp.v(), wb.v(k, slice(hh * 256 + e * 128, hh * 256 + (e + 1) * 128)),
                           uT.v(k, slice(HALO + tg * 512, HALO + (tg + 1) * 512)), start=(k == 0), stop=(k == 7))
                act(x1s.v(), pa.v(), AF.Copy, scale=scale)
                act(x2s.v(), pb.v(), AF.Copy, scale=scale)
                cs = rotb.v(0, slice(tg * 512, (tg + 1) * 512))
                sn = rotb.v(1, slice(tg * 512, (tg + 1) * 512))
                tt(t1.v(), x1s.v(), cs, ALU.mult)
                tt(t2.v(), x2s.v(), sn, ALU.mult)
                tt(dst.v(hh, 0, slice(tg * 512, (tg + 1) * 512)), t1.v(), t2.v(), ALU.subtract)
                tt(t3.v(), x1s.v(), sn, ALU.mult)
                tt(t4.v(), x2s.v(), cs, ALU.mult)
                tt(dst.v(hh, 1, slice(tg * 512, (tg + 1) * 512)), t3.v(), t4.v(), ALU.add)

    if upto >= 2:
        m_rot = al.mark()
        for hp in range(2 if upto >= 3 else 1):
            al.reset(m_rot)
            kT = al((2, 2, T), BF16)
            vp = al((NCH, 512), BF16)
            qT = al((2, 2, T), BF16)
            Z = al((2, 512), F32)
            Sbb = [al((2, 512), BF16) for _ in range(2)]
            kTM = [al((512,), BF16) for _ in range(3)]
            PTb = [al((2, 128), BF16) for _ in range(2)]
            gsum = al((2, NCH), F32)
            gsq = al((2, NCH), F32)
            gmean = al((2, NCH), F32)
            gmsq = al((2, NCH), F32)
            ve = al((2, NCH), F32)
            ve2 = al((2, NCH), F32)
            sdv = al((2, NCH), F32)
            rstd = al((2, NCH), F32)
            nmr = al((2, NCH), F32)
            m_hp2 = al.mark()
            rotb = al((2, T), F32)
            dload(rotb.v(), rot_d)
            tmps = [al((512,), F32) for _ in range(6)]
            slot = [al.at(tmps[0].addr, (2, 512), F32), al.at(tmps[2].addr, (2, 512), F32)]
            Lb = slot[1]
            pad_ = al((1024,), F32)
            wbufs = [al((8, 512), BF16) for _ in range(3)]
            m_p5 = al.mark()
            if hp == 0:
                wnext = [load_w(w_in_r, 4096 + hp * 512, bufs=wbufs), load_w(w_in_r, 5120 + hp * 512, bufs=wbufs),
                         load_w(w_in_r, 3072 + hp * 512, bufs=wbufs)]
            wk, wv, wq = wnext

            rotary_proj(wk, hp, kT, 1.0 / 16.0, rotb, tmps, (0, 1))
            wg = load_w(w_in_r, 6144 + hp * 512, bufs=wbufs)

            def k_tm(n):
                pkt = psbank(4 if n % 2 == 0 else 7, BF16)
                for hh in range(2):
                    for e in range(2):
                        tr(pkt.v(slice(hh * 256 + e * 128, hh * 256 + (e + 1) * 128)),
                           kT.v(hh, e, slice(n * 128, (n + 1) * 128)), ident_bf.v())
                act(kTM[n % 3].v(), pkt.v(slice(0, 512)), AF.Copy)

            def d_s(n):
                pds = [psbank(5), psbank(6)]
                km = kTM[n % 3]
                for hh in range(2):
                    for e in range(2):
                        mm(pds[hh].v(slice(e * 256, (e + 1) * 256)),
                           km.v(slice(hh * 256 + e * 128, hh * 256 + (e + 1) * 128)),
                           vp.v(n, slice(hh * 256, (hh + 1) * 256)), start=True, stop=True)
                return pds

            def v_proj(n):
                pv = psbank(2 + (n % 2))
                for k in range(8):
                    mm(pv.v(), uT.v(k, slice(HALO + n * 128, HALO + (n + 1) * 128)), wv.v(k), start=(k == 0), stop=(k == 7))
                for hh in range(2):
                    act(vp.v(n, slice(hh * 256, (hh + 1) * 256)), pv.v(slice(hh * 256, (hh + 1) * 256)), AF.Copy,
                        scale=cfv(C_VDEC + hp * 2 + hh))

            v_proj(0)
            v_proj(1)
            k_tm(0)
            for n in range(NCH):
                if n + 1 < NCH:
                    k_tm(n + 1)
                if n + 2 < NCH:
                    v_proj(n + 2)
                pds = d_s(n)
                for hh in range(2):
                    if n == 0:
                        cp(Z.v(hh), pds[hh].v())
                    else:
                        stt(Z.v(hh), Z.v(hh), cfv(C_G128 + hp * 2 + hh), pds[hh].v(), ALU.mult, ALU.add)
            dump("kT%d" % hp, kT.v(), (2, 2, T))
            dump("vp%d" % hp, vp.v(), (NCH, 512))
            for hh in range(2):
                ts(Lb.v(hh), Z.v(hh), cfv(C_G128 + hp * 2 + hh), ALU.mult)
            dump("Lb%d" % hp, Lb.v(), (2, 512))
            ev = dstore(st_in[hp].ap().rearrange("(h e p) d -> p h e d", h=2, e=2),
                        Buf(Lb.ap.rearrange("p h (e d) -> p h e d", e=2), Lb.addr, (2, 2, 256), 4).v(), didx=hp * 4)
            kb._sync("pool", [], [])
            kb.wait_event("pool", ev)
            cch = kb.sem_handles[kb.cc_sem]
            kb.count[kb.cc_sem] += 1
            ccv = kb.count[kb.cc_sem]
            si, so = st_in[hp], st_all[hp]
            kb.trace["pool"].append(("inc", kb.cc_sem, 1))
            kb.prog["pool"].append(lambda e, si=si, so=so, cch=cch: e.collective_compute(
                "AllGather", ALU.bypass, replica_groups=[[0, 1, 2, 3], [4, 5, 6, 7]],
                ins=[si.ap().opt()], outs=[so.ap().opt()]).then_inc(cch, 1))
            cc_event = (kb.cc_sem, ccv)

            rotary_proj(wq, hp, qT, 1.0, rotb, tmps, (0, 1))
            for fc in range(4):
                for tg in range(NTG):
                    pg = psbank(2 + (tg % 2))
                    for k in range(8):
                        mm(pg.v(), wg.v(k, slice(fc * 128, (fc + 1) * 128)),
                           uT.v(k, slice(HALO + tg * 512, HALO + (tg + 1) * 512)), start=(k == 0), stop=(k == 7))
                    act(yret.v(hp * 4 + fc, slice(tg * 512, (tg + 1) * 512)), pg.v(), AF.Silu)
            dump("qT%d" % hp, qT.v(), (2, 2, T))
            dump("sg%d" % hp, yret.v(slice(hp * 4, hp * 4 + 4)), (4, T))

            for r in range(4):
                sl = slot[r % 2]
                kb.wait_event("sp", cc_event)
                dload(Buf(sl.ap.rearrange("p h (e d) -> p h e d", e=2), sl.addr, (2, 2, 256), 4).v(),
                      st_all[hp].ap()[r * 512:(r + 1) * 512, :].rearrange("(h e p) d -> p h e d", h=2, e=2))
                for hh in range(2):
                    cfc = cfv(C_COEF + r * 4 + hp * 2 + hh)
                    if r == 0:
                        ts(Z.v(hh), sl.v(hh), cfc, ALU.mult)
                    else:
                        stt(Z.v(hh), sl.v(hh), cfc, Z.v(hh), ALU.mult, ALU.add)
            dump("Sinit%d" % hp, Z.v(), (2, 512))

            if upto >= 4:
                if hp == 0:
                    wnext = [load_w(w_in_r, 4096 + 512, bufs=wbufs), load_w(w_in_r, 5120 + 512, bufs=wbufs),
                             load_w(w_in_r, 3072 + 512, bufs=wbufs)]
                al.reset(m_hp2)
                obuf = al((NCH, 512), F32)
                assert al.top <= m_p5 - 3 * 8192
                al.reset(m_p5)
                rn4 = [al((4, 512), BF16) for _ in range(1)]
                rtt4 = al((4, 512), BF16)
                gjunk = al((256,), BF16)
                for hh in range(2):
                    cp(Sbb[0].v(hh), Z.v(hh))

                def scores(n):
                    psc = psbank(0 if n % 2 == 0 else 3)
                    for hh in range(2):
                        for e in range(2):
                            mm(psc.v(slice(hh * 128, (hh + 1) * 128)), kT.v(hh, e, slice(n * 128, (n + 1) * 128)),
                               qT.v(hh, e, slice(n * 128, (n + 1) * 128)), start=(e == 0), stop=(e == 1))
                    PT = PTb[n % 2]
                    tt(Buf(PT.ap.rearrange("p a b -> p (a b)"), PT.addr, (256,), 2).v(), psc.v(slice(0, 256)),
                       cfv(C_MASK, 256), ALU.mult)

                scores(0)
                k_tm(0)
                for n in range(NCH):
                    if n < NCH - 1:
                        pds = d_s(n)
                        Sbn = Sbb[(n + 1) % 2]
                        for hh in range(2):
                            g = cfv(C_G128 + hp * 2 + hh)
                            if n == 0:
                                tt(Z.v(hh), Z.v(hh), pds[hh].v(), ALU.add)
                            else:
                                stt(Z.v(hh), Z.v(hh), g, pds[hh].v(), ALU.mult, ALU.add)
                            ts(Sbn.v(hh), Z.v(hh), g, ALU.mult)
                    if n + 1 < NCH:
                        scores(n + 1)
                        if n + 1 < NCH - 1:
                            k_tm(n + 1)
                    PT = PTb[n % 2]
                    Sb = Sbb[n % 2]
                    po = psbank(1 + (n % 2))
                    for hh in range(2):
                        mm(po.v(slice(hh * 256, (hh + 1) * 256)), PT.v(hh), vp.v(n, slice(hh * 256, (hh + 1) * 256)),
                           start=True, stop=False)
                        for e in range(2):
                            mm(po.v(slice(hh * 256, (hh + 1) * 256)), qT.v(hh, e, slice(n * 128, (n + 1) * 128)),
                               Sb.v(hh, slice(e * 256, (e + 1) * 256)), start=False, stop=(e == 1))
                    for hh in range(2):
                        pv_ = po.v(slice(hh * 256, (hh + 1) * 256))
                        act(obuf.v(n, slice(hh * 256, (hh + 1) * 256)), pv_, AF.Identity, accum=gsum.v(hh, slice(n, n + 1)))
                        act(gjunk.v(), pv_, AF.Square, accum=gsq.v(hh, slice(n, n + 1)))
                ts(gmean.v(), gsum.v(), 1.0 / 256.0, ALU.mult)
                tt(gmsq.v(), gmean.v(), gmean.v(), ALU.mult)
                stt(ve.v(), gsq.v(), 1.0 / 256.0, gmsq.v(), ALU.mult, ALU.subtract)
                for hh in range(2):
                    ts(ve2.v(hh), ve.v(hh), cfv(C_EPSI + hp * 2 + hh), ALU.add)
                act(sdv.v(), ve2.v(), AF.Sqrt)
                recip(rstd.v(), sdv.v())
                stt(nmr.v(), gmean.v(), -1.0, rstd.v(), ALU.mult, ALU.mult)
                for g4 in range(4):
                    rb = rn4[0]
                    for c in range(4):
                        n = g4 * 4 + c
                        for hh in range(2):
                            ts(rb.v(c, slice(hh * 256, (hh + 1) * 256)), obuf.v(n, slice(hh * 256, (hh + 1) * 256)),
                               rstd.v(hh, slice(n, n + 1)), ALU.mult, nmr.v(hh, slice(n, n + 1)), ALU.add)
                    b0 = 2 * (g4 % 2)
                    prt = Buf(ps_all[:, b0 * 512:(b0 + 2) * 512].bitcast(BF16).rearrange("p (a b) -> p a b", b=512),
                              PS_BASE + b0 * 2048, (4, 512), 2)
                    for fb in range(4):
                        for c in range(4):
                            tr(prt.v(fb, slice(c * 128, (c + 1) * 128)), rb.v(c, slice(fb * 128, (fb + 1) * 128)), ident_bf.v())
                    for fb in range(4):
                        act(rtt4.v(fb), prt.v(fb), AF.Identity,
                            bias=cfv(C_GNB + hp * 4 + fb), scale=cfv(C_GNG + hp * 4 + fb))
                    yv = yret.v(slice(hp * 4, hp * 4 + 4), slice(g4 * 512, (g4 + 1) * 512))
                    tt(yv, rtt4.v(), yv, ALU.mult)
                dump("yret%d" % hp, yret.v(slice(hp * 4, hp * 4 + 4)), (4, T))
        al.reset(m_ret)

    if upto >= 5:
        yconv = al((8, T), BF16)
        m_conv = al.mark()
        cT = al((8, T), BF16)
        m_cv2 = al.mark()
        wbufs = [al((8, 512), BF16) for _ in range(3)]
        a0 = [al((TU,), BF16) for _ in range(2)]
        Dg = [al((CW, 128), BF16) for _ in range(2)]
        th = [al((512,), F32) for _ in range(3)]
        cacc = [al((512,), F32) for _ in range(2)]
        pasb = [al((512,), F32) for _ in range(3)]
        NPE = 19
        wsel = {}

        def conv_inproj(cc, tgis=range(5)):
            if cc == 0 and 0 in tgis:
                wsel[("a", 0)] = load_w(w_in_r, 0, bufs=wbufs)
                wsel[("b", 0)] = load_w(w_in_r, 1024, bufs=wbufs)
                wsel[("a", 1)] = load_w(w_in_r, 512, bufs=wbufs)
            if cc == 4 and 0 in tgis:
                wsel[("b", 1)] = load_w(w_in_r, 1024 + 512, bufs=wbufs)
            wa, wb_ = wsel[("a", cc // 4)], wsel[("b", cc // 4)]
            c4 = cc % 4
            a0c = a0[cc % 2]
            dg = Dg[cc % 2]
            if 0 in tgis:
                kb.op("pool", lambda e, o=dg.v(slice(0, NPE)), c=cwh.v(cc, slice(0, NPE)): e.tensor_tensor(
                    o.ap, ident_bf.v().ap.unsqueeze(1).broadcast_to([128, NPE, 128]),
                    c.ap.unsqueeze(2).broadcast_to([128, NPE, 128]), ALU.mult),
                    reads=[ident_bf.v(), cwh.v(cc)], writes=[dg.v(slice(0, NPE))])
            for tgi in tgis:
                if tgi == 0:
                    u0, n_ = 0, HALO
                else:
                    u0, n_ = HALO + (tgi - 1) * 512, 512
                pa = psbank(0 + 2 * (tgi % 2))
                pb = psbank(1 + 2 * (tgi % 2))
                for wsrc, pp in ((wa, pa), (wb_, pb)):
                    for k in range(8):
                        mm(pp.v(slice(0, n_)), wsrc.v(k, slice(c4 * 128, (c4 + 1) * 128)), uT.v(k, slice(u0, u0 + n_)),
                           start=(k == 0), stop=(k == 7))
                thb = th[tgi % 3]
                pab = pasb[tgi % 3]
                act(thb.v(slice(0, n_)), pb.v(slice(0, n_)), AF.Tanh, scale=0.5)
                act(pab.v(slice(0, n_)), pa.v(slice(0, n_)), AF.Copy)
                stt(a0c.v(slice(u0, u0 + n_)), thb.v(slice(0, n_)), 1.0, pab.v(slice(0, n_)), ALU.add, ALU.mult)
                if tgi == 0:
                    ts(a0c.v(slice(0, HALO)), a0c.v(slice(0, HALO)), cfv(C_HM), ALU.mult)

        def conv_taps(cc, pairs=range(2)):
            a0c = a0[cc % 2]
            dg = Dg[cc % 2]
            for tp_ in pairs:
                tgs = (2 * tp_, 2 * tp_ + 1)
                pcs = {}
                for tg in tgs:
                    pc = psbank(4 + tg)
                    pcs[tg] = pc
                    for j in range(NPE):
                        mm(pc.v(), dg.v(j), a0c.v(slice(98 + tg * 512 + j, 98 + tg * 512 + j + 512)),
                           start=(j == 0), stop=(j == NPE - 1))
                for j in range(NPE, CW):
                    for tg in tgs:
                        acc = cacc[tg % 2]
                        src = pcs[tg].v() if j == NPE else acc.v()
                        stt(acc.v(), a0c.v(slice(98 + tg * 512 + j, 98 + tg * 512 + j + 512)), cwh.v(cc, slice(j, j + 1)),
                            src, ALU.mult, ALU.add)
                for tg in tgs:
                    ts(cT.v(cc, slice(tg * 512, (tg + 1) * 512)), cacc[tg % 2].v(), cfv(C_CB + cc), ALU.add)

        conv_inproj(0)
        for cc in range(8):
            if cc + 1 < 8:
                conv_inproj(cc + 1, [0, 1])
            conv_taps(cc, [0])
            if cc + 1 < 8:
                conv_inproj(cc + 1, [2, 3])
            conv_taps(cc, [1])
            if cc + 1 < 8:
                conv_inproj(cc + 1, [4])
        dump("cT", cT.v(), (8, T))
        al.reset(m_cv2)
        wbufs = [al((8, 512), BF16) for _ in range(4)]
        wg = [load_w(w_in_r, 2048 + i * 512, bufs=wbufs) for i in range(2)]
        wp = [load_w(conv_pw_r, i * 512, bufs=wbufs) for i in range(2)]
        sq8 = al((8, 512), BF16)
        mean = al((T,), F32)
        rsl = al((T,), F32)
        nml = mean
        msq = al((512,), F32)
        n1 = [al((512,), F32) for _ in range(1)] * 2
        n2 = [al((512,), F32) for _ in range(2)]
        assert al.top - sq8.addr == 32768
        wout = al.at(sq8.addr, (16, D), BF16)

        def ln_squares(tg):
            tsl = slice(tg * 512, (tg + 1) * 512)
            for cc in range(8):
                sqb = sq8.v(cc)
                if cc % 3 == 2:
                    tt(sqb, cT.v(cc, tsl), cT.v(cc, tsl), ALU.mult, eng="pool")
                else:
                    act(sqb, cT.v(cc, tsl), AF.Square)

        def ln_stats_mm(tg):
            tsl = slice(tg * 512, (tg + 1) * 512)
            p1 = psbank(4)
            p2 = psbank(5)
            for cc in range(8):
                mm(p1.v(), ones_bf.v(), cT.v(cc, tsl), start=(cc == 0), stop=(cc == 7))
                mm(p2.v(), ones_bf.v(), sq8.v(cc), start=(cc == 0), stop=(cc == 7), signal=True)

        def ln_rs(tg):
            tsl = slice(tg * 512, (tg + 1) * 512)
            p1 = psbank(4)
            p2 = psbank(5)
            ts(mean.v(tsl), p1.v(), 1.0 / D, ALU.mult)
            tt(msq.v(), mean.v(tsl), mean.v(tsl), ALU.mult)
            stt(rsl.v(tsl), p2.v(), 1.0 / D, msq.v(), ALU.mult, ALU.subtract)
            act(rsl.v(tsl), rsl.v(tsl), AF.Sqrt, bias=EPS)
            recip(rsl.v(tsl), rsl.v(tsl))
            stt(nml.v(tsl), mean.v(tsl), -1.0, rsl.v(tsl), ALU.mult, ALU.mult)

        def ln_norm(tg, ccs=range(8)):
            tsl = slice(tg * 512, (tg + 1) * 512)
            for cc in ccs:
                tt(n1[cc % 2].v(), cT.v(cc, tsl), rsl.v(tsl), ALU.mult)
                tt(n2[cc % 2].v(), n1[cc % 2].v(), nml.v(tsl), ALU.add)
                act(cT.v(cc, tsl), n2[cc % 2].v(), AF.Silu, bias=cfv(C_LNB + cc), scale=cfv(C_LNG + cc))

        def pw_gate(tg, which, ocs=range(8)):
            tsl = slice(tg * 512, (tg + 1) * 512)
            for oc in ocs:
                o4 = oc % 4
                if which == "gate":
                    pg = psbank(0 + (oc % 2))
                    for k in range(8):
                        mm(pg.v(), wg[oc // 4].v(k, slice(o4 * 128, (o4 + 1) * 128)),
                           uT.v(k, slice(HALO + tg * 512, HALO + (tg + 1) * 512)), start=(k == 0), stop=(k == 7))
                    act(yconv.v(oc, tsl), pg.v(), AF.Silu)
                else:
                    py = psbank((2, 3, 6, 7)[oc % 4])
                    for k in range(8):
                        mm(py.v(), wp[oc // 4].v(k, slice(o4 * 128, (o4 + 1) * 128)), cT.v(k, tsl), start=(k == 0), stop=(k == 7))
                    tt(yconv.v(oc, tsl), py.v(), yconv.v(oc, tsl), ALU.mult)

        pw_gate(0, "gate")
        pw_gate(1, "gate")
        ln_squares(0)
        ln_stats_mm(0)
        ln_rs(0)
        ln_norm(0)
        ln_squares(1)
        for tg in range(NTG):
            if tg + 1 < NTG:
                ln_stats_mm(tg + 1)
                ln_rs(tg + 1)
            else:
                for half in range(2):
                    dload(wout.v(slice(None), slice(half * 512, (half + 1) * 512)),
                          w_out_r[:, :, half * 512:(half + 1) * 512], eng="pool")
            for i in range(8):
                if tg + 1 < NTG:
                    ln_norm(tg + 1, [i])
                pw_gate(tg, "pw", [i])
                if tg + 2 < NTG:
                    pw_gate(tg + 2, "gate", [i])
            if tg + 2 < NTG:
                ln_squares(tg + 2)
        dump("aT", cT.v(), (8, T))
        dump("yconv", yconv.v(), (8, T))
        al.reset(m_conv)

    if upto >= 6:
        xb = [al((4, D), F32) for _ in range(1)]
        hb = [al((4, D), F32) for _ in range(2)]
        junk = al((D,), BF16)
        ss = al((16,), F32)
        sd = al((16,), F32)
        rs = al((16,), F32)
        x_own_r = x_own.rearrange("(g j p) d -> g p j d", j=4, p=128)
        y_r = y_d.rearrange("(g j p) d -> g p j d", j=4, p=128)
        out_events = []
        for g in range(4):
            b = g % 2
            dload(xb[0].v(), x_own_r[g])
            for j in range(4):
                tt_ = g * 4 + j
                pout = pspair(2 * (tt_ % 4))
                for half in range(2):
                    for k in range(16):
                        src = yconv.v(k, slice(tt_ * 128, (tt_ + 1) * 128)) if k < 8 else \
                            yret.v(k - 8, slice(tt_ * 128, (tt_ + 1) * 128))
                        mm(pout.v(slice(half * 512, (half + 1) * 512)), src, wout.v(k, slice(half * 512, (half + 1) * 512)),
                           start=(k == 0), stop=(k == 15))
                tt(hb[b].v(j), pout.v(), gate_row.v(), ALU.mult)
                tt(hb[b].v(j), hb[b].v(j), xb[0].v(j), ALU.add)
                act(junk.v(), hb[b].v(j), AF.Square, accum=ss.v(slice(tt_, tt_ + 1)))
                if g == 3:
                    act(sd.v(slice(tt_, tt_ + 1)), ss.v(slice(tt_, tt_ + 1)), AF.Sqrt, bias=EPS, scale=1.0 / D)
                    recip(rs.v(slice(tt_, tt_ + 1)), sd.v(slice(tt_, tt_ + 1)))
                    stt(hb[b].v(j), hb[b].v(j), rs.v(slice(tt_, tt_ + 1)), fg_row.v(), ALU.mult, ALU.mult)
                    out_events.append(dstore(y_r[g][:, j, :], hb[b].v(j)))
            if g == 3:
                continue
            act(sd.v(slice(g * 4, g * 4 + 4)), ss.v(slice(g * 4, g * 4 + 4)), AF.Sqrt, bias=EPS, scale=1.0 / D)
            recip(rs.v(slice(g * 4, g * 4 + 4)), sd.v(slice(g * 4, g * 4 + 4)))
            for j in range(4):
                stt(hb[b].v(j), hb[b].v(j), rs.v(slice(g * 4 + j, g * 4 + j + 1)), fg_row.v(), ALU.mult, ALU.mult)
            out_events.append(dstore(y_r[g], hb[b].v()))
        for ev in out_events:
            kb.wait_event("sp", ev)

    for ev in dbg_out.values():
        kb.wait_event("sp", ev)
    for e in ("pe", "act", "dve", "pool"):
        s = kb.sem_of[e]
        if kb.count[s] > 0:
            kb._wait("sp", s, kb.count[s])
    for q in kb.dq["sp"] + kb.dq["pool"]:
        if kb.count[q] > 0:
            kb._wait("sp", q, kb.count[q])
    if kb.count[kb.cc_sem] > 0:
        kb._wait("sp", kb.cc_sem, kb.count[kb.cc_sem])

    with nc.Block() as block:
        @block.tensor
        def _(e):
            for f in kb.prog["pe"]:
                f(e)

        @block.scalar
        def _(e):
            for f in kb.prog["act"]:
                f(e)

        @block.vector
        def _(e):
            for f in kb.prog["dve"]:
                f(e)

        @block.gpsimd
        def _(e):
            for f in kb.prog["pool"]:
                f(e)

        @block.sync
        def _(e):
            for f in kb.prog["sp"]:
                f(e)
    stack.close()
    return nc, kb


def _col(v):
    return np.ascontiguousarray(np.asarray(v, np.float32).reshape(8, 128).T)


def _gammas():
    h = np.arange(4, dtype=np.float64)
    return np.log(1.0 - np.exp2(-5.0 - h))


def make_core_inputs(core, x, c, ada_w, ada_b, norm_g, w_in, conv_w, conv_b, conv_ln_g, conv_ln_b,
                     conv_pw, ret_gn_g, ret_gn_b, w_out, final_g):
    b, s = core // 4, core % 4
    x = np.asarray(x, np.float32)
    xo = np.ascontiguousarray(x[b, s * T:(s + 1) * T])
    if s == 0:
        xh = np.zeros((HALO, D), np.float32)
    else:
        xh = np.ascontiguousarray(x[b, s * T - HALO:s * T])
    cf = np.zeros((128, NCF), np.float32)
    cf[:, C_CCOL:C_CCOL + 8] = _col(np.asarray(c)[b])
    ab = np.asarray(ada_b, np.float32).reshape(-1)
    cf[:, C_ABSH:C_ABSH + 8] = _col(ab[0:D])
    cf[:, C_ABSC:C_ABSC + 8] = _col(ab[D:2 * D])
    cf[:, C_NG:C_NG + 8] = _col(np.asarray(norm_g).reshape(-1))
    cf[:, C_CB:C_CB + 8] = _col(np.asarray(conv_b).reshape(-1))
    cf[:, C_LNG:C_LNG + 8] = _col(np.asarray(conv_ln_g).reshape(-1))
    cf[:, C_LNB:C_LNB + 8] = _col(np.asarray(conv_ln_b).reshape(-1))
    cf[:, C_GNG:C_GNG + 8] = _col(np.asarray(ret_gn_g).reshape(-1))
    cf[:, C_GNB:C_GNB + 8] = _col(np.asarray(ret_gn_b).reshape(-1))
    cw = np.asarray(conv_w, np.float32).reshape(CW, 8, 128)
    cf[:, C_CWT:C_CWT + 8 * CW] = np.transpose(cw, (2, 1, 0)).reshape(128, 8 * CW)
    lg = _gammas()
    j = np.arange(128, dtype=np.float64)
    cf[:, C_VDEC:C_VDEC + 4] = np.exp(-lg[None, :] * (j[:, None] + 1.0))
    cf[:, C_EPSI:C_EPSI + 4] = EPS * np.exp(-2.0 * lg[None, :] * (j[:, None] + 1.0))
    cf[:, C_G128:C_G128 + 4] = np.exp(lg * 128.0)[None, :]
    coef = np.zeros((4, 4), np.float64)
    for r in range(4):
        if r < s:
            coef[r] = np.exp(lg * (float(T) * (s - 1 - r)))
    cf[:, C_COEF:C_COEF + 16] = coef.reshape(1, 16)
    cf[:, C_HM] = 0.0 if s == 0 else 1.0
    m = (j[:, None] <= j[None, :]).astype(np.float32)
    cf[:, C_MASK:C_MASK + 128] = m
    cf[:, C_MASK + 128:C_MASK + 256] = m
    cf[:, C_IDENT:C_IDENT + 128] = np.eye(128, dtype=np.float32)
    inv_freq = 1.0 / (10000.0 ** np.linspace(0.0, 1.0, 128, dtype=np.float64))
    pos = np.arange(s * T, (s + 1) * T, dtype=np.float64)
    theta = pos[None, :] * inv_freq[:, None]
    rot = np.stack([np.cos(theta), np.sin(theta)], axis=1).astype(np.float32)
    return {
        "x_own": xo, "x_halo": xh, "cf32": cf, "rot": np.ascontiguousarray(rot),
        "ada_w": np.ascontiguousarray(np.asarray(ada_w, np.float32).reshape(D, 3 * D)),
        "ada_b": np.ascontiguousarray(ab),
        "w_in": np.ascontiguousarray(np.asarray(w_in, np.float32).reshape(D, N_IN)),
        "conv_pw": np.ascontiguousarray(np.asarray(conv_pw, np.float32).reshape(D, D)),
        "w_out": np.ascontiguousarray(np.asarray(w_out, np.float32).reshape(2 * D, D)),
        "final_g": np.ascontiguousarray(np.asarray(final_g, np.float32).reshape(D)),
    }


_NC_CACHE = {}


def kernel(**inputs):
    if "nc" not in _NC_CACHE:
        _NC_CACHE["nc"] = build_nc()[0]
    nc = _NC_CACHE["nc"]
    in_maps = [make_core_inputs(core, **inputs) for core in range(8)]
    res = run_bass_kernel_spmd(nc, in_maps, core_ids=list(range(8)))
    out = np.zeros((2, 4 * T, D), np.float32)
    for core in range(8):
        b, s = core // 4, core % 4
        out[b, s * T:(s + 1) * T] = np.asarray(res.results[core]["y"], np.float32)
    return out
```
